# Optimizing a Trainium2 kernel written in Bass

```python
import math
import jax
import jax.numpy as jnp
from jax import lax
import numpy as np

D_MODEL = 1024
BATCH = 16
SEQ = 256
DEPTH = 2
DEC_BATCH = 4
DEC_SEQ = 2048
PAST_LEN = 256

GRID_W = 64
HEAD_DIM = 64
LRU_WIDTH = 256
LRU_BLOCKS = 4
LRU_BLOCK_W = LRU_WIDTH // LRU_BLOCKS
LRU_CONV_W = 4
LRU_CONV_LEFT = 2
LRU_C = 8.0
GQA_HEADS = 8
GQA_KV_HEADS = 2
GQA_GROUP = GQA_HEADS // GQA_KV_HEADS
GQA_WIDTH = GQA_HEADS * HEAD_DIM
DIFF_HEADS = 4
DIFF_V_DIM = HEAD_DIM
DIFF_QK_DIM = HEAD_DIM // 2
DIFF_WIDTH = DIFF_HEADS * DIFF_V_DIM
MIX_WIDTH = LRU_WIDTH + GQA_WIDTH + DIFF_WIDTH
PROJ_SIZES = [LRU_WIDTH, LRU_WIDTH, GQA_WIDTH, GQA_KV_HEADS * HEAD_DIM, GQA_KV_HEADS * HEAD_DIM,
              DIFF_WIDTH, DIFF_WIDTH, DIFF_WIDTH]
IN_WIDTH = sum(PROJ_SIZES)
D_FF = 2816
FFN_CONV_W = 3
FFN_CONV_LEFT = 1
ROPE_THETA = 10000.0
Q_BLOCK = 128
EPS = 1e-6

kernel_name = 'hybrid_diffusion_parallel_heads_step'


def _rmsnorm(x, g):
    xf = x.astype(jnp.float32)
    y = xf * lax.rsqrt(jnp.mean(xf * xf, axis=-1, keepdims=True) + EPS)
    return (y * g.astype(jnp.float32)).astype(x.dtype)


def _ada(cond, w, b):
    m = jnp.einsum('nd,de->ne', jax.nn.silu(cond), w) + b
    return jnp.split(m[:, None, :], 6, axis=-1)


def _dwconv(x, w, b, left):
    width, ch = w.shape
    y = lax.conv_general_dilated(x, w.reshape(width, 1, ch).astype(x.dtype), window_strides=(1,),
                                 padding=[(left, width - 1 - left)],
                                 dimension_numbers=('NWC', 'WIO', 'NWC'), feature_group_count=ch)
    return y + b


def _rope_2d(S, dim):
    rows = S // GRID_W
    t_row = jnp.repeat(jnp.arange(rows, dtype=jnp.float32), GRID_W)
    t_col = jnp.tile(jnp.arange(GRID_W, dtype=jnp.float32), rows)
    axis_dim = dim // 2
    inv = ROPE_THETA ** (-jnp.arange(0, axis_dim, 2, dtype=jnp.float32) / axis_dim)
    ar = t_row[:, None] * inv
    ac = t_col[:, None] * inv
    ang = jnp.concatenate([ar, ar, ac, ac], axis=-1)
    return jnp.cos(ang), jnp.sin(ang)


def _apply_rope(x, cos, sin):
    xf = x.astype(jnp.float32)
    x1, x2, x3, x4 = jnp.split(xf, 4, axis=-1)
    rot = jnp.concatenate([-x2, x1, -x4, x3], axis=-1)
    return (xf * cos + rot * sin).astype(x.dtype)


def _lin_combine(e1, e2):
    a1, b1 = e1
    a2, b2 = e2
    return a1 * a2, a2 * b1 + b2


def _rglru(x, w, b, lam, h0, reverse):
    B, S, W = x.shape
    g = jnp.einsum('bsnd,nde->bsne', x.reshape(B, S, LRU_BLOCKS, LRU_BLOCK_W), w) + b
    r, i = jnp.split(jax.nn.sigmoid(g.astype(jnp.float32)), 2, axis=-1)
    r = r.reshape(B, S, W)
    i = i.reshape(B, S, W)
    log_a = -LRU_C * r * jax.nn.softplus(-lam.astype(jnp.float32))
    a = jnp.exp(log_a)
    u = jnp.sqrt(-jnp.expm1(2.0 * log_a)) * (i * x.astype(jnp.float32))
    edge = S - 1 if reverse else 0
    u = u.at[:, edge].add(a[:, edge] * h0.astype(jnp.float32))
    _, hs = lax.associative_scan(_lin_combine, (a, u), reverse=reverse, axis=1)
    return hs


def _gqa_attend(q, k, v):
    B, S = q.shape[:2]
    nb = S // Q_BLOCK
    qb = jnp.moveaxis(q.reshape(B, nb, Q_BLOCK, GQA_KV_HEADS, GQA_GROUP, HEAD_DIM), 1, 0)
    scale = HEAD_DIM ** -0.5

    def one(qblk):
        s = jnp.einsum('bqhgd,bkhd->bhgqk', qblk, k).astype(jnp.float32) * scale
        p = jax.nn.softmax(s, axis=-1).astype(v.dtype)
        return jnp.einsum('bhgqk,bkhd->bqhgd', p, v)

    o = lax.map(one, qb)
    return jnp.moveaxis(o, 0, 1).reshape(B, S, GQA_WIDTH)


def _diff_attend(q, k, v, lam):
    B, S = q.shape[:2]
    nb = S // Q_BLOCK
    qb = jnp.moveaxis(q.reshape(B, nb, Q_BLOCK, DIFF_HEADS, 2, DIFF_QK_DIM), 1, 0)
    scale = DIFF_QK_DIM ** -0.5

    def one(qblk):
        s = jnp.einsum('bqhcd,bkhcd->bchqk', qblk, k).astype(jnp.float32) * scale
        p = jax.nn.softmax(s, axis=-1)
        att = (p[:, 0] - lam * p[:, 1]).astype(v.dtype)
        return jnp.einsum('bhqk,bkhd->bqhd', att, v)

    o = lax.map(one, qb)
    return jnp.moveaxis(o, 0, 1).reshape(B, S, DIFF_HEADS, DIFF_V_DIM)


def _mixer(h, p, lam_init, ctx):
    B, S, _ = h.shape
    offs = [int(o) for o in np.cumsum(PROJ_SIZES)[:-1]]
    lru_x, lru_g, gq, gk, gv, dq, dk, dv = jnp.split(jnp.einsum('bsd,de->bse', h, p['w_in']), offs, axis=-1)
    q = _rmsnorm(gq.reshape(B, S, GQA_HEADS, HEAD_DIM), p['q_g']).reshape(B, S, GQA_KV_HEADS, GQA_GROUP, HEAD_DIM)
    k = _rmsnorm(gk.reshape(B, S, GQA_KV_HEADS, HEAD_DIM), p['k_g'])
    v = gv.reshape(B, S, GQA_KV_HEADS, HEAD_DIM)
    dq = dq.reshape(B, S, DIFF_HEADS, 2, DIFF_QK_DIM)
    dk = dk.reshape(B, S, DIFF_HEADS, 2, DIFF_QK_DIM)
    dv = dv.reshape(B, S, DIFF_HEADS, DIFF_V_DIM)
    xc = _dwconv(lru_x, p['lru_conv_w'], p['lru_conv_b'], LRU_CONV_LEFT)
    if ctx is None:
        h0f = jnp.zeros((B, LRU_WIDTH), jnp.float32)
        h0b = h0f
        k_att, v_att, dk_att, dv_att = k, v, dk, dv
    else:
        ck, cv, cdk, cdv, cst = ctx
        cos, sin = _rope_2d(S, HEAD_DIM)
        q = _apply_rope(q, cos[:, None, None], sin[:, None, None])
        k = _apply_rope(k, cos[:, None], sin[:, None])
        cosd, sind = _rope_2d(S, DIFF_QK_DIM)
        dq = _apply_rope(dq, cosd[:, None, None], sind[:, None, None])
        dk = _apply_rope(dk, cosd[:, None, None], sind[:, None, None])
        k_att = jnp.concatenate([ck.astype(k.dtype), k], axis=1)
        v_att = jnp.concatenate([cv.astype(v.dtype), v], axis=1)
        dk_att = jnp.concatenate([cdk.astype(dk.dtype), dk], axis=1)
        dv_att = jnp.concatenate([cdv.astype(dv.dtype), dv], axis=1)
        h0f, h0b = cst[:, 0], cst[:, 1]
    hf = _rglru(xc, p['lru_gate_w'][0], p['lru_gate_b'][0], p['lru_lambda'][0], h0f, False)
    hb = _rglru(xc, p['lru_gate_w'][1], p['lru_gate_b'][1], p['lru_lambda'][1], h0b, True)
    lru_out = jax.nn.gelu(lru_g) * (hf + hb).astype(h.dtype)
    gqa_out = _gqa_attend(q, k_att, v_att)
    lq = p['diff_lambda'].astype(jnp.float32)
    lam = jnp.exp(jnp.sum(lq[0] * lq[1])) - jnp.exp(jnp.sum(lq[2] * lq[3])) + lam_init
    d_out = _rmsnorm(_diff_attend(dq, dk_att, dv_att, lam), p['diff_g']) * (1.0 - lam_init)
    mixed = jnp.concatenate([lru_out, gqa_out, d_out.reshape(B, S, DIFF_WIDTH)], axis=-1)
    out = jnp.einsum('bse,ed->bsd', mixed, p['w_out'])
    if ctx is None:
        state = jnp.stack([hf[:, -1], hb[:, 0]], axis=1).astype(h.dtype)
        return out, (k, v, dk, dv, state)
    return out, None


def _conv_ffn(h, p):
    u = _dwconv(jnp.einsum('bsd,de->bse', h, p['ffn_w_up']), p['ffn_conv_w'], p['ffn_conv_b'], FFN_CONV_LEFT)
    a, g = jnp.split(u, 2, axis=-1)
    return jnp.einsum('bsf,fd->bsd', jax.nn.silu(g) * a, p['ffn_w_down'])


def _layer(x, mod, p, lam_init, ctx):
    sh1, sc1, g1, sh2, sc2, g2 = mod
    h = _rmsnorm(x, p['norm1']) * (1.0 + sc1) + sh1
    m, new_ctx = _mixer(h, p, lam_init, ctx)
    x = x + g1 * m
    h = _rmsnorm(x, p['norm2']) * (1.0 + sc2) + sh2
    x = x + g2 * _conv_ffn(h, p)
    return x, new_ctx


def setup_inputs(seed: int = 0) -> dict:
    key = jax.random.key(seed)
    ks = jax.random.split(key, 32)

    def nrm(k, shape, scale):
        return jax.random.normal(k, shape, jnp.float32) * scale

    s = jax.random.uniform(ks[16], (DEPTH, 2, LRU_WIDTH), jnp.float32, 0.9, 0.999) ** (1.0 / LRU_C)
    return {
        'x_prompt': nrm(ks[0], (BATCH, SEQ, D_MODEL), 1.0),
        'x_sample': nrm(ks[1], (DEC_BATCH, DEC_SEQ, D_MODEL), 1.0),
        'cache_gqa_k': nrm(ks[2], (DEC_BATCH, DEPTH, PAST_LEN, GQA_KV_HEADS, HEAD_DIM), 1.0),
        'cache_gqa_v': nrm(ks[3], (DEC_BATCH, DEPTH, PAST_LEN, GQA_KV_HEADS, HEAD_DIM), 0.5),
        'cache_diff_k': nrm(ks[4], (DEC_BATCH, DEPTH, PAST_LEN, DIFF_HEADS, 2, DIFF_QK_DIM), 0.5),
        'cache_diff_v': nrm(ks[5], (DEC_BATCH, DEPTH, PAST_LEN, DIFF_HEADS, DIFF_V_DIM), 0.5),
        'state_lru': nrm(ks[6], (DEC_BATCH, DEPTH, 2, LRU_WIDTH), 0.5),
        'c': nrm(ks[7], (DEC_BATCH, D_MODEL), 1.0),
        'c_ctx': nrm(ks[8], (D_MODEL,), 1.0),
        'norm1_g': 1.0 + nrm(ks[9], (DEPTH, D_MODEL), 0.05),
        'norm2_g': 1.0 + nrm(ks[10], (DEPTH, D_MODEL), 0.05),
        'final_norm_g': 1.0 + nrm(ks[11], (D_MODEL,), 0.05),
        'ada_w': nrm(ks[12], (DEPTH, D_MODEL, 6 * D_MODEL), D_MODEL ** -0.5),
        'ada_b': nrm(ks[13], (DEPTH, 6 * D_MODEL), 0.02),
        'w_in': nrm(ks[14], (DEPTH, D_MODEL, IN_WIDTH), D_MODEL ** -0.5),
        'w_out': nrm(ks[15], (DEPTH, MIX_WIDTH, D_MODEL), MIX_WIDTH ** -0.5),
        'lru_conv_w': nrm(ks[17], (DEPTH, LRU_CONV_W, LRU_WIDTH), LRU_CONV_W ** -0.5),
        'lru_conv_b': nrm(ks[18], (DEPTH, LRU_WIDTH), 0.02),
        'lru_gate_w': nrm(ks[19], (DEPTH, 2, LRU_BLOCKS, LRU_BLOCK_W, 2 * LRU_BLOCK_W), LRU_BLOCK_W ** -0.5),
        'lru_gate_b': nrm(ks[20], (DEPTH, 2, LRU_BLOCKS, 2 * LRU_BLOCK_W), 0.02),
        'lru_lambda': jnp.log(s) - jnp.log1p(-s),
        'gqa_q_norm_g': 1.0 + nrm(ks[21], (DEPTH, HEAD_DIM), 0.05),
        'gqa_k_norm_g': 1.0 + nrm(ks[22], (DEPTH, HEAD_DIM), 0.05),
        'diff_lambda': nrm(ks[23], (DEPTH, 4, DIFF_QK_DIM), 0.1),
        'diff_norm_g': 1.0 + nrm(ks[24], (DEPTH, DIFF_V_DIM), 0.05),
        'ffn_w_up': nrm(ks[25], (DEPTH, D_MODEL, 2 * D_FF), D_MODEL ** -0.5),
        'ffn_conv_w': nrm(ks[26], (DEPTH, FFN_CONV_W, 2 * D_FF), FFN_CONV_W ** -0.5),
        'ffn_conv_b': nrm(ks[27], (DEPTH, 2 * D_FF), 0.02),
        'ffn_w_down': nrm(ks[28], (DEPTH, D_FF, D_MODEL), D_FF ** -0.5),
    }


def reference(x_prompt, x_sample, cache_gqa_k, cache_gqa_v, cache_diff_k, cache_diff_v, state_lru, c,
              c_ctx, norm1_g, norm2_g, final_norm_g, ada_w, ada_b, w_in, w_out, lru_conv_w, lru_conv_b,
              lru_gate_w, lru_gate_b, lru_lambda, gqa_q_norm_g, gqa_k_norm_g, diff_lambda, diff_norm_g,
              ffn_w_up, ffn_conv_w, ffn_conv_b, ffn_w_down):
    xp = x_prompt
    xs = x_sample
    ks, vs, dks, dvs, sts = [], [], [], [], []
    for l in range(DEPTH):
        p = {'norm1': norm1_g[l], 'norm2': norm2_g[l], 'w_in': w_in[l], 'w_out': w_out[l],
             'lru_conv_w': lru_conv_w[l], 'lru_conv_b': lru_conv_b[l], 'lru_gate_w': lru_gate_w[l],
             'lru_gate_b': lru_gate_b[l], 'lru_lambda': lru_lambda[l], 'q_g': gqa_q_norm_g[l],
             'k_g': gqa_k_norm_g[l], 'diff_lambda': diff_lambda[l], 'diff_g': diff_norm_g[l],
             'ffn_w_up': ffn_w_up[l], 'ffn_conv_w': ffn_conv_w[l], 'ffn_conv_b': ffn_conv_b[l],
             'ffn_w_down': ffn_w_down[l]}
        lam_init = 0.8 - 0.6 * math.exp(-0.3 * l)
        xp, (k_l, v_l, dk_l, dv_l, st_l) = _layer(xp, _ada(c_ctx[None], ada_w[l], ada_b[l]), p, lam_init, None)
        ks.append(k_l)
        vs.append(v_l)
        dks.append(dk_l)
        dvs.append(dv_l)
        sts.append(st_l)
        ctx = (cache_gqa_k[:, l], cache_gqa_v[:, l], cache_diff_k[:, l], cache_diff_v[:, l], state_lru[:, l])
        xs, _ = _layer(xs, _ada(c, ada_w[l], ada_b[l]), p, lam_init, ctx)
    y_prompt = _rmsnorm(xp, final_norm_g)
    y_sample = _rmsnorm(xs, final_norm_g)
    return (y_prompt, y_sample, jnp.stack(ks, axis=1), jnp.stack(vs, axis=1), jnp.stack(dks, axis=1),
            jnp.stack(dvs, axis=1), jnp.stack(sts, axis=1))
```

```python
import contextlib
import numpy as np
import concourse.bass as bass
import concourse.mybir as mybir
from concourse.bass_utils import run_bass_kernel_spmd

F32 = mybir.dt.float32
BF16 = mybir.dt.bfloat16
ALU = mybir.AluOpType
AF = mybir.ActivationFunctionType

D_MODEL = 1024
T = 2048
BLK = 512
NB = 4
NT = 16
KT = 18
TK = 2304
DEPTH = 2
D_FF = 2816
EPS = 1e-6
NEG = -30000.0


class Prog:
    def __init__(self, nc):
        self.nc = nc
        self.E = {'pe': nc.tensor, 'act': nc.scalar, 'dve': nc.vector, 'pool': nc.gpsimd, 'sp': nc.sync}
        self.ops = []

    def op(self, eng, fn, reads=(), writes=(), dma=None):
        self.ops.append((eng, fn, tuple(reads), tuple(writes), dma))

    def barrier(self):
        self.ops.append(None)

    def emit(self, final_eng='sp'):
        nc = self.nc
        ops = self.ops
        n = len(ops)
        last_w = {}
        readers = {}
        deps = [None] * n
        marked = [False] * n
        last_eng = {}
        pend_dma = []
        bar_set = None
        bar_done = set()
        for i, rec in enumerate(ops):
            if rec is None:
                s = set(last_eng.values()) | set(pend_dma)
                if bar_set is not None:
                    s |= bar_set
                bar_set = s
                bar_done = set()
                pend_dma = []
                continue
            eng, fn, rd, wr, dma = rec
            d = set()
            if bar_set is not None and eng not in bar_done:
                d |= bar_set
                bar_done.add(eng)
            for t in rd:
                j = last_w.get(t)
                if j is not None:
                    d.add(j)
            for t in wr:
                j = last_w.get(t)
                if j is not None:
                    d.add(j)
                r = readers.get(t)
                if r:
                    d.update(r[0].values())
                    d.update(r[1])
            d.discard(i)
            dd = []
            for j in d:
                ej, _, _, _, dmaj = ops[j]
                if dmaj is None and dma is None and ej == 'pe' and eng == 'pe':
                    continue
                dd.append(j)
                marked[j] = True
            deps[i] = dd
            for t in rd:
                r = readers.get(t)
                if r is None:
                    r = readers[t] = ({}, [])
                if dma is None:
                    r[0][eng] = i
                else:
                    r[1].append(i)
            for t in wr:
                last_w[t] = i
                readers[t] = ({}, [])
            if dma is None:
                last_eng[eng] = i
            else:
                pend_dma.append(i)
        cnt = {}
        mile = [None] * n
        for i, rec in enumerate(ops):
            if rec is None:
                continue
            eng, fn, rd, wr, dma = rec
            if dma is not None:
                cnt[dma] = cnt.get(dma, 0) + 16
                mile[i] = (dma, cnt[dma])
            elif marked[i]:
                k = 'E_' + eng
                cnt[k] = cnt.get(k, 0) + 1
                mile[i] = (k, cnt[k])
        sems = {}
        with contextlib.ExitStack() as st:
            for k in cnt:
                sems[k] = st.enter_context(nc.semaphore(k))
            waited = {e: {} for e in self.E}
            for i, rec in enumerate(ops):
                if rec is None:
                    continue
                eng, fn, rd, wr, dma = rec
                need = {}
                for j in deps[i]:
                    s, v = mile[j]
                    if need.get(s, 0) < v:
                        need[s] = v
                for s, v in need.items():
                    if waited[eng].get(s, 0) < v:
                        self.E[eng].wait_ge(sems[s], v)
                        waited[eng][s] = v
                ins = fn()
                if mile[i] is not None:
                    s, v = mile[i]
                    ins.then_inc(sems[s], 16 if dma is not None else 1)
            for k, v in cnt.items():
                if waited[final_eng].get(k, 0) < v:
                    self.E[final_eng].wait_ge(sems[k], v)
        return cnt


def build_program(n_layers=DEPTH, debug=False):
    nc = bass.Bass("TRN2", target_bir_lowering=False)
    P = Prog(nc)

    def din(name, shape):
        return nc.dram_tensor(name, list(shape), F32, kind="ExternalInput").ap()

    def dout(name, shape):
        return nc.dram_tensor(name, list(shape), F32, kind="ExternalOutput").ap()

    d_xT = din("xT", [D_MODEL, T])
    d_cT = din("cT", [128, 8])
    d_flags = din("flags", [128, 2])
    d_mB = din("mB4", [128, 4 * BLK])
    d_Ak = nc.dram_tensor("Ak", [9, TK], BF16, kind="ExternalInput").ap()
    d_ropeg = din("ropeg", [2, 128, T])
    d_roped = din("roped", [2, 128, T])
    d_permg = din("permg", [128, 128])
    d_permd = din("permd", [128, 128])
    d_ckT = din("ckT", [DEPTH, 128, 256])
    d_cv = din("cv", [DEPTH, 256, 128])
    d_cdkT = din("cdkT", [DEPTH, 256, 256])
    d_cdv = din("cdv", [DEPTH, 256, 256])
    d_h0 = din("h0", [DEPTH, 128, 4])
    d_n1 = din("n1g", [DEPTH, 128, 8])
    d_n2 = din("n2g", [DEPTH, 128, 8])
    d_fg = din("fng", [128, 8])
    d_adaw = din("ada_w", [DEPTH, D_MODEL, 6 * D_MODEL])
    d_adab = din("ada_b", [DEPTH, 128, 48])
    d_winF = din("w_inF", [DEPTH, D_MODEL, 13 * 128])
    d_winV = din("w_inV", [DEPTH, D_MODEL, 384])
    d_wout = din("w_outP", [DEPTH, D_MODEL, D_MODEL])
    d_lcw = din("lcw", [DEPTH, 128, 8])
    d_lcb = din("lcb", [DEPTH, 128, 2])
    d_gw = din("gw", [DEPTH, 8, 128, 128])
    d_gb = din("gb", [DEPTH, 128, 8])
    d_lam = din("lam", [DEPTH, 128, 4])
    d_qkg = din("qkg", [DEPTH, 128, 2])
    d_dl = din("dl", [DEPTH, 128])
    d_dg = din("dg", [DEPTH, 64])
    d_wup = din("w_upP", [DEPTH, D_MODEL, 11 * 512])
    d_fcw = din("fcw", [DEPTH, 128, 44 * 3])
    d_fcb = din("fcb", [DEPTH, 128, 44])
    d_wdn = din("w_dn", [DEPTH, D_FF, D_MODEL])

    o_yT = dout("yT", [D_MODEL, T])
    o_kT = dout("okT", [DEPTH, 128, T])
    o_v = dout("ov", [DEPTH, T, 128])
    o_dkT = dout("odkT", [DEPTH, 256, T])
    o_dv = dout("odv", [DEPTH, T, 256])
    o_st = dout("ost", [128, 64])

    base0 = (nc.sbuf_base + 31) // 32 * 32
    ARENA = 212800
    arena = nc.alloc_sbuf_tensor("arena", [128, ARENA], mybir.dt.uint8)
    cnt_name = [0]

    def at(off, shape, dt):
        cnt_name[0] += 1
        assert off % 32 == 0 and off >= 0, off
        nbytes = int(np.prod(shape[1:])) * (4 if dt == F32 else 2)
        assert off + nbytes <= ARENA, (off, nbytes)
        return nc.alloc_sbuf_tensor_at("t%d" % cnt_name[0], list(shape), dt, offset=base0 + off)

    OFF_X, OFF_ACT, OFF_C, OFF_XR = 0, 65536, 98304, 115456
    xT = at(OFF_X, [128, 8, T], F32)
    actT = at(OFF_ACT, [128, 8, T], BF16)

    cptr = [OFF_C]

    def calloc(shape, dt):
        nbytes = int(np.prod(shape[1:])) * (4 if dt == F32 else 2)
        nbytes = (nbytes + 31) // 32 * 32
        t = at(cptr[0], shape, dt)
        cptr[0] += nbytes
        assert cptr[0] <= OFF_XR
        return t

    ident = calloc([128, 128], BF16)
    ones = calloc([128, 128], BF16)
    bones = calloc([128, 128], BF16)
    permg = calloc([128, 128], BF16)
    permd = calloc([128, 128], BF16)
    gwb = calloc([128, 8, 128], BF16)
    mBb = calloc([128, 4, BLK], BF16)
    mod = calloc([128, 2, 48], F32)
    adab = calloc([128, 2, 48], F32)
    n1g = calloc([128, 2, 8], F32)
    n2g = calloc([128, 2, 8], F32)
    fng = calloc([128, 8], F32)
    gs1 = calloc([128, 2, 8], F32)
    gs2 = calloc([128, 2, 8], F32)
    cT = calloc([128, 8], F32)
    siluc = calloc([128, 8], F32)
    flags = calloc([128, 2], F32)
    lcw = calloc([128, 2, 8], F32)
    lcwn = calloc([128, 2, 8], F32)
    lcb = calloc([128, 2, 2], F32)
    gb = calloc([128, 2, 8], F32)
    lam = calloc([128, 2, 4], F32)
    lsc = calloc([128, 2, 4], F32)
    lsc2 = calloc([128, 2, 4], F32)
    h0 = calloc([128, 2, 4], F32)
    qkg = calloc([128, 2, 2], F32)
    dlt = calloc([128, 64], F32)
    dle = calloc([128, 4], F32)
    nlam = calloc([128, 2], F32)
    dg = calloc([128, 2, 64], F32)
    fcw = calloc([128, 2, 44 * 3], F32)
    fcwn = calloc([128, 2, 44 * 3], F32)
    fcb = calloc([128, 2, 44], F32)
    st = calloc([128, 64], F32)
    sml = calloc([128, 16], F32)
    sstage = calloc([128, 128], F32)

    X = OFF_XR
    LRUOUT = at(X + 0, [128, 2, T], BF16)
    QT = at(X + 8192, [128, 4, T], BF16)
    DQT = at(X + 24576, [128, 2, T], BF16)
    KZ = [at(X + 32768 + 4608 * i, [128, TK], BF16) for i in range(2)]
    DKZ = [at(X + 41984 + 4608 * i, [128, TK], BF16) for i in range(3)]
    VG = at(X + 55808, [128, KT, 2, 66], BF16)
    VD = at(X + 60576, [128, KT, 4, 66], BF16)
    WS = [at(X + 70080 + 4096 * i, [128, 8, 128], F32) for i in range(2)]
    WB = [at(X + 78272 + 2048 * i, [128, 8, 128], BF16) for i in range(2)]
    PT0 = X + 82368
    SQ = [at(PT0 + 1024 * i, [128, BLK], BF16) for i in range(2)]
    RT = [at(PT0 + 2048, [128, BLK], F32) for i in range(2)]
    QN = [at(PT0 + 4096 + 2048 * i, [128, BLK], F32) for i in range(2)]
    T1 = [at(PT0 + 8192 + 2048 * i, [128, BLK], F32) for i in range(2)]
    VST = [at(PT0 + 12288, [128, 4, 128], F32), at(PT0 + 4096 + 2048, [128, 4, 128], F32)]
    ROPET = at(X + 70080, [128, 4, BLK], F32)
    T2 = [at(X + 78272 + 2048 * i, [128, BLK], F32) for i in range(2)]
    LRUX = at(X + 8192, [128, 2, T], BF16)
    LRUG = at(X + 16384, [128, 2, T], BF16)
    L_R = [at(X + 24576 + 24576 * d, [128, T], F32) for d in range(2)]
    L_I = [at(X + 32768 + 24576 * d, [128, T], F32) for d in range(2)]
    L_T = [at(X + 40960 + 24576 * d, [128, T], F32) for d in range(2)]
    L_XC = at(X + 73728, [128, T], F32)
    L_XCB = at(X + 81920, [128, T], BF16)
    PTt = [at(X + 70080 + 2048 * i, [128, 2 * BLK], BF16) for i in range(2)]
    OTOK = [at(X + 74176 + 1024 * i, [128, 4, 128], BF16) for i in range(2)]
    DD1 = at(X + 76224, [128, 4, 64], F32)
    DD2 = at(X + 77248, [128, 4, 64], F32)
    QZ = [at(X + 78272 + 1024 * i, [128, BLK], BF16) for i in range(2)]
    ADAS = [at(X + 16384 * i, [128, 8, BLK], F32) for i in range(2)]
    WOS = [at(X + 8192 + 16384 * i, [128, 8, BLK], F32) for i in range(2)]
    WOB = [at(X + 40960 + 8192 * i, [128, 8, BLK], BF16) for i in range(2)]
    FWS = at(X + 0, [128, 8, BLK], F32)
    FWB = [at(X + 16384 + 8192 * i, [128, 8, BLK], BF16) for i in range(2)]
    FDS = at(X + 32768, [128, 2, D_MODEL], F32)
    FDB = [at(X + 40960 + 4096 * i, [128, 2, D_MODEL], BF16) for i in range(2)]
    FY = [at(X + 49152 + 4096 * i, [128, T], BF16) for i in range(2)]
    FU0 = [at(X + 57344 + 4096 * i, [128, T], BF16) for i in range(2)]
    FU2 = [at(X + 65536 + 4096 * i, [128, T], BF16) for i in range(2)]
    FACT = [at(X + 73728 + 8192 * i, [128, 2, T], BF16) for i in range(2)]
    FOUT = [at(X + 90112 + 2048 * i, [128, BLK], F32) for i in range(2)]

    ps = nc.alloc_psum_tensor("ps", [128, 8, BLK], F32)

    def dma(out, in_, key, reads=(), writes=(), eng='sp'):
        P.op(eng, lambda: P.E[eng].dma_start(out=out, in_=in_), reads, writes, dma=key)

    def mm(out, lhsT, rhs, start, stop, reads, writes, tp=None):
        if tp is None:
            P.op('pe', lambda: nc.tensor.matmul(out, lhsT=lhsT, rhs=rhs, start=start, stop=stop), reads, writes)
        else:
            P.op('pe', lambda: nc.tensor.matmul(out, lhsT=lhsT, rhs=rhs, start=start, stop=stop, tile_position=tp),
                 reads, writes)

    def act(out, in_, func, reads, writes, bias=None, scale=None, accum=None):
        kw = {}
        if bias is not None:
            kw['bias'] = bias
        if scale is not None:
            kw['scale'] = scale
        if accum is not None:
            kw['accum_out'] = accum
        P.op('act', lambda: nc.scalar.activation(out=out, in_=in_, func=func, **kw), reads, writes)

    def vtt(out, in0, in1, op, reads, writes, eng='dve'):
        P.op(eng, lambda: P.E[eng].tensor_tensor(out=out, in0=in0, in1=in1, op=op), reads, writes)

    def vts(out, in0, s1, s2, op0, op1, reads, writes, eng='dve'):
        if op1 is None:
            P.op(eng, lambda: P.E[eng].tensor_scalar(out=out, in0=in0, scalar1=s1, scalar2=None, op0=op0), reads, writes)
        else:
            P.op(eng, lambda: P.E[eng].tensor_scalar(out=out, in0=in0, scalar1=s1, scalar2=s2, op0=op0, op1=op1),
                 reads, writes)

    def pts2(out, in0, s1, s2, reads, writes):
        P.op('pool', lambda: nc.gpsimd.tensor_scalar(out=out, in0=in0, scalar1=s1, scalar2=s2, op0=ALU.mult, op1=ALU.add),
             reads, writes)

    def vstt(out, in0, scalar, in1, op0, op1, reads, writes):
        P.op('dve', lambda: nc.vector.scalar_tensor_tensor(out=out, in0=in0, scalar=scalar, in1=in1, op0=op0, op1=op1),
             reads, writes)

    def vcopy(out, in_, reads, writes, eng='dve'):
        P.op(eng, lambda: P.E[eng].tensor_copy(out=out, in_=in_), reads, writes)

    def vrecip(out, in_, reads, writes):
        P.op('dve', lambda: nc.vector.reciprocal(out=out, in_=in_), reads, writes)

    def memset(ap, val, writes, eng='dve'):
        P.op(eng, lambda: P.E[eng].memset(ap, val), (), writes)

    def bs(b):
        return slice(b * BLK, (b + 1) * BLK)

    for kc in range(8):
        dma(xT[:, kc, :], d_xT[kc * 128:(kc + 1) * 128, :], 'ld_x', writes=[('x', kc, b) for b in range(NB)])
    small_loads = [
        (cT[:], d_cT[:, :], 'cT'), (flags[:], d_flags[:, :], 'flags'),
        (fng[:], d_fg[:, :], 'fng'),
    ]
    for l in range(DEPTH):
        small_loads += [
            (adab[:, l, :], d_adab[l], 'adab'), (n1g[:, l, :], d_n1[l], 'n1g'), (n2g[:, l, :], d_n2[l], 'n2g'),
            (lcw[:, l, :], d_lcw[l], 'lcw'), (lcb[:, l, :], d_lcb[l], 'lcb'), (gb[:, l, :], d_gb[l], 'gb'),
            (lam[:, l, :], d_lam[l], 'lam'), (h0[:, l, :], d_h0[l], 'h0'), (qkg[:, l, :], d_qkg[l], 'qkg'),
            (dg[:, l, :], d_dg[l].partition_broadcast(128), 'dg'),
            (fcw[:, l, :], d_fcw[l], 'fcw'), (fcb[:, l, :], d_fcb[l], 'fcb'),
        ]
    for (o, i, k) in small_loads:
        dma(o, i, 'ld_c', writes=[k])
    CK = ['cT', 'flags', 'fng', 'adab', 'n1g', 'n2g', 'lcw', 'lcb', 'gb', 'lam', 'h0', 'qkg', 'dg', 'fcw', 'fcb']
    memset(ones[:], 1.0, ['ones'])
    memset(bones[:], 0.0, ['bones'])
    memset(bones[0:64, 0:64], 1.0, ['bones'])
    memset(bones[64:128, 64:128], 1.0, ['bones'])
    for (dst, src, k) in ((permg, d_permg, 'permg'), (permd, d_permd, 'permd')):
        dma(sstage[:], src[:, :], 'ld_c2', writes=['sstage'])
        vcopy(dst[:], sstage[:], ['sstage'], [k])
    d_ident = din("ident", [128, 128])
    dma(sstage[:], d_ident[:, :], 'ld_c2', writes=['sstage'])
    vcopy(ident[:], sstage[:], ['sstage'], ['ident'])
    dma(ADAS[1][:, 0:4, :].rearrange("p a b -> p (a b)"), d_mB[:, :], 'ld_c2', writes=['adas1'])
    vcopy(mBb[:], ADAS[1][:, 0:4, :], ['adas1'], ['mB'])
    P.barrier()
    act(siluc[:], cT[:], AF.Silu, CK, ['siluc'])
    act(lsc[:], lam[:], AF.Exp, CK, ['lsc'], scale=-1.0)
    act(lsc[:], lsc[:], AF.Ln, ['lsc'], ['lsc'], bias=1.0)
    vts(lsc2[:], lsc[:], -16.0, None, ALU.mult, None, ['lsc'], ['lsc2'])
    vts(lsc[:], lsc[:], -8.0, None, ALU.mult, None, ['lsc', 'lsc2'], ['lsc'])
    vts(lcwn[:], lcw[:], flags[:, 1:2], -1.0, ALU.mult, ALU.mult, CK, ['lcwn'])
    vts(fcwn[:], fcw[:], flags[:, 1:2], -1.0, ALU.mult, ALU.mult, CK, ['fcwn'])
    import math
    lam_init = [0.8 - 0.6 * math.exp(-0.3 * l) for l in range(DEPTH)]
    for l in range(DEPTH):
        dma(sstage[:], d_dl[l].partition_broadcast(128), 'ld_c2', writes=['sstage'])
        for k in range(2):
            vtt(dlt[:, 0:32], sstage[:, 64 * k:64 * k + 32], sstage[:, 64 * k + 32:64 * k + 64], ALU.mult, ['sstage'], ['dlt'])
            P.op('dve', lambda k=k, l=l: nc.vector.reduce_sum(out=dle[:, 2 * l + k:2 * l + k + 1], in_=dlt[:, 0:32],
                                                            axis=mybir.AxisListType.X), ['dlt'], ['dle'])
        act(dle[:, 2 * l:2 * l + 2], dle[:, 2 * l:2 * l + 2], AF.Exp, ['dle'], ['dle'])
        vts(sml[:, 0:1], dle[:, 2 * l + 1:2 * l + 2], -lam_init[l], None, ALU.add, None, ['dle'], ['sml'])
        vtt(nlam[:, l:l + 1], sml[:, 0:1], dle[:, 2 * l:2 * l + 1], ALU.subtract, ['sml', 'dle'], ['nlam'])
        vts(dg[:, l, :], dg[:, l, :], 1.0 - lam_init[l], None, ALU.mult, None, CK, ['dg'])
    P.barrier()

    ADS = [at(X + 16384 * i, [128, 8, BLK], F32) for i in range(3)]
    ADB = [at(X + 49152 + 8192 * i, [128, 8, BLK], BF16) for i in range(2)]
    ROW = at(X + 65536, [1, 6 * D_MODEL], F32)
    onef = calloc([128, 2], F32)
    mhalf = calloc([128, 4], F32)
    memset(mhalf[:], -0.5, ['mhalf'])
    silucb = calloc([128, 8], BF16)
    memset(onef[:], 1.0, ['onef'])
    vcopy(silucb[:], siluc[:], ['siluc'], ['silucb'])
    adac = [0]

    def ada(l):
        for g in range(12):
            n = adac[0]
            adac[0] += 1
            s3 = n % 3
            s = n % 2
            for hf in range(2):
                dma(ADS[s3][:, 4 * hf:4 * hf + 4, :],
                    d_adaw[l][hf * 512:(hf + 1) * 512, g * 512:(g + 1) * 512].rearrange("(kc p) n -> p kc n", p=128),
                    'ld_ada%d' % s3, writes=[('ads', s3)], eng=('sp' if hf == 0 else 'pool'))
            if n % 2 == 0:
                act(ADB[s][:], ADS[s3][:], AF.Copy, [('ads', s3)], [('adb', s)])
            else:
                vcopy(ADB[s][:], ADS[s3][:], [('ads', s3)], [('adb', s)])
            bank = g % 4
            for kc in range(8):
                mm(ps[0:1, bank, :], silucb[:, kc:kc + 1], ADB[s][:, kc, :], kc == 0, kc == 7,
                   [('adb', s), 'silucb'], [('ps', bank)])
            vcopy(ROW[0:1, g * 512:(g + 1) * 512], ps[0:1, bank, :], [('ps', bank)], ['row'])
        for col in range(48):
            mm(ps[:, 7, col:col + 1], ROW[0:1, col * 128:(col + 1) * 128], onef[0:1, 0:1], True, True,
               ['row', 'onef'], [('ps', 7)])
        vtt(mod[:, l, :], ps[:, 7, 0:48], adab[:, l, :], ALU.add, [('ps', 7)], [('mod', l)])
        vstt(gs1[:, l, :], mod[:, l, 8:16], 1.0, n1g[:, l, :], ALU.add, ALU.mult, [('mod', l)], [('gs', l)])
        vstt(gs2[:, l, :], mod[:, l, 32:40], 1.0, n2g[:, l, :], ALU.add, ALU.mult, [('mod', l)], [('gs', l)])

    ADS2 = at(X + 80320, [128, 8, 256], F32)
    ADB2 = at(X + 88512, [128, 8, 256], BF16)
    ROW2 = at(X + 92608, [1, 256], F32)
    ada1_state = [0]

    def ada1_scatter(g):
        for j in range(2):
            col = 256 + 2 * g + j
            mm(ps[:, 7, col:col + 1], ROW2[0:1, j * 128:(j + 1) * 128], onef[0:1, 0:1], True, True,
               ['row2', 'onef'], [('ps', 7)])

    def ada1_step():
        g = ada1_state[0]
        if g >= 24:
            return
        ada1_state[0] += 1
        if g > 0:
            ada1_scatter(g - 1)
        for hf in range(2):
            dma(ADS2[:, 4 * hf:4 * hf + 4, :],
                d_adaw[1][hf * 512:(hf + 1) * 512, g * 256:(g + 1) * 256].rearrange("(kc p) n -> p kc n", p=128),
                'ld_ada2', writes=['ads2'], eng=('sp' if hf == 0 else 'pool'))
        vcopy(ADB2[:], ADS2[:], ['ads2'], ['adb2'])
        for kc in range(8):
            mm(ps[0:1, 7, 0:256], silucb[:, kc:kc + 1], ADB2[:, kc, :], kc == 0, kc == 7, ['adb2', 'silucb'], [('ps', 7)])
        vcopy(ROW2[0:1, :], ps[0:1, 7, 0:256], [('ps', 7)], ['row2'])

    def ada1_finish():
        while ada1_state[0] < 24:
            ada1_step()
        ada1_scatter(23)
        vtt(mod[:, 1, :], ps[:, 7, 256:304], adab[:, 1, :], ALU.add, [('ps', 7)], [('mod', 1)])
        vstt(gs1[:, 1, :], mod[:, 1, 8:16], 1.0, n1g[:, 1, :], ALU.add, ALU.mult, [('mod', 1)], [('gs', 1)])
        vstt(gs2[:, 1, :], mod[:, 1, 32:40], 1.0, n2g[:, 1, :], ALU.add, ALU.mult, [('mod', 1)], [('gs', 1)])

    def norm_mod(gs_ap, sh_ap, rkeys, out_fn):
        for b in range(NB):
            for kc in range(8):
                if kc % 2 == 0:
                    act(SQ[kc % 2][:], xT[:, kc, bs(b)], AF.Square, [('x', kc, b)], [('sq', kc % 2)])
                else:
                    vtt(SQ[kc % 2][:], xT[:, kc, bs(b)], xT[:, kc, bs(b)], ALU.mult, [('x', kc, b)], [('sq', kc % 2)])
                mm(ps[:, b % 2, :], ones[:], SQ[kc % 2][:], kc == 0, kc == 7, [('sq', kc % 2), 'ones'], [('ps', b % 2)])
            act(RT[b % 2][:], ps[:, b % 2, :], AF.Ln, [('ps', b % 2)], [('rt', 0)], bias=EPS, scale=1.0 / D_MODEL)
            act(RT[b % 2][:], RT[b % 2][:], AF.Exp, [('rt', 0)], [('rt', 0)], scale=-0.5)
            for kc in range(8):
                vtt(T1[kc % 2][:], xT[:, kc, bs(b)], RT[b % 2][:], ALU.mult, [('x', kc, b), ('rt', 0)], [('t1', kc % 2)])
                out_fn(kc, b, T1[kc % 2], ('t1', kc % 2))

    wcount = [0]

    def load_w128(src_ap):
        i = wcount[0] % 2
        wcount[0] += 1
        dma(WS[i][:], src_ap.rearrange("(kc p) n -> p kc n", p=128), 'ld_ws%d' % i, writes=[('ws', i)])
        vcopy(WB[i][:], WS[i][:], [('ws', i)], [('wb', i)], eng='pool')
        return i

    def layer(l):
        sh1, g1 = mod[:, l, 0:8], mod[:, l, 16:24]
        sh2, g2 = mod[:, l, 24:32], mod[:, l, 40:48]
        MK = [('mod', l), ('gs', l)]

        def o1(kc, b, t, tk):
            if kc % 3 == 2:
                pts2(actT[:, kc, bs(b)], t[:], gs1[:, l, kc:kc + 1], sh1[:, kc:kc + 1], [tk] + MK, [('h', kc, b)])
                return
            act(actT[:, kc, bs(b)], t[:], AF.Identity, [tk] + MK, [('h', kc, b)],
                bias=sh1[:, kc:kc + 1], scale=gs1[:, l, kc:kc + 1])
        norm_mod(None, None, None, o1)

        for i in range(8):
            dma(sstage[:], d_gw[l, i], 'ld_c2', writes=['sstage'])
            vcopy(gwb[:, i, :], sstage[:], ['sstage'], ['gwb'], eng='pool')

        pbank = [0]

        def proj_chunk(ci, post):
            wi = load_w128(d_winF[l][:, ci * 128:(ci + 1) * 128])
            pend = None
            for b in range(NB):
                pb = pbank[0] % 6
                pbank[0] += 1
                for kc in range(8):
                    mm(ps[:, pb, :], WB[wi][:, kc, :], actT[:, kc, bs(b)], kc == 0, kc == 7,
                       [('wb', wi), ('h', kc, b)], [('ps', pb)])
                nxt = post(b, ps[:, pb, :], ('ps', pb))
                if pend is not None:
                    pend()
                pend = nxt
            if pend is not None:
                pend()

        for cc in range(2):
            def post_lx(b, pt, pk, cc=cc):
                act(LRUX[:, cc, bs(b)], pt, AF.Copy, [pk], [('lrux', cc)])
            proj_chunk(cc, post_lx)
        for cc in range(2):
            def post_lg(b, pt, pk, cc=cc):
                i = b % 2
                act(QN[i][:], pt, AF.Square, [pk], [('qn', i)])
                vts(QN[i][:], QN[i][:], 0.044715, 1.0, ALU.mult, ALU.add, [('qn', i)], [('qn', i)])
                vtt(QN[i][:], QN[i][:], pt, ALU.mult, [('qn', i), pk], [('qn', i)])
                act(QN[i][:], QN[i][:], AF.Sigmoid, [('qn', i)], [('qn', i)], scale=1.5957691216057308)
                vtt(LRUG[:, cc, bs(b)], QN[i][:], pt, ALU.mult, [('qn', i), pk], [('lrug', cc)])
            proj_chunk(2 + cc, post_lg)

        lru_phase(l)

        P.barrier()
        ctx_load(l)
        memset(VG[:, :, :, 64:65], 1.0, ['vg'], eng='pool')
        memset(VD[:, :, :, 64:65], 1.0, ['vd'], eng='pool')

        auxc = [0]

        def post_qk(dst_fn, gcol, out_dram=None):
            def post(b, pt, pk):
                i = auxc[0] % 2
                auxc[0] += 1
                act(SQ[i][:], pt, AF.Square, [pk], [('sq', i)])

                def cont():
                    ab = 6 + i
                    aux, auxk = ps[:, ab, :], ('ps', ab)
                    mm(aux, bones[:], SQ[i][:], True, True, [('sq', i), 'bones'], [auxk])
                    act(RT[i][:], aux, AF.Ln, [auxk], [('rt', 0)], bias=EPS, scale=1.0 / 64)
                    act(RT[i][:], RT[i][:], AF.Exp, [('rt', 0)], [('rt', 0)], scale=-0.5)
                    if out_dram is None:
                        vstt(dst_fn(b), pt, qkg[:, l, gcol:gcol + 1], RT[i][:], ALU.mult, ALU.mult, [pk, ('rt', 0)] + CK,
                             [('qkv',)])
                    else:
                        vstt(QN[i][:], pt, qkg[:, l, gcol:gcol + 1], RT[i][:], ALU.mult, ALU.mult, [pk, ('rt', 0)] + CK,
                             [('qn', i)])
                        for kv in range(2):
                            vcopy(KZ[kv][0:64, 256 + b * BLK:256 + (b + 1) * BLK], QN[i][kv * 64:(kv + 1) * 64, :],
                                  [('qn', i)], [('qkv',)])
                        dma(out_dram(b), QN[i][:], 'st_k', reads=[('qn', i)])
                return cont
            return post

        for c in range(4):
            proj_chunk(4 + c, post_qk(lambda b, c=c: QT[:, c, bs(b)], 0))
        proj_chunk(8, post_qk(None, 1,
                              out_dram=lambda b: o_kT[l][:, bs(b)]))
        for c in range(2):
            def post_dq(b, pt, pk, c=c):
                act(DQT[:, c, bs(b)], pt, AF.Copy, [pk], [('qkv',)])
            proj_chunk(9 + c, post_dq)
        for c in range(2):
            def post_dk(b, pt, pk, c=c):
                i = b % 2
                act(QN[i][:], pt, AF.Copy, [pk], [('qn', i)])
                for r in range(4):
                    n = 4 * c + r
                    vcopy(DKZ[n // 3][(n % 3) * 32:(n % 3) * 32 + 32, 256 + b * BLK:256 + (b + 1) * BLK],
                          QN[i][r * 32:(r + 1) * 32, :], [('qn', i)], [('qkv',)])
                dma(o_dkT[l][c * 128:(c + 1) * 128, bs(b)], QN[i][:], 'st_k', reads=[('qn', i)])
            proj_chunk(11 + c, post_dk)
        for vc in range(3):
            wi = load_w128(d_winV[l][:, vc * 128:(vc + 1) * 128])
            for tg in range(4):
                pbk = 4 * (vc % 2) + tg
                for tt in range(4):
                    tok = tg * 4 + tt
                    for kc in range(8):
                        mm(ps[:, pbk, tt * 128:(tt + 1) * 128], actT[:, kc, tok * 128:(tok + 1) * 128], WB[wi][:, kc, :],
                           kc == 0, kc == 7, [('wb', wi)] + [('h', kc, tok // 4)], [('ps', pbk)])
                i = tg % 2
                vk = ('vst', 0) if i == 0 else ('qn', 1)
                act(VST[i][:], ps[:, pbk, :].rearrange("p (a b) -> p a b", a=4), AF.Copy, [('ps', pbk)], [vk])
                if vc == 0:
                    vcopy(VG[:, 2 + tg * 4:2 + tg * 4 + 4, :, 0:64],
                          VST[i][:].rearrange("p a (h d) -> p a h d", h=2), [vk], [('qkv',)])
                    dma(o_v[l][tg * 512:(tg + 1) * 512, :].rearrange("(a p) n -> p a n", p=128), VST[i][:], 'st_v',
                        reads=[vk])
                else:
                    hb = 2 * (vc - 1)
                    vcopy(VD[:, 2 + tg * 4:2 + tg * 4 + 4, hb:hb + 2, 0:64],
                          VST[i][:].rearrange("p a (h d) -> p a h d", h=2), [vk], [('qkv',)])
                    dma(o_dv[l][tg * 512:(tg + 1) * 512, hb * 64:(hb + 2) * 64].rearrange("(a p) n -> p a n", p=128),
                        VST[i][:], 'st_v', reads=[vk])
        P.barrier()
        for b in range(NB):
            dma(ROPET[:, 0:2, :], d_ropeg[:, :, bs(b)].rearrange("a p n -> p a n"), 'ld_rope', writes=['ropet'])
            dma(ROPET[:, 2:4, :], d_roped[:, :, bs(b)].rearrange("a p n -> p a n"), 'ld_rope', writes=['ropet'])
            items = [(QT[:, c, bs(b)], permg, 0, 128) for c in range(4)]
            items += [(KZ[kv][0:64, 256 + b * BLK:256 + (b + 1) * BLK], permg, 0, 64) for kv in range(2)]
            items += [(DQT[:, c, bs(b)], permd, 2, 128) for c in range(2)]
            items += [(DKZ[t][:, 256 + b * BLK:256 + (b + 1) * BLK], permd, 2, 128) for t in range(3)]
            for n, (ap_, pm, to, np_) in enumerate(items):
                i = n % 2
                mm(ps[0:np_, i, :], pm[0:np_, 0:np_], ap_, True, True, [('rp', b, n)], [('ps', i)])
                vtt(T1[i][0:np_, :], ap_, ROPET[0:np_, to, :], ALU.mult, [('rp', b, n), 'ropet'], [('t1', i)])
                vtt(T2[i][0:np_, :], ps[0:np_, i, :], ROPET[0:np_, to + 1, :], ALU.mult, [('ps', i), 'ropet'], [('t2', i)])
                vtt(ap_, T1[i][0:np_, :], T2[i][0:np_, :], ALU.add, [('t1', i), ('t2', i)], [('rp', b, n)])
        P.barrier()
        for t in range(3):
            dma(DKZ[t][96:105, :], d_Ak[:, :], 'ld_ctx', writes=[('qkv',)])

        attention(l)
        if l == 0 and DEPTH > 1:
            ada1_finish()
        P.barrier()

        for og in range(2):
            dma(WOS[og][:], d_wout[l][:, og * 512:(og + 1) * 512].rearrange("(kc p) n -> p kc n", p=128),
                'ld_wo%d' % og, writes=[('wos', og)])
            act(WOB[og][:], WOS[og][:], AF.Copy, [('wos', og)], [('wob', og)])
        n = 0
        for og in range(2):
            for oc in range(4):
                ch = og * 4 + oc
                for b in range(NB):
                    pbk = n % 8
                    n += 1
                    for kc in range(8):
                        src = LRUOUT[:, kc, bs(b)] if kc < 2 else actT[:, kc, bs(b)]
                        mm(ps[:, pbk, :], WOB[og][:, kc, oc * 128:(oc + 1) * 128], src, kc == 0, kc == 7,
                           [('wob', og)], [('ps', pbk)])
                    vstt(xT[:, ch, bs(b)], ps[:, pbk, :], g1[:, ch:ch + 1], xT[:, ch, bs(b)], ALU.mult, ALU.add,
                         [('ps', pbk), ('x', ch, b)] + MK, [('x', ch, b)])
        P.barrier()

        def o2(kc, b, t, tk):
            if kc % 3 == 2:
                pts2(actT[:, kc, bs(b)], t[:], gs2[:, l, kc:kc + 1], sh2[:, kc:kc + 1], [tk] + MK, [('h', kc, b)])
                return
            act(actT[:, kc, bs(b)], t[:], AF.Identity, [tk] + MK, [('h', kc, b)],
                bias=sh2[:, kc:kc + 1], scale=gs2[:, l, kc:kc + 1])
        norm_mod(None, None, None, o2)
        ffn(l, g2, MK)
        P.barrier()

    def ctx_load(l):
        dma(T1[0][:, 0:256], d_ckT[l], 'ld_ctx', writes=[('t1', 0)])
        for kv in range(2):
            memset(KZ[kv][64:128, :], 0.0, [('qkv',)], eng='pool')
            dma(KZ[kv][64:73, :], d_Ak[:, :], 'ld_ctx', writes=[('qkv',)])
            vcopy(KZ[kv][0:64, 0:256], T1[0][kv * 64:(kv + 1) * 64, 0:256], [('t1', 0)], [('qkv',)])
        stg = [(T1[1], ('t1', 1)), (RT[0], ('rt', 0))]
        for t in range(3):
            memset(DKZ[t][96:128, :], 0.0, [('qkv',)], eng='pool')
        memset(DKZ[2][64:96, :], 0.0, [('qkv',)], eng='pool')
        for c in range(2):
            tt, tk = stg[c]
            dma(tt[:, 0:256], d_cdkT[l][c * 128:(c + 1) * 128, :], 'ld_ctx', writes=[tk])
            for r in range(4):
                n = 4 * c + r
                vcopy(DKZ[n // 3][(n % 3) * 32:(n % 3) * 32 + 32, 0:256], tt[r * 32:(r + 1) * 32, 0:256], [tk], [('qkv',)])
        dma(QN[0][:, 0:256].rearrange("p (a n) -> p a n", a=2), d_cv[l].rearrange("(a p) n -> p a n", p=128), 'ld_ctx',
            writes=[('qn', 0)])
        vcopy(VG[:, 0:2, :, 0:64], QN[0][:, 0:256].rearrange("p (a h d) -> p a h d", a=2, h=2), [('qn', 0)], [('qkv',)])
        dma(QN[1][:].rearrange("p (a n) -> p a n", a=2), d_cdv[l].rearrange("(a p) n -> p a n", p=128), 'ld_ctx',
            writes=[('qn', 1)])
        vcopy(VD[:, 0:2, :, 0:64], QN[1][:].rearrange("p (a h d) -> p a h d", a=2, h=4), [('qn', 1)], [('qkv',)])

    def lru_phase(l):
        P.barrier()
        H2 = T // 2
        for cc in range(2):
            x = LRUX[:, cc, :]
            w = lambda k: lcw[:, l, cc * 4 + k:cc * 4 + k + 1]
            wn = lambda k: lcwn[:, l, cc * 4 + k:cc * 4 + k + 1]
            XK = ['xc']
            act(L_XC[:], x, AF.Identity, [('lrux', cc)] + CK, XK, bias=lcb[:, l, cc:cc + 1], scale=w(2))
            vstt(L_XC[:, 2:T], x[:, 0:T - 2], w(0), L_XC[:, 2:T], ALU.mult, ALU.add, XK + CK, XK)
            vstt(L_XC[:, 1:T], x[:, 0:T - 1], w(1), L_XC[:, 1:T], ALU.mult, ALU.add, XK + CK, XK)
            vstt(L_XC[:, 0:T - 1], x[:, 1:T], w(3), L_XC[:, 0:T - 1], ALU.mult, ALU.add, XK + CK, XK)
            vstt(L_XC[:, 256:T:256], x[:, 254:T - 2:256], wn(0), L_XC[:, 256:T:256], ALU.mult, ALU.add, XK + ['lcwn'], XK)
            vstt(L_XC[:, 256:T:256], x[:, 255:T - 1:256], wn(1), L_XC[:, 256:T:256], ALU.mult, ALU.add, XK + ['lcwn'], XK)
            vstt(L_XC[:, 257:T:256], x[:, 255:T - 1:256], wn(0), L_XC[:, 257:T:256], ALU.mult, ALU.add, XK + ['lcwn'], XK)
            vstt(L_XC[:, 255:T - 1:256], x[:, 256:T:256], wn(3), L_XC[:, 255:T - 1:256], ALU.mult, ALU.add, XK + ['lcwn'], XK)
            vcopy(L_XCB[:], L_XC[:], XK, ['xcb'])
            for h in range(2):
                for dr in range(2):
                    for ri in range(2):
                        combo = dr * 2 + ri
                        gi = dr * 4 + cc * 2 + ri
                        for k in range(2):
                            b = 2 * h + k
                            mm(ps[:, combo * 2 + k, :], gwb[:, gi, :], L_XCB[:, bs(b)], True, True, ['xcb', 'gwb'],
                               [('ps', combo * 2 + k)])
                for dr in range(2):
                    for ri in range(2):
                        combo = dr * 2 + ri
                        gi = dr * 4 + cc * 2 + ri
                        dst = (L_R if ri == 0 else L_I)[dr]
                        dk = ('lr' if ri == 0 else 'li', dr)
                        act(dst[:, h * H2:(h + 1) * H2], ps[:, combo * 2:combo * 2 + 2, :].rearrange("p a b -> p (a b)"),
                            AF.Sigmoid, [('ps', combo * 2), ('ps', combo * 2 + 1)] + CK, [dk], bias=gb[:, l, gi:gi + 1])
            for dr in range(2):
                sc2 = lsc2[:, l, dr * 2 + cc:dr * 2 + cc + 1]
                act(L_T[dr][:], L_R[dr][:], AF.Exp, [('lr', dr), 'lsc2'], [('lt', dr)], scale=sc2)
            for dr in range(2):
                sc = lsc[:, l, dr * 2 + cc:dr * 2 + cc + 1]
                act(L_R[dr][:], L_R[dr][:], AF.Exp, [('lr', dr), 'lsc'], [('lr', dr)], scale=sc)
            for dr in range(2):
                act(L_T[dr][:], L_T[dr][:], AF.Sqrt, [('lt', dr)], [('lt', dr)], bias=1.0, scale=-1.0)
            for dr in range(2):
                A, I_, T_ = L_R[dr], L_I[dr], L_T[dr]
                ak, ik, tk = ('lr', dr), ('li', dr), ('lt', dr)
                vtt(I_[:], I_[:], L_XC[:], ALU.mult, [ik] + XK, [ik])
                vtt(I_[:], I_[:], T_[:], ALU.mult, [ik, tk], [ik])
                e = 0 if dr == 0 else T - 1
                hh = h0[:, l, dr * 2 + cc:dr * 2 + cc + 1]
                vstt(I_[:, e:e + 1], A[:, e:e + 1], hh, I_[:, e:e + 1], ALU.mult, ALU.add, [ik, ak] + CK, [ik])
                if dr == 0:
                    vts(A[:, 256:T:256], A[:, 256:T:256], flags[:, 0:1], None, ALU.mult, None, [ak] + CK, [ak])
                    P.op('dve', lambda A=A, I_=I_, T_=T_: nc.vector.tensor_tensor_scan(
                        out=T_[:], data0=A[:], data1=I_[:], initial=0.0, op0=ALU.mult, op1=ALU.add), [ak, ik, tk], [tk])
                    vcopy(st[:, (l * 4 + cc) * 8:(l * 4 + cc) * 8 + 8], T_[:, 255:T:256], [tk], ['st'])
                else:
                    vts(A[:, 255:T - 1:256], A[:, 255:T - 1:256], flags[:, 0:1], None, ALU.mult, None, [ak] + CK, [ak])
                    P.op('dve', lambda A=A, I_=I_, T_=T_: nc.vector.tensor_tensor_scan(
                        out=T_[:, ::-1], data0=A[:, ::-1], data1=I_[:, ::-1], initial=0.0, op0=ALU.mult, op1=ALU.add),
                        [ak, ik, tk], [tk])
                    vcopy(st[:, (l * 4 + 2 + cc) * 8:(l * 4 + 2 + cc) * 8 + 8], T_[:, 0:T:256], [tk], ['st'])
            vtt(L_T[0][:], L_T[0][:], L_T[1][:], ALU.add, [('lt', 0), ('lt', 1)], [('lt', 0)])
            vtt(LRUOUT[:, cc, :], L_T[0][:], LRUG[:, cc, :], ALU.mult, [('lt', 0), ('lrug', cc)], [('lruout', cc)])

    def attention(l):
        SKEW = 2
        sc_d = 32 ** -0.5
        heads = []
        pair = 0
        for c in range(4):
            for qb in range(NB):
                for hh in range(2):
                    p0 = hh * 64
                    heads.append(dict(kind='g', qrows=QT[p0:p0 + 64, c, bs(qb)], p0=p0, nk=64,
                                      kfn=(lambda kt, hh=hh: KZ[hh][:, kt * 128:(kt + 1) * 128]),
                                      vfn=(lambda kt, hh=hh: VG[:, kt, hh, 0:65]), scale=0.125, qb=qb, hh=hh,
                                      comp=0, pair=pair, last=(hh == 1), dst=2 + c))
                pair += 1
        for c in range(2):
            for qb in range(NB):
                for hh in range(2):
                    for comp in range(2):
                        p0 = hh * 64 + comp * 32
                        heads.append(dict(kind='d', qrows=DQT[p0:p0 + 32, c, bs(qb)], p0=p0, nk=32,
                                          kfn=(lambda kt, n=4 * c + 2 * hh + comp: DKZ[n // 3][:, kt * 128:(kt + 1) * 128]),
                                          slot=(4 * c + 2 * hh + comp) % 3,
                                          vfn=(lambda kt, h=2 * c + hh: VD[:, kt, h, 0:65]), scale=sc_d, qb=qb, hh=hh,
                                          comp=comp, pair=pair, last=(hh == 1 and comp == 1), dst=6 + c))
                pair += 1
        KP = KT // 2
        tiles = [(hi, kp) for hi in range(len(heads)) for kp in range(KP)]
        NTL = len(tiles)

        def accv(hi):
            ab = 4 + hi % 2
            v = ps[:, ab, 0:260].rearrange("p (j c) -> p j c", j=4)
            return ab, v

        def finish(H, hi):
            oi = H['pair'] % 2
            hh = H['hh']
            ab, v = accv(hi)
            acc4, den4 = v[:, :, 0:64], v[:, :, 64:65]
            PACC = [('ps', ab)]
            rec = sml[:, 0:4].unsqueeze(2)
            P.op('dve', lambda: nc.vector.reciprocal(out=rec, in_=den4), PACC, ['smlr'])
            recb = rec.broadcast_to([128, 4, 64])
            if H['kind'] == 'g':
                vtt(OTOK[oi][:, :, hh * 64:(hh + 1) * 64], acc4, recb, ALU.mult, PACC + ['smlr'], [('otok', oi)])
            elif H['comp'] == 0:
                vtt(DD1[:], acc4, recb, ALU.mult, PACC + ['smlr'], ['dd1'])
            else:
                vts(sml[:, 0:4], sml[:, 0:4], nlam[:, l:l + 1], None, ALU.mult, None, ['smlr', 'nlam'], ['smlr'])
                vtt(DD2[:], acc4, recb, ALU.mult, PACC + ['smlr'], ['dd2'])

        def finish_b(H):
            oi = H['pair'] % 2
            hh = H['hh']
            vtt(DD2[:], DD2[:], DD1[:], ALU.add, ['dd2', 'dd1'], ['dd2'])
            vtt(DD1[:], DD2[:], DD2[:], ALU.mult, ['dd2', 'dd1'], ['dd1'])
            P.op('dve', lambda: nc.vector.reduce_sum(out=sml[:, 8:12], in_=DD1[:], axis=mybir.AxisListType.X),
                 ['dd1'], ['smls'])
            vts(sml[:, 8:12], sml[:, 8:12], 1.0 / 64, EPS, ALU.mult, ALU.add, ['smls'], ['smls'])
            vtt(sml[:, 8:12], sml[:, 8:12], mhalf[:, 0:4], ALU.pow, ['smls', 'mhalf'], ['smls'], eng='pool')
            vtt(DD2[:], DD2[:], sml[:, 8:12].unsqueeze(2).broadcast_to([128, 4, 64]), ALU.mult, ['dd2', 'smls'], ['dd2'])
            vtt(OTOK[oi][:, :, hh * 64:(hh + 1) * 64], DD2[:], dg[:, l, :].unsqueeze(1).broadcast_to([128, 4, 64]),
                ALU.mult, ['dd2', 'dg'], [('otok', oi)])

        def transposes(H):
            oi = H['pair'] % 2
            tpb = ps[:, 6, :].bitcast(BF16)
            for j in range(4):
                P.op('pe', lambda j=j: nc.tensor.transpose(tpb[:, j * 128:(j + 1) * 128], OTOK[oi][:, j, :], ident[:]),
                     [('otok', oi), 'ident'], [('ps', 6)])
            vcopy(actT[:, H['dst'], bs(H['qb'])], tpb[:, 0:BLK], [('ps', 6)], [('h', H['dst'], H['qb'])])

        SKEW = 1
        deferred = []
        for i in range(NTL + SKEW):
            if l == 0 and DEPTH > 1 and i % 24 == 12:
                ada1_step()
            if i < NTL:
                hi, kp = tiles[i]
                H = heads[hi]
                zi = hi % 2
                s = i % 2
                if kp == 0:
                    memset(QZ[zi][:], 0.0, [('qz', zi)], eng='pool')
                    if H['kind'] == 'g':
                        vcopy(QZ[zi][0:64, :], H['qrows'], [('qkv',)], [('qz', zi)], eng='pool')
                        vcopy(QZ[zi][64:73, :], mBb[64:73, H['qb'], :], ['mB'], [('qz', zi)], eng='pool')
                    else:
                        r0 = H['slot'] * 32
                        vcopy(QZ[zi][r0:r0 + 32, :], H['qrows'], [('qkv',)], [('qz', zi)], eng='pool')
                        vcopy(QZ[zi][96:105, :], mBb[96:105, H['qb'], :], ['mB'], [('qz', zi)], eng='pool')
                for t in range(2):
                    kt = 2 * kp + t
                    bk = 2 * s + t
                    mm(ps[:, bk, :], H['kfn'](kt), QZ[zi][:], True, True, [('qkv',), ('qz', zi)], [('ps', bk)])
                act(PTt[s][:], ps[:, 2 * s:2 * s + 2, :].rearrange("p a b -> p (a b)"), AF.Exp,
                    [('ps', 2 * s), ('ps', 2 * s + 1)], [('pt', s)], scale=H['scale'])
            if i >= SKEW:
                hi, kp = tiles[i - SKEW]
                H = heads[hi]
                s = (i - SKEW) % 2
                ab, v = accv(hi)
                for t in range(2):
                    kt = 2 * kp + t
                    for j in range(4):
                        first = (kt == 0 and j == 0)
                        P.op('pe', lambda t=t, j=j, kt=kt, first=first, v=v, s=s, H=H: nc.tensor.matmul(
                            v[:, j, :], lhsT=PTt[s][:, t * BLK + j * 128:t * BLK + (j + 1) * 128], rhs=H['vfn'](kt),
                            start=first, stop=(kt == KT - 1), skip_group_check=True),
                            [('pt', s), ('qkv',)], [('ps', ab)])
                if kp == KP - 1:
                    finish(H, hi)
                    if H['kind'] == 'd' and H['comp'] == 1:
                        deferred.append((i + 2, 'b', H))
                        if H['last']:
                            deferred.append((i + 4, 't', H))
                    elif H['last']:
                        deferred.append((i + 1, 't', H))
            deferred.sort(key=lambda d: d[0])
            while deferred and deferred[0][0] <= i:
                d = deferred.pop(0)
                (finish_b if d[1] == 'b' else transposes)(d[2])
        while deferred:
            d = deferred.pop(0)
            (finish_b if d[1] == 'b' else transposes)(d[2])

    def ffn(l, g2, MK):
        H2 = T // 2
        nps = [0]
        nfo = [0]
        nhp = [0]

        def load_up(g):
            wi = g % 2
            dma(FWS[:], d_wup[l][:, g * 512:(g + 1) * 512].rearrange("(kc p) n -> p kc n", p=128), 'ld_fu',
                writes=['fws'])
            vcopy(FWB[wi][:], FWS[:], ['fws'], [('fwb', wi)])

        def load_dn(g):
            wi = g % 2
            dma(FDS[:], d_wdn[l][g * 256:(g + 1) * 256, :].rearrange("(j p) n -> p j n", p=128), 'ld_fd', writes=['fds'])
            vcopy(FDB[wi][:], FDS[:], ['fds'], [('fdb', wi)])

        def down_piece(g, k0, k1):
            wi = g % 2
            for k in range(k0, k1):
                oc, b = k // NB, k % NB
                n = nps[0]
                nps[0] += 1
                pbk = 4 + n % 4
                for jj in range(2):
                    mm(ps[:, pbk, :], FDB[wi][:, jj, oc * 128:(oc + 1) * 128], FACT[wi][:, jj, bs(b)], jj == 0, jj == 1,
                       [('fdb', wi), ('fact', wi)], [('ps', pbk)])
                if n % 8 not in (1, 4, 6):
                    vstt(xT[:, oc, bs(b)], ps[:, pbk, :], g2[:, oc:oc + 1], xT[:, oc, bs(b)], ALU.mult, ALU.add,
                         [('ps', pbk), ('x', oc, b)] + MK, [('x', oc, b)])
                else:
                    fi = nfo[0] % 2
                    nfo[0] += 1
                    act(FOUT[fi][:], ps[:, pbk, :], AF.Identity, [('ps', pbk)] + MK, [('fout', fi)], scale=g2[:, oc:oc + 1])
                    vtt(xT[:, oc, bs(b)], xT[:, oc, bs(b)], FOUT[fi][:], ALU.add, [('fout', fi), ('x', oc, b)],
                        [('x', oc, b)], eng='pool')

        def up(g, gd):
            wi = g % 2
            piece = 0
            for jj in range(2):
                j = 2 * g + jj
                for part in range(2):
                    col = (part * 2 + jj) * 128
                    fc = part * 22 + j
                    cw = lambda k, fc=fc: fcw[:, l, fc * 3 + k:fc * 3 + k + 1]
                    Y, U0, U2 = FY[part], FU0[part], FU2[part]
                    for h in range(2):
                        pr = nhp[0] % 2
                        nhp[0] += 1
                        pb = 2 * pr
                        for k in range(2):
                            b = 2 * h + k
                            for kc in range(8):
                                mm(ps[:, pb + k, :], FWB[wi][:, kc, col:col + 128], actT[:, kc, bs(b)], kc == 0, kc == 7,
                                   [('fwb', wi), ('h', kc, b)], [('ps', pb + k)])
                        PK = [('ps', pb), ('ps', pb + 1)]
                        p2 = ps[:, pb:pb + 2, :].rearrange("p a b -> p (a b)")
                        t0 = h * H2
                        yk, u0k, u2k = ('fy', part, h), ('fu0', part), ('fu2', part)
                        act(Y[:, t0:t0 + H2], p2, AF.Identity, PK + CK, [yk], bias=fcb[:, l, fc:fc + 1], scale=cw(1))
                        fl = flags[:, 0:1]
                        y0, y1 = ('fy', part, 0), ('fy', part, 1)
                        if h == 0:
                            act(U0[:, 1:H2 + 1], p2, AF.Copy, PK + CK, [u0k], scale=cw(0))
                            act(U2[:, 0:H2 - 1], p2[:, 1:H2], AF.Copy, PK + CK, [u2k], scale=cw(2))
                            vts(U0[:, 256:H2 + 1:256], U0[:, 256:H2 + 1:256], fl, None, ALU.mult, None, [u0k] + CK, [u0k])
                            vts(U2[:, 255:H2 - 1:256], U2[:, 255:H2 - 1:256], fl, None, ALU.mult, None, [u2k] + CK, [u2k])
                            vtt(Y[:, 0:H2], Y[:, 0:H2], U0[:, 0:H2], ALU.add, [y0, u0k], [y0])
                            vtt(Y[:, 0:H2 - 2], Y[:, 0:H2 - 2], U2[:, 0:H2 - 2], ALU.add, [y0, u2k], [y0])
                        else:
                            act(U0[:, H2 + 1:T], p2[:, 0:H2 - 1], AF.Copy, PK + CK, [u0k], scale=cw(0))
                            act(U2[:, H2 - 1:T - 1], p2, AF.Copy, PK + CK, [u2k], scale=cw(2))
                            vts(U0[:, H2 + 256:T:256], U0[:, H2 + 256:T:256], fl, None, ALU.mult, None, [u0k] + CK, [u0k])
                            vts(U2[:, H2 - 1:T - 1:256], U2[:, H2 - 1:T - 1:256], fl, None, ALU.mult, None, [u2k] + CK, [u2k])
                            vtt(Y[:, H2:T], Y[:, H2:T], U0[:, H2:T], ALU.add, [y1, u0k], [y1])
                            vtt(Y[:, H2 - 2:T], Y[:, H2 - 2:T], U2[:, H2 - 2:T], ALU.add, [y0, y1, u2k], [y0, y1])
                        if gd is not None:
                            down_piece(gd, piece * 4, piece * 4 + 4)
                        piece += 1
                        if piece == 4 and g + 1 < 11:
                            load_up(g + 1)
                YK = [('fy', p_, h_) for p_ in range(2) for h_ in range(2)]
                act(FY[1][:], FY[1][:], AF.Silu, YK, [('fy', 1, 0), ('fy', 1, 1)])
                vtt(FACT[wi][:, jj, :], FY[1][:], FY[0][:], ALU.mult, YK, [('fact', wi)])
            if g + 1 < 11:
                load_dn(g + 1)

        memset(FU0[0][:, 0:1], 0.0, [('fu0', 0)])
        memset(FU0[1][:, 0:1], 0.0, [('fu0', 1)])
        memset(FU2[0][:, T - 1:T], 0.0, [('fu2', 0)])
        memset(FU2[1][:, T - 1:T], 0.0, [('fu2', 1)])
        load_up(0)
        load_dn(0)
        up(0, None)
        for g in range(11):
            if g + 1 < 11:
                up(g + 1, g)
            else:
                down_piece(g, 0, 32)

    ada(0)
    P.barrier()
    for l in range(n_layers):
        layer(l)

    def of(kc, b, t, tk):
        i = (kc + b) % 2
        act(QN[i][:], t[:], AF.Identity, [tk, 'fng'], [('qn', i)], scale=fng[:, kc:kc + 1])
        dma(o_yT[kc * 128:(kc + 1) * 128, bs(b)], QN[i][:], 'st_y', reads=[('qn', i)])
    norm_mod(None, None, None, of)
    dma(o_st[:, :], st[:], 'st_s', reads=['st'])
    P.emit()
    return nc


def _rope_tables(dim, is_sample):
    if not is_sample:
        return np.stack([np.ones((128, T), np.float32), np.zeros((128, T), np.float32)])
    GRID_W = 64
    rows = T // GRID_W
    t_row = np.repeat(np.arange(rows, dtype=np.float32), GRID_W)
    t_col = np.tile(np.arange(GRID_W, dtype=np.float32), rows)
    axis_dim = dim // 2
    inv = (np.float32(10000.0) ** (-np.arange(0, axis_dim, 2, dtype=np.float32) / np.float32(axis_dim))).astype(np.float32)
    ar = t_row[:, None] * inv
    ac = t_col[:, None] * inv
    ang = np.concatenate([ar, ar, ac, ac], axis=-1)
    cos = np.cos(ang).astype(np.float32).T
    sin = np.sin(ang).astype(np.float32).T
    rep = 128 // dim
    return np.stack([np.tile(cos, (rep, 1)), np.tile(sin, (rep, 1))]).astype(np.float32)


def _perm_matrix(dim):
    q = dim // 4
    Pm = np.zeros((128, 128), np.float32)
    for hb in range(0, 128, dim):
        for d in range(dim):
            quarter = d // q
            m = hb + d
            if quarter % 2 == 0:
                Pm[hb + d + q, m] = -1.0
            else:
                Pm[hb + d - q, m] = 1.0
    return Pm


_CACHE = {}


def kernel(**inp):
    f32 = np.float32
    g = {k: np.asarray(v) for k, v in inp.items()}
    L = DEPTH

    def c(a):
        return np.ascontiguousarray(a, dtype=f32)

    def pk8(v):
        return c(v.reshape(8, 128).T)

    w_in = g['w_in']
    colsF = []
    colsF += list(range(0, 256))
    colsF += list(range(256, 512))
    for cch in range(4):
        for h in (cch, cch + 4):
            colsF += list(range(512 + 64 * h, 512 + 64 * h + 64))
    colsF += list(range(1024, 1152))
    colsF += list(range(1280, 1536))
    colsF += list(range(1536, 1792))
    colsV = list(range(1152, 1280)) + list(range(1792, 2048))
    w_inF = c(w_in[:, :, colsF])
    w_inV = c(w_in[:, :, colsV])
    rows_o = list(range(0, 256))
    for cch in range(4):
        for h in (cch, cch + 4):
            rows_o += list(range(256 + 64 * h, 256 + 64 * h + 64))
    rows_o += list(range(768, 1024))
    w_outP = c(g['w_out'][:, rows_o, :])
    up = g['ffn_w_up']
    cols_up = []
    for grp in range(11):
        for part in range(2):
            for jj in range(2):
                j = 2 * grp + jj
                cols_up += list(range(part * D_FF + 128 * j, part * D_FF + 128 * j + 128))
    w_upP = c(up[:, :, cols_up])
    shared = {
        'ada_w': c(g['ada_w']),
        'ada_b': c(g['ada_b'].reshape(L, 48, 128).transpose(0, 2, 1)),
        'n1g': c(g['norm1_g'].reshape(L, 8, 128).transpose(0, 2, 1)),
        'n2g': c(g['norm2_g'].reshape(L, 8, 128).transpose(0, 2, 1)),
        'fng': pk8(g['final_norm_g']),
        'w_inF': w_inF, 'w_inV': w_inV, 'w_outP': w_outP, 'w_upP': w_upP, 'w_dn': c(g['ffn_w_down']),
        'lcw': c(g['lru_conv_w'].reshape(L, 4, 2, 128).transpose(0, 3, 2, 1).reshape(L, 128, 8)),
        'lcb': c(g['lru_conv_b'].reshape(L, 2, 128).transpose(0, 2, 1)),
        'lam': c(g['lru_lambda'].reshape(L, 2, 2, 128).transpose(0, 3, 1, 2).reshape(L, 128, 4)),
        'qkg': c(np.stack([np.tile(g['gqa_q_norm_g'], (1, 2)), np.tile(g['gqa_k_norm_g'], (1, 2))], axis=-1)),
        'dl': c(g['diff_lambda'].reshape(L, 128)),
        'dg': c(g['diff_norm_g']),
        'fcw': c(g['ffn_conv_w'].reshape(L, 3, 44, 128).transpose(0, 3, 2, 1).reshape(L, 128, 132)),
        'fcb': c(g['ffn_conv_b'].reshape(L, 44, 128).transpose(0, 2, 1)),
        'permg': _perm_matrix(64), 'permd': _perm_matrix(32),
        'ident': np.eye(128, dtype=f32),
    }
    gwsrc = g['lru_gate_w']
    gw = np.zeros((L, 8, 128, 128), f32)
    gbsrc = g['lru_gate_b']
    gbv = np.zeros((L, 128, 8), f32)
    for l in range(L):
        for dr in range(2):
            for cc in range(2):
                for ri in range(2):
                    idx = dr * 4 + cc * 2 + ri
                    for bl in range(2):
                        blk = 2 * cc + bl
                        gw[l, idx, bl * 64:(bl + 1) * 64, bl * 64:(bl + 1) * 64] = gwsrc[l, dr, blk, :, ri * 64:(ri + 1) * 64]
                        gbv[l, bl * 64:(bl + 1) * 64, idx] = gbsrc[l, dr, blk, ri * 64:(ri + 1) * 64]
    shared['gw'] = gw
    shared['gb'] = gbv

    mA4 = np.zeros((128, 9 * 128), f32)
    for j in range(9):
        mA4[j, j * 128:(j + 1) * 128] = 1.0

    def mB4_for(is_sample):
        m = np.zeros((128, 4 * BLK), f32)
        if not is_sample:
            qseg = np.arange(T) // 256 + 1
            for j in range(9):
                m[j, :] = np.where(qseg == j, 0.0, NEG)
                m[64 + j, :] = m[j, :]
                m[96 + j, :] = m[j, :]
        return m

    import ml_dtypes
    Ak = np.zeros((9, TK), f32)
    for j in range(9):
        Ak[j, j * 256:(j + 1) * 256] = 1.0
    Ak = Ak.astype(ml_dtypes.bfloat16)

    ropeg_s, roped_s = _rope_tables(64, True), _rope_tables(32, True)
    ropeg_p, roped_p = _rope_tables(64, False), _rope_tables(32, False)
    mB_s, mB_p = mB4_for(True), mB4_for(False)

    xs, xp = g['x_sample'], g['x_prompt']
    in_maps = []
    for core in range(8):
        m = dict(shared)
        m['Ak'] = Ak
        if core < 4:
            b = core
            m['xT'] = c(xs[b].T)
            m['cT'] = pk8(g['c'][b])
            m['flags'] = c(np.tile(np.array([[1.0, 0.0]], f32), (128, 1)))
            m['mB4'] = mB_s
            m['ropeg'], m['roped'] = ropeg_s, roped_s
            m['ckT'] = c(g['cache_gqa_k'][b].reshape(L, 256, 128).transpose(0, 2, 1))
            m['cv'] = c(g['cache_gqa_v'][b].reshape(L, 256, 128))
            m['cdkT'] = c(g['cache_diff_k'][b].reshape(L, 256, 256).transpose(0, 2, 1))
            m['cdv'] = c(g['cache_diff_v'][b].reshape(L, 256, 256))
            m['h0'] = c(g['state_lru'][b].reshape(L, 2, 2, 128).transpose(0, 3, 1, 2).reshape(L, 128, 4))
        else:
            i = (core - 4) % 2
            m['xT'] = c(xp[8 * i:8 * i + 8].reshape(T, D_MODEL).T)
            m['cT'] = pk8(g['c_ctx'])
            m['flags'] = c(np.tile(np.array([[0.0, 1.0]], f32), (128, 1)))
            m['mB4'] = mB_p
            m['ropeg'], m['roped'] = ropeg_p, roped_p
            m['ckT'] = np.zeros((L, 128, 256), f32)
            m['cv'] = np.zeros((L, 256, 128), f32)
            m['cdkT'] = np.zeros((L, 256, 256), f32)
            m['cdv'] = np.zeros((L, 256, 256), f32)
            m['h0'] = np.zeros((L, 128, 4), f32)
        in_maps.append(m)

    if 'nc' not in _CACHE:
        _CACHE['nc'] = build_program()
    nc = _CACHE['nc']
    res = run_bass_kernel_spmd(nc, in_maps, core_ids=list(range(8)))
    R = res.results

    y_sample = np.stack([R[b]['yT'].T for b in range(4)]).astype(f32)
    yp, ks, vs, dks, dvs, sts = [], [], [], [], [], []
    for i in range(2):
        r = R[4 + i]
        yT = r['yT']
        for s in range(8):
            sl = slice(256 * s, 256 * s + 256)
            yp.append(yT[:, sl].T)
            ks.append(np.stack([r['okT'][l][:, sl].T.reshape(256, 2, 64) for l in range(L)]))
            vs.append(np.stack([r['ov'][l][sl, :].reshape(256, 2, 64) for l in range(L)]))
            dks.append(np.stack([r['odkT'][l][:, sl].T.reshape(256, 4, 2, 32) for l in range(L)]))
            dvs.append(np.stack([r['odv'][l][sl, :].reshape(256, 4, 64) for l in range(L)]))
            o = r['ost'].reshape(128, L, 2, 2, 8)[:, :, :, :, s]
            sts.append(o.transpose(1, 2, 3, 0).reshape(L, 2, 256))
    out = (np.stack(yp).astype(f32), y_sample, np.stack(ks).astype(f32), np.stack(vs).astype(f32),
           np.stack(dks).astype(f32), np.stack(dvs).astype(f32), np.stack(sts).astype(f32))
    return tuple(np.ascontiguousarray(o) for o in out)
```

```python
import contextlib
import numpy as np
import concourse.bass as bass
import concourse.mybir as mybir
from concourse.bass_utils import run_bass_kernel_spmd

F32 = mybir.dt.float32
BF16 = mybir.dt.bfloat16
ALU = mybir.AluOpType
AF = mybir.ActivationFunctionType

D_MODEL = 1024
T = 2048
BLK = 512
NB = 4
NT = 16
KT = 18
TK = 2304
DEPTH = 2
D_FF = 2816
EPS = 1e-6
NEG = -30000.0


class Prog:
    def __init__(self, nc):
        self.nc = nc
        self.E = {'pe': nc.tensor, 'act': nc.scalar, 'dve': nc.vector, 'pool': nc.gpsimd, 'sp': nc.sync}
        self.ops = []

    def op(self, eng, fn, reads=(), writes=(), dma=None):
        self.ops.append((eng, fn, tuple(reads), tuple(writes), dma))

    def barrier(self):
        self.ops.append(None)

    def emit(self, final_eng='sp'):
        nc = self.nc
        ops = self.ops
        n = len(ops)
        last_w = {}
        readers = {}
        deps = [None] * n
        marked = [False] * n
        last_eng = {}
        pend_dma = []
        bar_set = None
        bar_done = set()
        for i, rec in enumerate(ops):
            if rec is None:
                s = set(last_eng.values()) | set(pend_dma)
                if bar_set is not None:
                    s |= bar_set
                bar_set = s
                bar_done = set()
                pend_dma = []
                continue
            eng, fn, rd, wr, dma = rec
            d = set()
            if bar_set is not None and eng not in bar_done:
                d |= bar_set
                bar_done.add(eng)
            for t in rd:
                j = last_w.get(t)
                if j is not None:
                    d.add(j)
            for t in wr:
                j = last_w.get(t)
                if j is not None:
                    d.add(j)
                r = readers.get(t)
                if r:
                    d.update(r[0].values())
                    d.update(r[1])
            d.discard(i)
            dd = []
            for j in d:
                ej, _, _, _, dmaj = ops[j]
                if dmaj is None and dma is None and ej == 'pe' and eng == 'pe':
                    continue
                dd.append(j)
                marked[j] = True
            deps[i] = dd
            for t in rd:
                r = readers.get(t)
                if r is None:
                    r = readers[t] = ({}, [])
                if dma is None:
                    r[0][eng] = i
                else:
                    r[1].append(i)
            for t in wr:
                last_w[t] = i
                readers[t] = ({}, [])
            if dma is None:
                last_eng[eng] = i
            else:
                pend_dma.append(i)
        cnt = {}
        mile = [None] * n
        for i, rec in enumerate(ops):
            if rec is None:
                continue
            eng, fn, rd, wr, dma = rec
            if dma is not None:
                cnt[dma] = cnt.get(dma, 0) + 16
                mile[i] = (dma, cnt[dma])
            elif marked[i]:
                k = 'E_' + eng
                cnt[k] = cnt.get(k, 0) + 1
                mile[i] = (k, cnt[k])
        sems = {}
        with contextlib.ExitStack() as st:
            for k in cnt:
                sems[k] = st.enter_context(nc.semaphore(k))
            waited = {e: {} for e in self.E}
            for i, rec in enumerate(ops):
                if rec is None:
                    continue
                eng, fn, rd, wr, dma = rec
                need = {}
                for j in deps[i]:
                    s, v = mile[j]
                    if need.get(s, 0) < v:
                        need[s] = v
                for s, v in need.items():
                    if waited[eng].get(s, 0) < v:
                        self.E[eng].wait_ge(sems[s], v)
                        waited[eng][s] = v
                ins = fn()
                if mile[i] is not None:
                    s, v = mile[i]
                    ins.then_inc(sems[s], 16 if dma is not None else 1)
            for k, v in cnt.items():
                if waited[final_eng].get(k, 0) < v:
                    self.E[final_eng].wait_ge(sems[k], v)
        return cnt


def build_program(n_layers=DEPTH, debug=False):
    nc = bass.Bass("TRN2", target_bir_lowering=False)
    P = Prog(nc)

    def din(name, shape):
        return nc.dram_tensor(name, list(shape), F32, kind="ExternalInput").ap()

    def dout(name, shape):
        return nc.dram_tensor(name, list(shape), F32, kind="ExternalOutput").ap()

    d_xT = din("xT", [D_MODEL, T])
    d_cT = din("cT", [128, 8])
    d_flags = din("flags", [128, 2])
    d_mB = din("mB4", [128, 4 * BLK])
    d_Ak = nc.dram_tensor("Ak", [9, TK], BF16, kind="ExternalInput").ap()
    d_ropeg = din("ropeg", [2, 128, T])
    d_roped = din("roped", [2, 128, T])
    d_permg = din("permg", [128, 128])
    d_permd = din("permd", [128, 128])
    d_ckT = din("ckT", [DEPTH, 128, 256])
    d_cv = din("cv", [DEPTH, 256, 128])
    d_cdkT = din("cdkT", [DEPTH, 256, 256])
    d_cdv = din("cdv", [DEPTH, 256, 256])
    d_h0 = din("h0", [DEPTH, 128, 4])
    d_n1 = din("n1g", [DEPTH, 128, 8])
    d_n2 = din("n2g", [DEPTH, 128, 8])
    d_fg = din("fng", [128, 8])
    d_adaw = din("ada_w", [DEPTH, D_MODEL, 6 * D_MODEL])
    d_adab = din("ada_b", [DEPTH, 128, 48])
    d_winC = din("w_inC", [DEPTH, 16, 128, 8 * 128])
    d_wout = din("w_outP", [DEPTH, D_MODEL, D_MODEL])
    d_lcw = din("lcw", [DEPTH, 128, 8])
    d_lcb = din("lcb", [DEPTH, 128, 2])
    d_gw = din("gw", [DEPTH, 8, 128, 128])
    d_gb = din("gb", [DEPTH, 128, 8])
    d_lam = din("lam", [DEPTH, 128, 4])
    d_qkg = din("qkg", [DEPTH, 128, 2])
    d_dl = din("dl", [DEPTH, 128])
    d_dg = din("dg", [DEPTH, 64])
    d_wup = din("w_upP", [DEPTH, D_MODEL, 11 * 512])
    d_fcw = din("fcw", [DEPTH, 128, 44 * 3])
    d_fcb = din("fcb", [DEPTH, 128, 44])
    d_wdn = din("w_dn", [DEPTH, D_FF, D_MODEL])

    o_yT = dout("yT", [D_MODEL, T])
    o_kT = dout("okT", [DEPTH, 128, T])
    o_v = dout("ov", [DEPTH, T, 128])
    o_dkT = dout("odkT", [DEPTH, 256, T])
    o_dv = dout("odv", [DEPTH, T, 256])
    o_st = dout("ost", [128, 64])

    base0 = (nc.sbuf_base + 31) // 32 * 32
    ARENA = 212800
    arena = nc.alloc_sbuf_tensor("arena", [128, ARENA], mybir.dt.uint8)
    cnt_name = [0]

    def at(off, shape, dt):
        cnt_name[0] += 1
        assert off % 32 == 0 and off >= 0, off
        nbytes = int(np.prod(shape[1:])) * (4 if dt == F32 else 2)
        assert off + nbytes <= ARENA, (off, nbytes)
        return nc.alloc_sbuf_tensor_at("t%d" % cnt_name[0], list(shape), dt, offset=base0 + off)

    OFF_X, OFF_ACT, OFF_C, OFF_XR = 0, 65536, 98304, 115456
    xT = at(OFF_X, [128, 8, T], F32)
    actT = at(OFF_ACT, [128, 8, T], BF16)

    cptr = [OFF_C]

    def calloc(shape, dt):
        nbytes = int(np.prod(shape[1:])) * (4 if dt == F32 else 2)
        nbytes = (nbytes + 31) // 32 * 32
        t = at(cptr[0], shape, dt)
        cptr[0] += nbytes
        assert cptr[0] <= OFF_XR
        return t

    ident = calloc([128, 128], BF16)
    ones = calloc([128, 128], BF16)
    bones = calloc([128, 128], BF16)
    permg = calloc([128, 128], BF16)
    permd = calloc([128, 128], BF16)
    gwb = calloc([128, 8, 128], BF16)
    mBb = calloc([128, 4, BLK], BF16)
    mod = calloc([128, 2, 48], F32)
    adab = calloc([128, 2, 48], F32)
    n1g = calloc([128, 2, 8], F32)
    n2g = calloc([128, 2, 8], F32)
    fng = calloc([128, 8], F32)
    gs1 = calloc([128, 2, 8], F32)
    gs2 = calloc([128, 2, 8], F32)
    cT = calloc([128, 8], F32)
    siluc = calloc([128, 8], F32)
    flags = calloc([128, 2], F32)
    lcw = calloc([128, 2, 8], F32)
    lcwn = calloc([128, 2, 8], F32)
    lcb = calloc([128, 2, 2], F32)
    gb = calloc([128, 2, 8], F32)
    lam = calloc([128, 2, 4], F32)
    lsc = calloc([128, 2, 4], F32)
    lsc2 = calloc([128, 2, 4], F32)
    h0 = calloc([128, 2, 4], F32)
    qkg = calloc([128, 2, 2], F32)
    dlt = calloc([128, 64], F32)
    dle = calloc([128, 4], F32)
    nlam = calloc([128, 2], F32)
    dg = calloc([128, 2, 64], F32)
    fcw = calloc([128, 2, 44 * 3], F32)
    fcwn = calloc([128, 2, 44 * 3], F32)
    fcb = calloc([128, 2, 44], F32)
    st = calloc([128, 64], F32)
    sml = calloc([128, 16], F32)
    sstage = calloc([128, 128], F32)

    X = OFF_XR
    LRUOUT = at(X + 0, [128, 2, T], BF16)
    QT = at(X + 8192, [128, 4, T], BF16)
    DQT = at(X + 24576, [128, 2, T], BF16)
    KZ = [at(X + 32768 + 4608 * i, [128, TK], BF16) for i in range(2)]
    DKZ = [at(X + 41984 + 4608 * i, [128, TK], BF16) for i in range(3)]
    VG = at(X + 55808, [128, KT, 2, 66], BF16)
    VD = at(X + 60576, [128, KT, 4, 66], BF16)
    WS = [at(X + 70080 + 4096 * i, [128, 8, 128], F32) for i in range(2)]
    WB = [at(X + 78272 + 2048 * i, [128, 8, 128], BF16) for i in range(2)]
    PT0 = X + 82368
    SQ = [at(PT0 + 1024 * i, [128, BLK], BF16) for i in range(2)]
    RT = [at(PT0 + 2048, [128, BLK], F32) for i in range(2)]
    QN = [at(PT0 + 4096 + 2048 * i, [128, BLK], F32) for i in range(2)]
    T1 = [at(PT0 + 8192 + 2048 * i, [128, BLK], F32) for i in range(2)]
    VST = [at(PT0 + 12288, [128, 4, 128], F32), at(PT0 + 4096 + 2048, [128, 4, 128], F32)]
    ROPET = at(X + 70080, [128, 4, BLK], F32)
    T2 = [at(X + 78272 + 2048 * i, [128, BLK], F32) for i in range(2)]
    LRUX = at(X + 8192, [128, 2, T], BF16)
    LRUG = at(X + 16384, [128, 2, T], BF16)
    L_R = [at(X + 24576 + 24576 * d, [128, T], F32) for d in range(2)]
    L_I = [at(X + 32768 + 24576 * d, [128, T], F32) for d in range(2)]
    L_T = [at(X + 40960 + 24576 * d, [128, T], F32) for d in range(2)]
    L_XC = at(X + 73728, [128, T], F32)
    L_XCB = at(X + 81920, [128, T], BF16)
    PTt = [at(X + 70080 + 2048 * i, [128, 2 * BLK], BF16) for i in range(2)]
    OTOK = [at(X + 74176 + 1024 * i, [128, 4, 128], BF16) for i in range(2)]
    DD1 = at(X + 76224, [128, 4, 64], F32)
    DD2 = at(X + 77248, [128, 4, 64], F32)
    QZ = [at(X + 78272 + 1024 * i, [128, BLK], BF16) for i in range(2)]
    ADAS = [at(X + 16384 * i, [128, 8, BLK], F32) for i in range(2)]
    WOS = [at(X + 8192 + 16384 * i, [128, 8, BLK], F32) for i in range(2)]
    WOB = [at(X + 40960 + 8192 * i, [128, 8, BLK], BF16) for i in range(2)]
    FWS = at(X + 0, [128, 8, BLK], F32)
    FWB = [at(X + 16384 + 8192 * i, [128, 8, BLK], BF16) for i in range(2)]
    FDS = at(X + 32768, [128, 2, D_MODEL], F32)
    FDB = [at(X + 40960 + 4096 * i, [128, 2, D_MODEL], BF16) for i in range(2)]
    FY = [at(X + 49152 + 4096 * i, [128, T], BF16) for i in range(2)]
    FU0 = [at(X + 57344 + 4096 * i, [128, T], BF16) for i in range(2)]
    FU2 = [at(X + 65536 + 4096 * i, [128, T], BF16) for i in range(2)]
    FACT = [at(X + 73728 + 8192 * i, [128, 2, T], BF16) for i in range(2)]
    FOUT = [at(X + 90112 + 2048 * i, [128, BLK], F32) for i in range(2)]

    ps = nc.alloc_psum_tensor("ps", [128, 8, BLK], F32)

    def dma(out, in_, key, reads=(), writes=(), eng='sp'):
        P.op(eng, lambda: P.E[eng].dma_start(out=out, in_=in_), reads, writes, dma=key)

    def mm(out, lhsT, rhs, start, stop, reads, writes, tp=None):
        if tp is None:
            P.op('pe', lambda: nc.tensor.matmul(out, lhsT=lhsT, rhs=rhs, start=start, stop=stop), reads, writes)
        else:
            P.op('pe', lambda: nc.tensor.matmul(out, lhsT=lhsT, rhs=rhs, start=start, stop=stop, tile_position=tp),
                 reads, writes)

    def act(out, in_, func, reads, writes, bias=None, scale=None, accum=None):
        kw = {}
        if bias is not None:
            kw['bias'] = bias
        if scale is not None:
            kw['scale'] = scale
        if accum is not None:
            kw['accum_out'] = accum
        P.op('act', lambda: nc.scalar.activation(out=out, in_=in_, func=func, **kw), reads, writes)

    def vtt(out, in0, in1, op, reads, writes, eng='dve'):
        P.op(eng, lambda: P.E[eng].tensor_tensor(out=out, in0=in0, in1=in1, op=op), reads, writes)

    def vts(out, in0, s1, s2, op0, op1, reads, writes, eng='dve'):
        if op1 is None:
            P.op(eng, lambda: P.E[eng].tensor_scalar(out=out, in0=in0, scalar1=s1, scalar2=None, op0=op0), reads, writes)
        else:
            P.op(eng, lambda: P.E[eng].tensor_scalar(out=out, in0=in0, scalar1=s1, scalar2=s2, op0=op0, op1=op1),
                 reads, writes)

    def vstt(out, in0, scalar, in1, op0, op1, reads, writes):
        P.op('dve', lambda: nc.vector.scalar_tensor_tensor(out=out, in0=in0, scalar=scalar, in1=in1, op0=op0, op1=op1),
             reads, writes)

    def vcopy(out, in_, reads, writes, eng='dve'):
        P.op(eng, lambda: P.E[eng].tensor_copy(out=out, in_=in_), reads, writes)

    def vrecip(out, in_, reads, writes):
        P.op('dve', lambda: nc.vector.reciprocal(out=out, in_=in_), reads, writes)

    def memset(ap, val, writes, eng='dve'):
        P.op(eng, lambda: P.E[eng].memset(ap, val), (), writes)

    def bs(b):
        return slice(b * BLK, (b + 1) * BLK)

    for kc in range(8):
        dma(xT[:, kc, :], d_xT[kc * 128:(kc + 1) * 128, :], 'ld_x', writes=[('x', kc, b) for b in range(NB)])
    small_loads = [
        (cT[:], d_cT[:, :], 'cT'), (flags[:], d_flags[:, :], 'flags'),
        (fng[:], d_fg[:, :], 'fng'),
    ]
    for l in range(DEPTH):
        small_loads += [
            (adab[:, l, :], d_adab[l], 'adab'), (n1g[:, l, :], d_n1[l], 'n1g'), (n2g[:, l, :], d_n2[l], 'n2g'),
            (lcw[:, l, :], d_lcw[l], 'lcw'), (lcb[:, l, :], d_lcb[l], 'lcb'), (gb[:, l, :], d_gb[l], 'gb'),
            (lam[:, l, :], d_lam[l], 'lam'), (h0[:, l, :], d_h0[l], 'h0'), (qkg[:, l, :], d_qkg[l], 'qkg'),
            (dg[:, l, :], d_dg[l].partition_broadcast(128), 'dg'),
            (fcw[:, l, :], d_fcw[l], 'fcw'), (fcb[:, l, :], d_fcb[l], 'fcb'),
        ]
    for (o, i, k) in small_loads:
        dma(o, i, 'ld_c', writes=[k])
    CK = ['cT', 'flags', 'fng', 'adab', 'n1g', 'n2g', 'lcw', 'lcb', 'gb', 'lam', 'h0', 'qkg', 'dg', 'fcw', 'fcb']
    memset(ones[:], 1.0, ['ones'])
    memset(bones[:], 0.0, ['bones'])
    memset(bones[0:64, 0:64], 1.0, ['bones'])
    memset(bones[64:128, 64:128], 1.0, ['bones'])
    for (dst, src, k) in ((permg, d_permg, 'permg'), (permd, d_permd, 'permd')):
        dma(sstage[:], src[:, :], 'ld_c2', writes=['sstage'])
        vcopy(dst[:], sstage[:], ['sstage'], [k])
    d_ident = din("ident", [128, 128])
    dma(sstage[:], d_ident[:, :], 'ld_c2', writes=['sstage'])
    vcopy(ident[:], sstage[:], ['sstage'], ['ident'])
    dma(ADAS[1][:, 0:4, :].rearrange("p a b -> p (a b)"), d_mB[:, :], 'ld_c2', writes=['adas1'])
    vcopy(mBb[:], ADAS[1][:, 0:4, :], ['adas1'], ['mB'])
    P.barrier()
    act(siluc[:], cT[:], AF.Silu, CK, ['siluc'])
    act(lsc[:], lam[:], AF.Exp, CK, ['lsc'], scale=-1.0)
    act(lsc[:], lsc[:], AF.Ln, ['lsc'], ['lsc'], bias=1.0)
    vts(lsc2[:], lsc[:], -16.0, None, ALU.mult, None, ['lsc'], ['lsc2'])
    vts(lsc[:], lsc[:], -8.0, None, ALU.mult, None, ['lsc', 'lsc2'], ['lsc'])
    vts(lcwn[:], lcw[:], flags[:, 1:2], -1.0, ALU.mult, ALU.mult, CK, ['lcwn'])
    vts(fcwn[:], fcw[:], flags[:, 1:2], -1.0, ALU.mult, ALU.mult, CK, ['fcwn'])
    import math
    lam_init = [0.8 - 0.6 * math.exp(-0.3 * l) for l in range(DEPTH)]
    for l in range(DEPTH):
        dma(sstage[:], d_dl[l].partition_broadcast(128), 'ld_c2', writes=['sstage'])
        for k in range(2):
            vtt(dlt[:, 0:32], sstage[:, 64 * k:64 * k + 32], sstage[:, 64 * k + 32:64 * k + 64], ALU.mult, ['sstage'], ['dlt'])
            P.op('dve', lambda k=k, l=l: nc.vector.reduce_sum(out=dle[:, 2 * l + k:2 * l + k + 1], in_=dlt[:, 0:32],
                                                            axis=mybir.AxisListType.X), ['dlt'], ['dle'])
        act(dle[:, 2 * l:2 * l + 2], dle[:, 2 * l:2 * l + 2], AF.Exp, ['dle'], ['dle'])
        vts(sml[:, 0:1], dle[:, 2 * l + 1:2 * l + 2], -lam_init[l], None, ALU.add, None, ['dle'], ['sml'])
        vtt(nlam[:, l:l + 1], sml[:, 0:1], dle[:, 2 * l:2 * l + 1], ALU.subtract, ['sml', 'dle'], ['nlam'])
        vts(dg[:, l, :], dg[:, l, :], 1.0 - lam_init[l], None, ALU.mult, None, CK, ['dg'])
    P.barrier()

    ADS = [at(X + 16384 * i, [128, 8, BLK], F32) for i in range(3)]
    ADB = [at(X + 49152 + 8192 * i, [128, 8, BLK], BF16) for i in range(2)]
    ROW = at(X + 65536, [1, 6 * D_MODEL], F32)
    onef = calloc([128, 2], F32)
    mhalf = calloc([128, 4], F32)
    memset(mhalf[:], -0.5, ['mhalf'])
    silucb = calloc([128, 8], BF16)
    memset(onef[:], 1.0, ['onef'])
    vcopy(silucb[:], siluc[:], ['siluc'], ['silucb'])
    adac = [0]

    def ada(l):
        for g in range(12):
            n = adac[0]
            adac[0] += 1
            s3 = n % 3
            s = n % 2
            for hf in range(2):
                dma(ADS[s3][:, 4 * hf:4 * hf + 4, :],
                    d_adaw[l][hf * 512:(hf + 1) * 512, g * 512:(g + 1) * 512].rearrange("(kc p) n -> p kc n", p=128),
                    'ld_ada%d' % s3, writes=[('ads', s3)], eng=('sp' if hf == 0 else 'pool'))
            if n % 2 == 0:
                act(ADB[s][:], ADS[s3][:], AF.Copy, [('ads', s3)], [('adb', s)])
            else:
                vcopy(ADB[s][:], ADS[s3][:], [('ads', s3)], [('adb', s)])
            bank = g % 4
            for kc in range(8):
                mm(ps[0:1, bank, :], silucb[:, kc:kc + 1], ADB[s][:, kc, :], kc == 0, kc == 7,
                   [('adb', s), 'silucb'], [('ps', bank)])
            vcopy(ROW[0:1, g * 512:(g + 1) * 512], ps[0:1, bank, :], [('ps', bank)], ['row'])
        for col in range(48):
            mm(ps[:, 7, col:col + 1], ROW[0:1, col * 128:(col + 1) * 128], onef[0:1, 0:1], True, True,
               ['row', 'onef'], [('ps', 7)])
        vtt(mod[:, l, :], ps[:, 7, 0:48], adab[:, l, :], ALU.add, [('ps', 7)], [('mod', l)])
        vstt(gs1[:, l, :], mod[:, l, 8:16], 1.0, n1g[:, l, :], ALU.add, ALU.mult, [('mod', l)], [('gs', l)])
        vstt(gs2[:, l, :], mod[:, l, 32:40], 1.0, n2g[:, l, :], ALU.add, ALU.mult, [('mod', l)], [('gs', l)])

    ADS2 = at(X + 80320, [128, 8, 256], F32)
    ADB2 = at(X + 88512, [128, 8, 256], BF16)
    ROW2 = at(X + 92608, [1, 256], F32)
    ada1_state = [0]

    def ada1_scatter(g):
        for j in range(2):
            col = 256 + 2 * g + j
            mm(ps[:, 7, col:col + 1], ROW2[0:1, j * 128:(j + 1) * 128], onef[0:1, 0:1], True, True,
               ['row2', 'onef'], [('ps', 7)])

    def ada1_step():
        g = ada1_state[0]
        if g >= 24:
            return
        ada1_state[0] += 1
        if g > 0:
            ada1_scatter(g - 1)
        for hf in range(2):
            dma(ADS2[:, 4 * hf:4 * hf + 4, :],
                d_adaw[1][hf * 512:(hf + 1) * 512, g * 256:(g + 1) * 256].rearrange("(kc p) n -> p kc n", p=128),
                'ld_ada2', writes=['ads2'], eng=('sp' if hf == 0 else 'pool'))
        vcopy(ADB2[:], ADS2[:], ['ads2'], ['adb2'])
        for kc in range(8):
            mm(ps[0:1, 7, 0:256], silucb[:, kc:kc + 1], ADB2[:, kc, :], kc == 0, kc == 7, ['adb2', 'silucb'], [('ps', 7)])
        vcopy(ROW2[0:1, :], ps[0:1, 7, 0:256], [('ps', 7)], ['row2'])

    def ada1_finish():
        while ada1_state[0] < 24:
            ada1_step()
        ada1_scatter(23)
        vtt(mod[:, 1, :], ps[:, 7, 256:304], adab[:, 1, :], ALU.add, [('ps', 7)], [('mod', 1)])
        vstt(gs1[:, 1, :], mod[:, 1, 8:16], 1.0, n1g[:, 1, :], ALU.add, ALU.mult, [('mod', 1)], [('gs', 1)])
        vstt(gs2[:, 1, :], mod[:, 1, 32:40], 1.0, n2g[:, 1, :], ALU.add, ALU.mult, [('mod', 1)], [('gs', 1)])

    def norm_mod(gs_ap, sh_ap, rkeys, out_fn):
        for b in range(NB):
            for kc in range(8):
                if kc % 2 == 0:
                    act(SQ[kc % 2][:], xT[:, kc, bs(b)], AF.Square, [('x', kc, b)], [('sq', kc % 2)])
                else:
                    vtt(SQ[kc % 2][:], xT[:, kc, bs(b)], xT[:, kc, bs(b)], ALU.mult, [('x', kc, b)], [('sq', kc % 2)])
                mm(ps[:, b % 2, :], ones[:], SQ[kc % 2][:], kc == 0, kc == 7, [('sq', kc % 2), 'ones'], [('ps', b % 2)])
            act(RT[b % 2][:], ps[:, b % 2, :], AF.Ln, [('ps', b % 2)], [('rt', 0)], bias=EPS, scale=1.0 / D_MODEL)
            act(RT[b % 2][:], RT[b % 2][:], AF.Exp, [('rt', 0)], [('rt', 0)], scale=-0.5)
            for kc in range(8):
                vtt(T1[kc % 2][:], xT[:, kc, bs(b)], RT[b % 2][:], ALU.mult, [('x', kc, b), ('rt', 0)], [('t1', kc % 2)])
                out_fn(kc, b, T1[kc % 2], ('t1', kc % 2))

    wcount = [0]

    def load_w128(src_ap):
        i = wcount[0] % 2
        wcount[0] += 1
        dma(WS[i][:].rearrange("p kc n -> p (kc n)"), src_ap, 'ld_ws%d' % i, writes=[('ws', i)])
        vcopy(WB[i][:], WS[i][:], [('ws', i)], [('wb', i)], eng='pool')
        return i

    def layer(l):
        sh1, g1 = mod[:, l, 0:8], mod[:, l, 16:24]
        sh2, g2 = mod[:, l, 24:32], mod[:, l, 40:48]
        MK = [('mod', l), ('gs', l)]

        def o1(kc, b, t, tk):
            act(actT[:, kc, bs(b)], t[:], AF.Identity, [tk] + MK, [('h', kc, b)],
                bias=sh1[:, kc:kc + 1], scale=gs1[:, l, kc:kc + 1])
        norm_mod(None, None, None, o1)

        for i in range(8):
            dma(sstage[:], d_gw[l, i], 'ld_c2', writes=['sstage'])
            vcopy(gwb[:, i, :], sstage[:], ['sstage'], ['gwb'], eng='pool')

        pbank = [0]

        def proj_chunk(ci, post):
            wi = load_w128(d_winC[l, ci])
            pend = None
            for b in range(NB):
                pb = pbank[0] % 6
                pbank[0] += 1
                for kc in range(8):
                    mm(ps[:, pb, :], WB[wi][:, kc, :], actT[:, kc, bs(b)], kc == 0, kc == 7,
                       [('wb', wi), ('h', kc, b)], [('ps', pb)])
                nxt = post(b, ps[:, pb, :], ('ps', pb))
                if pend is not None:
                    pend()
                pend = nxt
            if pend is not None:
                pend()

        for cc in range(2):
            def post_lx(b, pt, pk, cc=cc):
                act(LRUX[:, cc, bs(b)], pt, AF.Copy, [pk], [('lrux', cc)])
            proj_chunk(cc, post_lx)
        for cc in range(2):
            def post_lg(b, pt, pk, cc=cc):
                i = b % 2
                act(QN[i][:], pt, AF.Square, [pk], [('qn', i)])
                vts(QN[i][:], QN[i][:], 0.044715, 1.0, ALU.mult, ALU.add, [('qn', i)], [('qn', i)])
                vtt(QN[i][:], QN[i][:], pt, ALU.mult, [('qn', i), pk], [('qn', i)])
                act(QN[i][:], QN[i][:], AF.Sigmoid, [('qn', i)], [('qn', i)], scale=1.5957691216057308)
                vtt(LRUG[:, cc, bs(b)], QN[i][:], pt, ALU.mult, [('qn', i), pk], [('lrug', cc)])
            proj_chunk(2 + cc, post_lg)

        lru_phase(l)

        P.barrier()
        ctx_load(l)
        memset(VG[:, :, :, 64:65], 1.0, ['vg'], eng='pool')
        memset(VD[:, :, :, 64:65], 1.0, ['vd'], eng='pool')

        auxc = [0]

        def post_qk(dst_fn, gcol, out_dram=None):
            def post(b, pt, pk):
                i = auxc[0] % 2
                auxc[0] += 1
                act(SQ[i][:], pt, AF.Square, [pk], [('sq', i)])

                def cont():
                    ab = 6 + i
                    aux, auxk = ps[:, ab, :], ('ps', ab)
                    mm(aux, bones[:], SQ[i][:], True, True, [('sq', i), 'bones'], [auxk])
                    act(RT[i][:], aux, AF.Ln, [auxk], [('rt', 0)], bias=EPS, scale=1.0 / 64)
                    act(RT[i][:], RT[i][:], AF.Exp, [('rt', 0)], [('rt', 0)], scale=-0.5)
                    if out_dram is None:
                        vstt(dst_fn(b), pt, qkg[:, l, gcol:gcol + 1], RT[i][:], ALU.mult, ALU.mult, [pk, ('rt', 0)] + CK,
                             [('qkv',)])
                    else:
                        vstt(QN[i][:], pt, qkg[:, l, gcol:gcol + 1], RT[i][:], ALU.mult, ALU.mult, [pk, ('rt', 0)] + CK,
                             [('qn', i)])
                        for kv in range(2):
                            vcopy(KZ[kv][0:64, 256 + b * BLK:256 + (b + 1) * BLK], QN[i][kv * 64:(kv + 1) * 64, :],
                                  [('qn', i)], [('qkv',)])
                        dma(out_dram(b), QN[i][:], 'st_k', reads=[('qn', i)])
                return cont
            return post

        for c in range(4):
            proj_chunk(4 + c, post_qk(lambda b, c=c: QT[:, c, bs(b)], 0))
        proj_chunk(8, post_qk(None, 1,
                              out_dram=lambda b: o_kT[l][:, bs(b)]))
        for c in range(2):
            def post_dq(b, pt, pk, c=c):
                act(DQT[:, c, bs(b)], pt, AF.Copy, [pk], [('qkv',)])
            proj_chunk(9 + c, post_dq)
        for c in range(2):
            def post_dk(b, pt, pk, c=c):
                i = b % 2
                act(QN[i][:], pt, AF.Copy, [pk], [('qn', i)])
                for r in range(4):
                    n = 4 * c + r
                    vcopy(DKZ[n // 3][(n % 3) * 32:(n % 3) * 32 + 32, 256 + b * BLK:256 + (b + 1) * BLK],
                          QN[i][r * 32:(r + 1) * 32, :], [('qn', i)], [('qkv',)])
                dma(o_dkT[l][c * 128:(c + 1) * 128, bs(b)], QN[i][:], 'st_k', reads=[('qn', i)])
            proj_chunk(11 + c, post_dk)
        for vc in range(3):
            wi = load_w128(d_winC[l, 13 + vc])
            for tg in range(4):
                pbk = 4 * (vc % 2) + tg
                for tt in range(4):
                    tok = tg * 4 + tt
                    for kc in range(8):
                        mm(ps[:, pbk, tt * 128:(tt + 1) * 128], actT[:, kc, tok * 128:(tok + 1) * 128], WB[wi][:, kc, :],
                           kc == 0, kc == 7, [('wb', wi)] + [('h', kc, tok // 4)], [('ps', pbk)])
                i = tg % 2
                vk = ('vst', 0) if i == 0 else ('qn', 1)
                act(VST[i][:], ps[:, pbk, :].rearrange("p (a b) -> p a b", a=4), AF.Copy, [('ps', pbk)], [vk])
                if vc == 0:
                    vcopy(VG[:, 2 + tg * 4:2 + tg * 4 + 4, :, 0:64],
                          VST[i][:].rearrange("p a (h d) -> p a h d", h=2), [vk], [('qkv',)])
                    dma(o_v[l][tg * 512:(tg + 1) * 512, :].rearrange("(a p) n -> p a n", p=128), VST[i][:], 'st_v',
                        reads=[vk])
                else:
                    hb = 2 * (vc - 1)
                    vcopy(VD[:, 2 + tg * 4:2 + tg * 4 + 4, hb:hb + 2, 0:64],
                          VST[i][:].rearrange("p a (h d) -> p a h d", h=2), [vk], [('qkv',)])
                    dma(o_dv[l][tg * 512:(tg + 1) * 512, hb * 64:(hb + 2) * 64].rearrange("(a p) n -> p a n", p=128),
                        VST[i][:], 'st_v', reads=[vk])
        P.barrier()
        for b in range(NB):
            dma(ROPET[:, 0:2, :], d_ropeg[:, :, bs(b)].rearrange("a p n -> p a n"), 'ld_rope', writes=['ropet'])
            dma(ROPET[:, 2:4, :], d_roped[:, :, bs(b)].rearrange("a p n -> p a n"), 'ld_rope', writes=['ropet'])
            items = [(QT[:, c, bs(b)], permg, 0, 128) for c in range(4)]
            items += [(KZ[kv][0:64, 256 + b * BLK:256 + (b + 1) * BLK], permg, 0, 64) for kv in range(2)]
            items += [(DQT[:, c, bs(b)], permd, 2, 128) for c in range(2)]
            items += [(DKZ[t][:, 256 + b * BLK:256 + (b + 1) * BLK], permd, 2, 128) for t in range(3)]
            for n, (ap_, pm, to, np_) in enumerate(items):
                i = n % 2
                mm(ps[0:np_, i, :], pm[0:np_, 0:np_], ap_, True, True, [('rp', b, n)], [('ps', i)])
                vtt(T1[i][0:np_, :], ap_, ROPET[0:np_, to, :], ALU.mult, [('rp', b, n), 'ropet'], [('t1', i)])
                vtt(T2[i][0:np_, :], ps[0:np_, i, :], ROPET[0:np_, to + 1, :], ALU.mult, [('ps', i), 'ropet'], [('t2', i)])
                vtt(ap_, T1[i][0:np_, :], T2[i][0:np_, :], ALU.add, [('t1', i), ('t2', i)], [('rp', b, n)])
        P.barrier()
        for t in range(3):
            dma(DKZ[t][96:105, :], d_Ak[:, :], 'ld_ctx', writes=[('qkv',)])

        attention(l)
        if l == 0 and DEPTH > 1:
            ada1_finish()
        P.barrier()

        for og in range(2):
            dma(WOS[og][:], d_wout[l][:, og * 512:(og + 1) * 512].rearrange("(kc p) n -> p kc n", p=128),
                'ld_wo%d' % og, writes=[('wos', og)])
            act(WOB[og][:], WOS[og][:], AF.Copy, [('wos', og)], [('wob', og)])
        n = 0
        for og in range(2):
            for oc in range(4):
                ch = og * 4 + oc
                for b in range(NB):
                    pbk = n % 8
                    n += 1
                    for kc in range(8):
                        src = LRUOUT[:, kc, bs(b)] if kc < 2 else actT[:, kc, bs(b)]
                        mm(ps[:, pbk, :], WOB[og][:, kc, oc * 128:(oc + 1) * 128], src, kc == 0, kc == 7,
                           [('wob', og)], [('ps', pbk)])
                    vstt(xT[:, ch, bs(b)], ps[:, pbk, :], g1[:, ch:ch + 1], xT[:, ch, bs(b)], ALU.mult, ALU.add,
                         [('ps', pbk), ('x', ch, b)] + MK, [('x', ch, b)])
        P.barrier()

        def o2(kc, b, t, tk):
            act(actT[:, kc, bs(b)], t[:], AF.Identity, [tk] + MK, [('h', kc, b)],
                bias=sh2[:, kc:kc + 1], scale=gs2[:, l, kc:kc + 1])
        norm_mod(None, None, None, o2)
        ffn(l, g2, MK)
        P.barrier()

    def ctx_load(l):
        dma(T1[0][:, 0:256], d_ckT[l], 'ld_ctx', writes=[('t1', 0)])
        for kv in range(2):
            memset(KZ[kv][64:128, :], 0.0, [('qkv',)], eng='pool')
            dma(KZ[kv][64:73, :], d_Ak[:, :], 'ld_ctx', writes=[('qkv',)])
            vcopy(KZ[kv][0:64, 0:256], T1[0][kv * 64:(kv + 1) * 64, 0:256], [('t1', 0)], [('qkv',)])
        stg = [(T1[1], ('t1', 1)), (RT[0], ('rt', 0))]
        for t in range(3):
            memset(DKZ[t][96:128, :], 0.0, [('qkv',)], eng='pool')
        memset(DKZ[2][64:96, :], 0.0, [('qkv',)], eng='pool')
        for c in range(2):
            tt, tk = stg[c]
            dma(tt[:, 0:256], d_cdkT[l][c * 128:(c + 1) * 128, :], 'ld_ctx', writes=[tk])
            for r in range(4):
                n = 4 * c + r
                vcopy(DKZ[n // 3][(n % 3) * 32:(n % 3) * 32 + 32, 0:256], tt[r * 32:(r + 1) * 32, 0:256], [tk], [('qkv',)])
        dma(QN[0][:, 0:256].rearrange("p (a n) -> p a n", a=2), d_cv[l].rearrange("(a p) n -> p a n", p=128), 'ld_ctx',
            writes=[('qn', 0)])
        vcopy(VG[:, 0:2, :, 0:64], QN[0][:, 0:256].rearrange("p (a h d) -> p a h d", a=2, h=2), [('qn', 0)], [('qkv',)])
        dma(QN[1][:].rearrange("p (a n) -> p a n", a=2), d_cdv[l].rearrange("(a p) n -> p a n", p=128), 'ld_ctx',
            writes=[('qn', 1)])
        vcopy(VD[:, 0:2, :, 0:64], QN[1][:].rearrange("p (a h d) -> p a h d", a=2, h=4), [('qn', 1)], [('qkv',)])

    def lru_phase(l):
        P.barrier()
        H2 = T // 2
        for cc in range(2):
            x = LRUX[:, cc, :]
            w = lambda k: lcw[:, l, cc * 4 + k:cc * 4 + k + 1]
            wn = lambda k: lcwn[:, l, cc * 4 + k:cc * 4 + k + 1]
            XK = ['xc']
            act(L_XC[:], x, AF.Identity, [('lrux', cc)] + CK, XK, bias=lcb[:, l, cc:cc + 1], scale=w(2))
            vstt(L_XC[:, 2:T], x[:, 0:T - 2], w(0), L_XC[:, 2:T], ALU.mult, ALU.add, XK + CK, XK)
            vstt(L_XC[:, 1:T], x[:, 0:T - 1], w(1), L_XC[:, 1:T], ALU.mult, ALU.add, XK + CK, XK)
            vstt(L_XC[:, 0:T - 1], x[:, 1:T], w(3), L_XC[:, 0:T - 1], ALU.mult, ALU.add, XK + CK, XK)
            vstt(L_XC[:, 256:T:256], x[:, 254:T - 2:256], wn(0), L_XC[:, 256:T:256], ALU.mult, ALU.add, XK + ['lcwn'], XK)
            vstt(L_XC[:, 256:T:256], x[:, 255:T - 1:256], wn(1), L_XC[:, 256:T:256], ALU.mult, ALU.add, XK + ['lcwn'], XK)
            vstt(L_XC[:, 257:T:256], x[:, 255:T - 1:256], wn(0), L_XC[:, 257:T:256], ALU.mult, ALU.add, XK + ['lcwn'], XK)
            vstt(L_XC[:, 255:T - 1:256], x[:, 256:T:256], wn(3), L_XC[:, 255:T - 1:256], ALU.mult, ALU.add, XK + ['lcwn'], XK)
            vcopy(L_XCB[:], L_XC[:], XK, ['xcb'])
            for h in range(2):
                for dr in range(2):
                    for ri in range(2):
                        combo = dr * 2 + ri
                        gi = dr * 4 + cc * 2 + ri
                        for k in range(2):
                            b = 2 * h + k
                            mm(ps[:, combo * 2 + k, :], gwb[:, gi, :], L_XCB[:, bs(b)], True, True, ['xcb', 'gwb'],
                               [('ps', combo * 2 + k)])
                for dr in range(2):
                    for ri in range(2):
                        combo = dr * 2 + ri
                        gi = dr * 4 + cc * 2 + ri
                        dst = (L_R if ri == 0 else L_I)[dr]
                        dk = ('lr' if ri == 0 else 'li', dr)
                        act(dst[:, h * H2:(h + 1) * H2], ps[:, combo * 2:combo * 2 + 2, :].rearrange("p a b -> p (a b)"),
                            AF.Sigmoid, [('ps', combo * 2), ('ps', combo * 2 + 1)] + CK, [dk], bias=gb[:, l, gi:gi + 1])
            for dr in range(2):
                sc2 = lsc2[:, l, dr * 2 + cc:dr * 2 + cc + 1]
                act(L_T[dr][:], L_R[dr][:], AF.Exp, [('lr', dr), 'lsc2'], [('lt', dr)], scale=sc2)
            for dr in range(2):
                sc = lsc[:, l, dr * 2 + cc:dr * 2 + cc + 1]
                act(L_R[dr][:], L_R[dr][:], AF.Exp, [('lr', dr), 'lsc'], [('lr', dr)], scale=sc)
            for dr in range(2):
                act(L_T[dr][:], L_T[dr][:], AF.Sqrt, [('lt', dr)], [('lt', dr)], bias=1.0, scale=-1.0)
            for dr in range(2):
                A, I_, T_ = L_R[dr], L_I[dr], L_T[dr]
                ak, ik, tk = ('lr', dr), ('li', dr), ('lt', dr)
                vtt(I_[:], I_[:], L_XC[:], ALU.mult, [ik] + XK, [ik])
                vtt(I_[:], I_[:], T_[:], ALU.mult, [ik, tk], [ik])
                e = 0 if dr == 0 else T - 1
                hh = h0[:, l, dr * 2 + cc:dr * 2 + cc + 1]
                vstt(I_[:, e:e + 1], A[:, e:e + 1], hh, I_[:, e:e + 1], ALU.mult, ALU.add, [ik, ak] + CK, [ik])
                if dr == 0:
                    vts(A[:, 256:T:256], A[:, 256:T:256], flags[:, 0:1], None, ALU.mult, None, [ak] + CK, [ak])
                    P.op('dve', lambda A=A, I_=I_, T_=T_: nc.vector.tensor_tensor_scan(
                        out=T_[:], data0=A[:], data1=I_[:], initial=0.0, op0=ALU.mult, op1=ALU.add), [ak, ik, tk], [tk])
                    vcopy(st[:, (l * 4 + cc) * 8:(l * 4 + cc) * 8 + 8], T_[:, 255:T:256], [tk], ['st'])
                else:
                    vts(A[:, 255:T - 1:256], A[:, 255:T - 1:256], flags[:, 0:1], None, ALU.mult, None, [ak] + CK, [ak])
                    P.op('dve', lambda A=A, I_=I_, T_=T_: nc.vector.tensor_tensor_scan(
                        out=T_[:, ::-1], data0=A[:, ::-1], data1=I_[:, ::-1], initial=0.0, op0=ALU.mult, op1=ALU.add),
                        [ak, ik, tk], [tk])
                    vcopy(st[:, (l * 4 + 2 + cc) * 8:(l * 4 + 2 + cc) * 8 + 8], T_[:, 0:T:256], [tk], ['st'])
            vtt(L_T[0][:], L_T[0][:], L_T[1][:], ALU.add, [('lt', 0), ('lt', 1)], [('lt', 0)])
            vtt(LRUOUT[:, cc, :], L_T[0][:], LRUG[:, cc, :], ALU.mult, [('lt', 0), ('lrug', cc)], [('lruout', cc)])

    def attention(l):
        SKEW = 2
        sc_d = 32 ** -0.5
        heads = []
        pair = 0
        for c in range(4):
            for qb in range(NB):
                for hh in range(2):
                    p0 = hh * 64
                    heads.append(dict(kind='g', qrows=QT[p0:p0 + 64, c, bs(qb)], p0=p0, nk=64,
                                      kfn=(lambda kt, hh=hh: KZ[hh][:, kt * 128:(kt + 1) * 128]),
                                      vfn=(lambda kt, hh=hh: VG[:, kt, hh, 0:65]), scale=0.125, qb=qb, hh=hh,
                                      comp=0, pair=pair, last=(hh == 1), dst=2 + c))
                pair += 1
        for c in range(2):
            for qb in range(NB):
                for hh in range(2):
                    for comp in range(2):
                        p0 = hh * 64 + comp * 32
                        heads.append(dict(kind='d', qrows=DQT[p0:p0 + 32, c, bs(qb)], p0=p0, nk=32,
                                          kfn=(lambda kt, n=4 * c + 2 * hh + comp: DKZ[n // 3][:, kt * 128:(kt + 1) * 128]),
                                          slot=(4 * c + 2 * hh + comp) % 3,
                                          vfn=(lambda kt, h=2 * c + hh: VD[:, kt, h, 0:65]), scale=sc_d, qb=qb, hh=hh,
                                          comp=comp, pair=pair, last=(hh == 1 and comp == 1), dst=6 + c))
                pair += 1
        KP = KT // 2
        tiles = [(hi, kp) for hi in range(len(heads)) for kp in range(KP)]
        NTL = len(tiles)

        def accv(hi):
            ab = 4 + hi % 2
            v = ps[:, ab, 0:260].rearrange("p (j c) -> p j c", j=4)
            return ab, v

        def finish(H, hi):
            oi = H['pair'] % 2
            hh = H['hh']
            ab, v = accv(hi)
            acc4, den4 = v[:, :, 0:64], v[:, :, 64:65]
            PACC = [('ps', ab)]
            rec = sml[:, 0:4].unsqueeze(2)
            P.op('dve', lambda: nc.vector.reciprocal(out=rec, in_=den4), PACC, ['smlr'])
            recb = rec.broadcast_to([128, 4, 64])
            if H['kind'] == 'g':
                vtt(OTOK[oi][:, :, hh * 64:(hh + 1) * 64], acc4, recb, ALU.mult, PACC + ['smlr'], [('otok', oi)])
            elif H['comp'] == 0:
                vtt(DD1[:], acc4, recb, ALU.mult, PACC + ['smlr'], ['dd1'])
            else:
                vts(sml[:, 0:4], sml[:, 0:4], nlam[:, l:l + 1], None, ALU.mult, None, ['smlr', 'nlam'], ['smlr'])
                vtt(DD2[:], acc4, recb, ALU.mult, PACC + ['smlr'], ['dd2'])

        def finish_b(H):
            oi = H['pair'] % 2
            hh = H['hh']
            vtt(DD2[:], DD2[:], DD1[:], ALU.add, ['dd2', 'dd1'], ['dd2'])
            vtt(DD1[:], DD2[:], DD2[:], ALU.mult, ['dd2', 'dd1'], ['dd1'])
            P.op('dve', lambda: nc.vector.reduce_sum(out=sml[:, 8:12], in_=DD1[:], axis=mybir.AxisListType.X),
                 ['dd1'], ['smls'])
            vts(sml[:, 8:12], sml[:, 8:12], 1.0 / 64, EPS, ALU.mult, ALU.add, ['smls'], ['smls'])
            vtt(sml[:, 8:12], sml[:, 8:12], mhalf[:, 0:4], ALU.pow, ['smls', 'mhalf'], ['smls'], eng='pool')
            vtt(DD2[:], DD2[:], sml[:, 8:12].unsqueeze(2).broadcast_to([128, 4, 64]), ALU.mult, ['dd2', 'smls'], ['dd2'])
            vtt(OTOK[oi][:, :, hh * 64:(hh + 1) * 64], DD2[:], dg[:, l, :].unsqueeze(1).broadcast_to([128, 4, 64]),
                ALU.mult, ['dd2', 'dg'], [('otok', oi)])

        def transposes(H):
            oi = H['pair'] % 2
            tpb = ps[:, 6, :].bitcast(BF16)
            for j in range(4):
                P.op('pe', lambda j=j: nc.tensor.transpose(tpb[:, j * 128:(j + 1) * 128], OTOK[oi][:, j, :], ident[:]),
                     [('otok', oi), 'ident'], [('ps', 6)])
            vcopy(actT[:, H['dst'], bs(H['qb'])], tpb[:, 0:BLK], [('ps', 6)], [('h', H['dst'], H['qb'])])

        SKEW = 1
        deferred = []
        for i in range(NTL + SKEW):
            if l == 0 and DEPTH > 1 and i % 24 == 12:
                ada1_step()
            if i < NTL:
                hi, kp = tiles[i]
                H = heads[hi]
                zi = hi % 2
                s = i % 2
                if kp == 0:
                    memset(QZ[zi][:], 0.0, [('qz', zi)], eng='pool')
                    if H['kind'] == 'g':
                        vcopy(QZ[zi][0:64, :], H['qrows'], [('qkv',)], [('qz', zi)], eng='pool')
                        vcopy(QZ[zi][64:73, :], mBb[64:73, H['qb'], :], ['mB'], [('qz', zi)], eng='pool')
                    else:
                        r0 = H['slot'] * 32
                        vcopy(QZ[zi][r0:r0 + 32, :], H['qrows'], [('qkv',)], [('qz', zi)], eng='pool')
                        vcopy(QZ[zi][96:105, :], mBb[96:105, H['qb'], :], ['mB'], [('qz', zi)], eng='pool')
                for t in range(2):
                    kt = 2 * kp + t
                    bk = 2 * s + t
                    mm(ps[:, bk, :], H['kfn'](kt), QZ[zi][:], True, True, [('qkv',), ('qz', zi)], [('ps', bk)])
                act(PTt[s][:], ps[:, 2 * s:2 * s + 2, :].rearrange("p a b -> p (a b)"), AF.Exp,
                    [('ps', 2 * s), ('ps', 2 * s + 1)], [('pt', s)], scale=H['scale'])
            if i >= SKEW:
                hi, kp = tiles[i - SKEW]
                H = heads[hi]
                s = (i - SKEW) % 2
                ab, v = accv(hi)
                for t in range(2):
                    kt = 2 * kp + t
                    for j in range(4):
                        first = (kt == 0 and j == 0)
                        P.op('pe', lambda t=t, j=j, kt=kt, first=first, v=v, s=s, H=H: nc.tensor.matmul(
                            v[:, j, :], lhsT=PTt[s][:, t * BLK + j * 128:t * BLK + (j + 1) * 128], rhs=H['vfn'](kt),
                            start=first, stop=(kt == KT - 1), skip_group_check=True),
                            [('pt', s), ('qkv',)], [('ps', ab)])
                if kp == KP - 1:
                    finish(H, hi)
                    if H['kind'] == 'd' and H['comp'] == 1:
                        deferred.append((i + 2, 'b', H))
                        if H['last']:
                            deferred.append((i + 4, 't', H))
                    elif H['last']:
                        deferred.append((i + 1, 't', H))
            deferred.sort(key=lambda d: d[0])
            while deferred and deferred[0][0] <= i:
                d = deferred.pop(0)
                (finish_b if d[1] == 'b' else transposes)(d[2])
        while deferred:
            d = deferred.pop(0)
            (finish_b if d[1] == 'b' else transposes)(d[2])

    def ffn(l, g2, MK):
        H2 = T // 2
        nps = [0]
        nfo = [0]
        nhp = [0]

        def load_up(g):
            wi = g % 2
            dma(FWS[:], d_wup[l][:, g * 512:(g + 1) * 512].rearrange("(kc p) n -> p kc n", p=128), 'ld_fu',
                writes=['fws'])
            vcopy(FWB[wi][:], FWS[:], ['fws'], [('fwb', wi)])

        def load_dn(g):
            wi = g % 2
            dma(FDS[:], d_wdn[l][g * 256:(g + 1) * 256, :].rearrange("(j p) n -> p j n", p=128), 'ld_fd', writes=['fds'])
            vcopy(FDB[wi][:], FDS[:], ['fds'], [('fdb', wi)])

        def down_piece(g, k0, k1):
            wi = g % 2
            for k in range(k0, k1):
                oc, b = k // NB, k % NB
                n = nps[0]
                nps[0] += 1
                pbk = 4 + n % 4
                for jj in range(2):
                    mm(ps[:, pbk, :], FDB[wi][:, jj, oc * 128:(oc + 1) * 128], FACT[wi][:, jj, bs(b)], jj == 0, jj == 1,
                       [('fdb', wi), ('fact', wi)], [('ps', pbk)])
                if n % 8 not in (1, 4, 6):
                    vstt(xT[:, oc, bs(b)], ps[:, pbk, :], g2[:, oc:oc + 1], xT[:, oc, bs(b)], ALU.mult, ALU.add,
                         [('ps', pbk), ('x', oc, b)] + MK, [('x', oc, b)])
                else:
                    fi = nfo[0] % 2
                    nfo[0] += 1
                    act(FOUT[fi][:], ps[:, pbk, :], AF.Identity, [('ps', pbk)] + MK, [('fout', fi)], scale=g2[:, oc:oc + 1])
                    vtt(xT[:, oc, bs(b)], xT[:, oc, bs(b)], FOUT[fi][:], ALU.add, [('fout', fi), ('x', oc, b)],
                        [('x', oc, b)], eng='pool')

        def up(g, gd):
            wi = g % 2
            piece = 0
            for jj in range(2):
                j = 2 * g + jj
                for part in range(2):
                    col = (part * 2 + jj) * 128
                    fc = part * 22 + j
                    cw = lambda k, fc=fc: fcw[:, l, fc * 3 + k:fc * 3 + k + 1]
                    Y, U0, U2 = FY[part], FU0[part], FU2[part]
                    for h in range(2):
                        pr = nhp[0] % 2
                        nhp[0] += 1
                        pb = 2 * pr
                        for k in range(2):
                            b = 2 * h + k
                            for kc in range(8):
                                mm(ps[:, pb + k, :], FWB[wi][:, kc, col:col + 128], actT[:, kc, bs(b)], kc == 0, kc == 7,
                                   [('fwb', wi), ('h', kc, b)], [('ps', pb + k)])
                        PK = [('ps', pb), ('ps', pb + 1)]
                        p2 = ps[:, pb:pb + 2, :].rearrange("p a b -> p (a b)")
                        t0 = h * H2
                        yk, u0k, u2k = ('fy', part, h), ('fu0', part), ('fu2', part)
                        act(Y[:, t0:t0 + H2], p2, AF.Identity, PK + CK, [yk], bias=fcb[:, l, fc:fc + 1], scale=cw(1))
                        fl = flags[:, 0:1]
                        y0, y1 = ('fy', part, 0), ('fy', part, 1)
                        if h == 0:
                            act(U0[:, 1:H2 + 1], p2, AF.Copy, PK + CK, [u0k], scale=cw(0))
                            act(U2[:, 0:H2 - 1], p2[:, 1:H2], AF.Copy, PK + CK, [u2k], scale=cw(2))
                            vts(U0[:, 256:H2 + 1:256], U0[:, 256:H2 + 1:256], fl, None, ALU.mult, None, [u0k] + CK, [u0k])
                            vts(U2[:, 255:H2 - 1:256], U2[:, 255:H2 - 1:256], fl, None, ALU.mult, None, [u2k] + CK, [u2k])
                            vtt(Y[:, 0:H2], Y[:, 0:H2], U0[:, 0:H2], ALU.add, [y0, u0k], [y0])
                            vtt(Y[:, 0:H2 - 2], Y[:, 0:H2 - 2], U2[:, 0:H2 - 2], ALU.add, [y0, u2k], [y0])
                        else:
                            act(U0[:, H2 + 1:T], p2[:, 0:H2 - 1], AF.Copy, PK + CK, [u0k], scale=cw(0))
                            act(U2[:, H2 - 1:T - 1], p2, AF.Copy, PK + CK, [u2k], scale=cw(2))
                            vts(U0[:, H2 + 256:T:256], U0[:, H2 + 256:T:256], fl, None, ALU.mult, None, [u0k] + CK, [u0k])
                            vts(U2[:, H2 - 1:T - 1:256], U2[:, H2 - 1:T - 1:256], fl, None, ALU.mult, None, [u2k] + CK, [u2k])
                            vtt(Y[:, H2:T], Y[:, H2:T], U0[:, H2:T], ALU.add, [y1, u0k], [y1])
                            vtt(Y[:, H2 - 2:T], Y[:, H2 - 2:T], U2[:, H2 - 2:T], ALU.add, [y0, y1, u2k], [y0, y1])
                        if gd is not None:
                            down_piece(gd, piece * 4, piece * 4 + 4)
                        piece += 1
                        if piece == 4 and g + 1 < 11:
                            load_up(g + 1)
                YK = [('fy', p_, h_) for p_ in range(2) for h_ in range(2)]
                act(FY[1][:], FY[1][:], AF.Silu, YK, [('fy', 1, 0), ('fy', 1, 1)])
                vtt(FACT[wi][:, jj, :], FY[1][:], FY[0][:], ALU.mult, YK, [('fact', wi)])
            if g + 1 < 11:
                load_dn(g + 1)

        memset(FU0[0][:, 0:1], 0.0, [('fu0', 0)])
        memset(FU0[1][:, 0:1], 0.0, [('fu0', 1)])
        memset(FU2[0][:, T - 1:T], 0.0, [('fu2', 0)])
        memset(FU2[1][:, T - 1:T], 0.0, [('fu2', 1)])
        load_up(0)
        load_dn(0)
        up(0, None)
        for g in range(11):
            if g + 1 < 11:
                up(g + 1, g)
            else:
                down_piece(g, 0, 32)

    ada(0)
    P.barrier()
    for l in range(n_layers):
        layer(l)

    def of(kc, b, t, tk):
        i = (kc + b) % 2
        act(QN[i][:], t[:], AF.Identity, [tk, 'fng'], [('qn', i)], scale=fng[:, kc:kc + 1])
        dma(o_yT[kc * 128:(kc + 1) * 128, bs(b)], QN[i][:], 'st_y', reads=[('qn', i)])
    norm_mod(None, None, None, of)
    dma(o_st[:, :], st[:], 'st_s', reads=['st'])
    P.emit()
    return nc


def _rope_tables(dim, is_sample):
    if not is_sample:
        return np.stack([np.ones((128, T), np.float32), np.zeros((128, T), np.float32)])
    GRID_W = 64
    rows = T // GRID_W
    t_row = np.repeat(np.arange(rows, dtype=np.float32), GRID_W)
    t_col = np.tile(np.arange(GRID_W, dtype=np.float32), rows)
    axis_dim = dim // 2
    inv = (np.float32(10000.0) ** (-np.arange(0, axis_dim, 2, dtype=np.float32) / np.float32(axis_dim))).astype(np.float32)
    ar = t_row[:, None] * inv
    ac = t_col[:, None] * inv
    ang = np.concatenate([ar, ar, ac, ac], axis=-1)
    cos = np.cos(ang).astype(np.float32).T
    sin = np.sin(ang).astype(np.float32).T
    rep = 128 // dim
    return np.stack([np.tile(cos, (rep, 1)), np.tile(sin, (rep, 1))]).astype(np.float32)


def _perm_matrix(dim):
    q = dim // 4
    Pm = np.zeros((128, 128), np.float32)
    for hb in range(0, 128, dim):
        for d in range(dim):
            quarter = d // q
            m = hb + d
            if quarter % 2 == 0:
                Pm[hb + d + q, m] = -1.0
            else:
                Pm[hb + d - q, m] = 1.0
    return Pm


_CACHE = {}


def kernel(**inp):
    f32 = np.float32
    g = {k: np.asarray(v) for k, v in inp.items()}
    L = DEPTH

    def c(a):
        return np.ascontiguousarray(a, dtype=f32)

    def pk8(v):
        return c(v.reshape(8, 128).T)

    w_in = g['w_in']
    colsF = []
    colsF += list(range(0, 256))
    colsF += list(range(256, 512))
    for cch in range(4):
        for h in (cch, cch + 4):
            colsF += list(range(512 + 64 * h, 512 + 64 * h + 64))
    colsF += list(range(1024, 1152))
    colsF += list(range(1280, 1536))
    colsF += list(range(1536, 1792))
    colsV = list(range(1152, 1280)) + list(range(1792, 2048))
    w_inF = c(w_in[:, :, colsF])
    w_inV = c(w_in[:, :, colsV])
    w_all = np.concatenate([w_inF, w_inV], axis=2)
    w_inC = c(w_all.reshape(L, 8, 128, 16, 128).transpose(0, 3, 2, 1, 4).reshape(L, 16, 128, 1024))
    rows_o = list(range(0, 256))
    for cch in range(4):
        for h in (cch, cch + 4):
            rows_o += list(range(256 + 64 * h, 256 + 64 * h + 64))
    rows_o += list(range(768, 1024))
    w_outP = c(g['w_out'][:, rows_o, :])
    up = g['ffn_w_up']
    cols_up = []
    for grp in range(11):
        for part in range(2):
            for jj in range(2):
                j = 2 * grp + jj
                cols_up += list(range(part * D_FF + 128 * j, part * D_FF + 128 * j + 128))
    w_upP = c(up[:, :, cols_up])
    shared = {
        'ada_w': c(g['ada_w']),
        'ada_b': c(g['ada_b'].reshape(L, 48, 128).transpose(0, 2, 1)),
        'n1g': c(g['norm1_g'].reshape(L, 8, 128).transpose(0, 2, 1)),
        'n2g': c(g['norm2_g'].reshape(L, 8, 128).transpose(0, 2, 1)),
        'fng': pk8(g['final_norm_g']),
        'w_inC': w_inC, 'w_outP': w_outP, 'w_upP': w_upP, 'w_dn': c(g['ffn_w_down']),
        'lcw': c(g['lru_conv_w'].reshape(L, 4, 2, 128).transpose(0, 3, 2, 1).reshape(L, 128, 8)),
        'lcb': c(g['lru_conv_b'].reshape(L, 2, 128).transpose(0, 2, 1)),
        'lam': c(g['lru_lambda'].reshape(L, 2, 2, 128).transpose(0, 3, 1, 2).reshape(L, 128, 4)),
        'qkg': c(np.stack([np.tile(g['gqa_q_norm_g'], (1, 2)), np.tile(g['gqa_k_norm_g'], (1, 2))], axis=-1)),
        'dl': c(g['diff_lambda'].reshape(L, 128)),
        'dg': c(g['diff_norm_g']),
        'fcw': c(g['ffn_conv_w'].reshape(L, 3, 44, 128).transpose(0, 3, 2, 1).reshape(L, 128, 132)),
        'fcb': c(g['ffn_conv_b'].reshape(L, 44, 128).transpose(0, 2, 1)),
        'permg': _perm_matrix(64), 'permd': _perm_matrix(32),
        'ident': np.eye(128, dtype=f32),
    }
    gwsrc = g['lru_gate_w']
    gw = np.zeros((L, 8, 128, 128), f32)
    gbsrc = g['lru_gate_b']
    gbv = np.zeros((L, 128, 8), f32)
    for l in range(L):
        for dr in range(2):
            for cc in range(2):
                for ri in range(2):
                    idx = dr * 4 + cc * 2 + ri
                    for bl in range(2):
                        blk = 2 * cc + bl
                        gw[l, idx, bl * 64:(bl + 1) * 64, bl * 64:(bl + 1) * 64] = gwsrc[l, dr, blk, :, ri * 64:(ri + 1) * 64]
                        gbv[l, bl * 64:(bl + 1) * 64, idx] = gbsrc[l, dr, blk, ri * 64:(ri + 1) * 64]
    shared['gw'] = gw
    shared['gb'] = gbv

    mA4 = np.zeros((128, 9 * 128), f32)
    for j in range(9):
        mA4[j, j * 128:(j + 1) * 128] = 1.0

    def mB4_for(is_sample):
        m = np.zeros((128, 4 * BLK), f32)
        if not is_sample:
            qseg = np.arange(T) // 256 + 1
            for j in range(9):
                m[j, :] = np.where(qseg == j, 0.0, NEG)
                m[64 + j, :] = m[j, :]
                m[96 + j, :] = m[j, :]
        return m

    import ml_dtypes
    Ak = np.zeros((9, TK), f32)
    for j in range(9):
        Ak[j, j * 256:(j + 1) * 256] = 1.0
    Ak = Ak.astype(ml_dtypes.bfloat16)

    ropeg_s, roped_s = _rope_tables(64, True), _rope_tables(32, True)
    ropeg_p, roped_p = _rope_tables(64, False), _rope_tables(32, False)
    mB_s, mB_p = mB4_for(True), mB4_for(False)

    xs, xp = g['x_sample'], g['x_prompt']
    in_maps = []
    for core in range(8):
        m = dict(shared)
        m['Ak'] = Ak
        if core < 4:
            b = core
            m['xT'] = c(xs[b].T)
            m['cT'] = pk8(g['c'][b])
            m['flags'] = c(np.tile(np.array([[1.0, 0.0]], f32), (128, 1)))
            m['mB4'] = mB_s
            m['ropeg'], m['roped'] = ropeg_s, roped_s
            m['ckT'] = c(g['cache_gqa_k'][b].reshape(L, 256, 128).transpose(0, 2, 1))
            m['cv'] = c(g['cache_gqa_v'][b].reshape(L, 256, 128))
            m['cdkT'] = c(g['cache_diff_k'][b].reshape(L, 256, 256).transpose(0, 2, 1))
            m['cdv'] = c(g['cache_diff_v'][b].reshape(L, 256, 256))
            m['h0'] = c(g['state_lru'][b].reshape(L, 2, 2, 128).transpose(0, 3, 1, 2).reshape(L, 128, 4))
        else:
            i = (core - 4) % 2
            m['xT'] = c(xp[8 * i:8 * i + 8].reshape(T, D_MODEL).T)
            m['cT'] = pk8(g['c_ctx'])
            m['flags'] = c(np.tile(np.array([[0.0, 1.0]], f32), (128, 1)))
            m['mB4'] = mB_p
            m['ropeg'], m['roped'] = ropeg_p, roped_p
            m['ckT'] = np.zeros((L, 128, 256), f32)
            m['cv'] = np.zeros((L, 256, 128), f32)
            m['cdkT'] = np.zeros((L, 256, 256), f32)
            m['cdv'] = np.zeros((L, 256, 256), f32)
            m['h0'] = np.zeros((L, 128, 4), f32)
        in_maps.append(m)

    if 'nc' not in _CACHE:
        _CACHE['nc'] = build_program()
    nc = _CACHE['nc']
    res = run_bass_kernel_spmd(nc, in_maps, core_ids=list(range(8)))
    R = res.results

    y_sample = np.stack([R[b]['yT'].T for b in range(4)]).astype(f32)
    yp, ks, vs, dks, dvs, sts = [], [], [], [], [], []
    for i in range(2):
        r = R[4 + i]
        yT = r['yT']
        for s in range(8):
            sl = slice(256 * s, 256 * s + 256)
            yp.append(yT[:, sl].T)
            ks.append(np.stack([r['okT'][l][:, sl].T.reshape(256, 2, 64) for l in range(L)]))
            vs.append(np.stack([r['ov'][l][sl, :].reshape(256, 2, 64) for l in range(L)]))
            dks.append(np.stack([r['odkT'][l][:, sl].T.reshape(256, 4, 2, 32) for l in range(L)]))
            dvs.append(np.stack([r['odv'][l][sl, :].reshape(256, 4, 64) for l in range(L)]))
            o = r['ost'].reshape(128, L, 2, 2, 8)[:, :, :, :, s]
            sts.append(o.transpose(1, 2, 3, 0).reshape(L, 2, 256))
    out = (np.stack(yp).astype(f32), y_sample, np.stack(ks).astype(f32), np.stack(vs).astype(f32),
           np.stack(dks).astype(f32), np.stack(dvs).astype(f32), np.stack(sts).astype(f32))
    return tuple(np.ascontiguousarray(o) for o in out)
```

```python
import contextlib
import numpy as np
import concourse.bass as bass
import concourse.mybir as mybir
from concourse.bass_utils import run_bass_kernel_spmd

F32 = mybir.dt.float32
BF16 = mybir.dt.bfloat16
ALU = mybir.AluOpType
AF = mybir.ActivationFunctionType

D_MODEL = 1024
T = 2048
BLK = 512
NB = 4
NT = 16
KT = 18
TK = 2304
DEPTH = 2
D_FF = 2816
EPS = 1e-6
NEG = -30000.0


class Prog:
    def __init__(self, nc):
        self.nc = nc
        self.E = {'pe': nc.tensor, 'act': nc.scalar, 'dve': nc.vector, 'pool': nc.gpsimd, 'sp': nc.sync}
        self.ops = []

    def op(self, eng, fn, reads=(), writes=(), dma=None):
        self.ops.append((eng, fn, tuple(reads), tuple(writes), dma))

    def barrier(self):
        self.ops.append(None)

    def emit(self, final_eng='sp'):
        nc = self.nc
        ops = self.ops
        n = len(ops)
        last_w = {}
        readers = {}
        deps = [None] * n
        marked = [False] * n
        last_eng = {}
        pend_dma = []
        bar_set = None
        bar_done = set()
        for i, rec in enumerate(ops):
            if rec is None:
                s = set(last_eng.values()) | set(pend_dma)
                if bar_set is not None:
                    s |= bar_set
                bar_set = s
                bar_done = set()
                pend_dma = []
                continue
            eng, fn, rd, wr, dma = rec
            d = set()
            if bar_set is not None and eng not in bar_done:
                d |= bar_set
                bar_done.add(eng)
            for t in rd:
                j = last_w.get(t)
                if j is not None:
                    d.add(j)
            for t in wr:
                j = last_w.get(t)
                if j is not None:
                    d.add(j)
                r = readers.get(t)
                if r:
                    d.update(r[0].values())
                    d.update(r[1])
            d.discard(i)
            dd = []
            for j in d:
                ej, _, _, _, dmaj = ops[j]
                if dmaj is None and dma is None and ej == 'pe' and eng == 'pe':
                    continue
                dd.append(j)
                marked[j] = True
            deps[i] = dd
            for t in rd:
                r = readers.get(t)
                if r is None:
                    r = readers[t] = ({}, [])
                if dma is None:
                    r[0][eng] = i
                else:
                    r[1].append(i)
            for t in wr:
                last_w[t] = i
                readers[t] = ({}, [])
            if dma is None:
                last_eng[eng] = i
            else:
                pend_dma.append(i)
        cnt = {}
        mile = [None] * n
        for i, rec in enumerate(ops):
            if rec is None:
                continue
            eng, fn, rd, wr, dma = rec
            if dma is not None:
                cnt[dma] = cnt.get(dma, 0) + 16
                mile[i] = (dma, cnt[dma])
            elif marked[i]:
                k = 'E_' + eng
                cnt[k] = cnt.get(k, 0) + 1
                mile[i] = (k, cnt[k])
        sems = {}
        with contextlib.ExitStack() as st:
            for k in cnt:
                sems[k] = st.enter_context(nc.semaphore(k))
            waited = {e: {} for e in self.E}
            for i, rec in enumerate(ops):
                if rec is None:
                    continue
                eng, fn, rd, wr, dma = rec
                need = {}
                for j in deps[i]:
                    s, v = mile[j]
                    if need.get(s, 0) < v:
                        need[s] = v
                for s, v in need.items():
                    if waited[eng].get(s, 0) < v:
                        self.E[eng].wait_ge(sems[s], v)
                        waited[eng][s] = v
                ins = fn()
                if mile[i] is not None:
                    s, v = mile[i]
                    ins.then_inc(sems[s], 16 if dma is not None else 1)
            for k, v in cnt.items():
                if waited[final_eng].get(k, 0) < v:
                    self.E[final_eng].wait_ge(sems[k], v)
        return cnt


def build_program(n_layers=DEPTH, debug=False):
    nc = bass.Bass("TRN2", target_bir_lowering=False)
    P = Prog(nc)

    def din(name, shape):
        return nc.dram_tensor(name, list(shape), F32, kind="ExternalInput").ap()

    def dout(name, shape):
        return nc.dram_tensor(name, list(shape), F32, kind="ExternalOutput").ap()

    d_xT = din("xT", [D_MODEL, T])
    d_cT = din("cT", [128, 8])
    d_flags = din("flags", [128, 2])
    d_mB = din("mB4", [128, 4 * BLK])
    d_Ak = nc.dram_tensor("Ak", [9, TK], BF16, kind="ExternalInput").ap()
    d_ropeg = din("ropeg", [2, 128, T])
    d_roped = din("roped", [2, 128, T])
    d_permg = din("permg", [128, 128])
    d_permd = din("permd", [128, 128])
    d_ckT = din("ckT", [DEPTH, 128, 256])
    d_cv = din("cv", [DEPTH, 256, 128])
    d_cdkT = din("cdkT", [DEPTH, 256, 256])
    d_cdv = din("cdv", [DEPTH, 256, 256])
    d_h0 = din("h0", [DEPTH, 128, 4])
    d_n1 = din("n1g", [DEPTH, 128, 8])
    d_n2 = din("n2g", [DEPTH, 128, 8])
    d_fg = din("fng", [128, 8])
    d_adaw = din("ada_w", [DEPTH, D_MODEL, 6 * D_MODEL])
    d_adab = din("ada_b", [DEPTH, 128, 48])
    d_winC = din("w_inC", [DEPTH, 16, 128, 8 * 128])
    d_wout = din("w_outP", [DEPTH, D_MODEL, D_MODEL])
    d_lcw = din("lcw", [DEPTH, 128, 8])
    d_lcb = din("lcb", [DEPTH, 128, 2])
    d_gw = din("gw", [DEPTH, 8, 128, 128])
    d_gb = din("gb", [DEPTH, 128, 8])
    d_lam = din("lam", [DEPTH, 128, 4])
    d_qkg = din("qkg", [DEPTH, 128, 2])
    d_dl = din("dl", [DEPTH, 128])
    d_dg = din("dg", [DEPTH, 64])
    d_wup = din("w_upP", [DEPTH, D_MODEL, 11 * 512])
    d_fcw = din("fcw", [DEPTH, 128, 44 * 3])
    d_fcb = din("fcb", [DEPTH, 128, 44])
    d_wdn = din("w_dn", [DEPTH, D_FF, D_MODEL])

    o_yT = dout("yT", [D_MODEL, T])
    o_kT = dout("okT", [DEPTH, 128, T])
    o_v = dout("ov", [DEPTH, T, 128])
    o_dkT = dout("odkT", [DEPTH, 256, T])
    o_dv = dout("odv", [DEPTH, T, 256])
    o_st = dout("ost", [128, 64])

    base0 = (nc.sbuf_base + 31) // 32 * 32
    ARENA = 212800
    arena = nc.alloc_sbuf_tensor("arena", [128, ARENA], mybir.dt.uint8)
    cnt_name = [0]

    def at(off, shape, dt):
        cnt_name[0] += 1
        assert off % 32 == 0 and off >= 0, off
        nbytes = int(np.prod(shape[1:])) * (4 if dt == F32 else 2)
        assert off + nbytes <= ARENA, (off, nbytes)
        return nc.alloc_sbuf_tensor_at("t%d" % cnt_name[0], list(shape), dt, offset=base0 + off)

    OFF_X, OFF_ACT, OFF_C, OFF_XR = 0, 65536, 98304, 115456
    xT = at(OFF_X, [128, 8, T], F32)
    actT = at(OFF_ACT, [128, 8, T], BF16)

    cptr = [OFF_C]

    def calloc(shape, dt):
        nbytes = int(np.prod(shape[1:])) * (4 if dt == F32 else 2)
        nbytes = (nbytes + 31) // 32 * 32
        t = at(cptr[0], shape, dt)
        cptr[0] += nbytes
        assert cptr[0] <= OFF_XR
        return t

    ident = calloc([128, 128], BF16)
    ones = calloc([128, 128], BF16)
    bones = calloc([128, 128], BF16)
    permg = calloc([128, 128], BF16)
    permd = calloc([128, 128], BF16)
    gwb = calloc([128, 8, 128], BF16)
    mBb = calloc([128, 4, BLK], BF16)
    mod = calloc([128, 2, 48], F32)
    adab = calloc([128, 2, 48], F32)
    n1g = calloc([128, 2, 8], F32)
    n2g = calloc([128, 2, 8], F32)
    fng = calloc([128, 8], F32)
    gs1 = calloc([128, 2, 8], F32)
    gs2 = calloc([128, 2, 8], F32)
    cT = calloc([128, 8], F32)
    siluc = calloc([128, 8], F32)
    flags = calloc([128, 2], F32)
    lcw = calloc([128, 2, 8], F32)
    lcwn = calloc([128, 2, 8], F32)
    lcb = calloc([128, 2, 2], F32)
    gb = calloc([128, 2, 8], F32)
    lam = calloc([128, 2, 4], F32)
    lsc = calloc([128, 2, 4], F32)
    lsc2 = calloc([128, 2, 4], F32)
    h0 = calloc([128, 2, 4], F32)
    qkg = calloc([128, 2, 2], F32)
    dlt = calloc([128, 64], F32)
    dle = calloc([128, 4], F32)
    nlam = calloc([128, 2], F32)
    dg = calloc([128, 2, 64], F32)
    fcw = calloc([128, 2, 44 * 3], F32)
    fcwn = calloc([128, 2, 44 * 3], F32)
    fcb = calloc([128, 2, 44], F32)
    st = calloc([128, 64], F32)
    sml = calloc([128, 16], F32)
    sstage = calloc([128, 128], F32)

    X = OFF_XR
    LRUOUT = at(X + 0, [128, 2, T], BF16)
    QT = at(X + 8192, [128, 4, T], BF16)
    DQT = at(X + 24576, [128, 2, T], BF16)
    KZ = [at(X + 32768 + 4608 * i, [128, TK], BF16) for i in range(2)]
    DKZ = [at(X + 41984 + 4608 * i, [128, TK], BF16) for i in range(3)]
    VG = at(X + 55808, [128, KT, 2, 66], BF16)
    VD = at(X + 60576, [128, KT, 4, 66], BF16)
    WS = [at(X + 70080 + 4096 * i, [128, 8, 128], F32) for i in range(2)]
    WB = [at(X + 78272 + 2048 * i, [128, 8, 128], BF16) for i in range(2)]
    PT0 = X + 82368
    SQ = [at(PT0 + 1024 * i, [128, BLK], BF16) for i in range(2)]
    RT = [at(PT0 + 2048, [128, BLK], F32) for i in range(2)]
    QN = [at(PT0 + 4096 + 2048 * i, [128, BLK], F32) for i in range(2)]
    T1 = [at(PT0 + 8192 + 2048 * i, [128, BLK], F32) for i in range(2)]
    VST = [at(PT0 + 12288, [128, 4, 128], F32), at(PT0 + 4096 + 2048, [128, 4, 128], F32)]
    ROPET = at(X + 70080, [128, 4, BLK], F32)
    T2 = [at(X + 78272 + 2048 * i, [128, BLK], F32) for i in range(2)]
    LRUX = at(X + 8192, [128, 2, T], BF16)
    LRUG = at(X + 16384, [128, 2, T], BF16)
    L_R = [at(X + 24576 + 24576 * d, [128, T], F32) for d in range(2)]
    L_I = [at(X + 32768 + 24576 * d, [128, T], F32) for d in range(2)]
    L_T = [at(X + 40960 + 24576 * d, [128, T], F32) for d in range(2)]
    L_XC = at(X + 73728, [128, T], F32)
    L_XCB = at(X + 81920, [128, T], BF16)
    PTt = [at(X + 70080 + 2048 * i, [128, 2 * BLK], BF16) for i in range(2)]
    OTOK = [at(X + 74176 + 1024 * i, [128, 4, 128], BF16) for i in range(2)]
    DD1 = at(X + 76224, [128, 4, 64], F32)
    DD2 = at(X + 77248, [128, 4, 64], F32)
    QZ = [at(X + 78272 + 1024 * i, [128, BLK], BF16) for i in range(2)]
    ADAS = [at(X + 16384 * i, [128, 8, BLK], F32) for i in range(2)]
    WOS = [at(X + 8192 + 16384 * i, [128, 8, BLK], F32) for i in range(2)]
    WOB = [at(X + 40960 + 8192 * i, [128, 8, BLK], BF16) for i in range(2)]
    FWS = at(X + 0, [128, 8, BLK], F32)
    FWB = [at(X + 16384 + 8192 * i, [128, 8, BLK], BF16) for i in range(2)]
    FDS = at(X + 32768, [128, 2, D_MODEL], F32)
    FDB = [at(X + 40960 + 4096 * i, [128, 2, D_MODEL], BF16) for i in range(2)]
    FY = [at(X + 49152 + 4096 * i, [128, T], BF16) for i in range(2)]
    FU0 = [at(X + 57344 + 4096 * i, [128, T], BF16) for i in range(2)]
    FU2 = [at(X + 65536 + 4096 * i, [128, T], BF16) for i in range(2)]
    FACT = [at(X + 73728 + 8192 * i, [128, 2, T], BF16) for i in range(2)]
    FOUT = [at(X + 90112 + 2048 * i, [128, BLK], F32) for i in range(2)]

    ps = nc.alloc_psum_tensor("ps", [128, 8, BLK], F32)

    def dma(out, in_, key, reads=(), writes=(), eng='sp'):
        P.op(eng, lambda: P.E[eng].dma_start(out=out, in_=in_), reads, writes, dma=key)

    def mm(out, lhsT, rhs, start, stop, reads, writes, tp=None):
        if tp is None:
            P.op('pe', lambda: nc.tensor.matmul(out, lhsT=lhsT, rhs=rhs, start=start, stop=stop), reads, writes)
        else:
            P.op('pe', lambda: nc.tensor.matmul(out, lhsT=lhsT, rhs=rhs, start=start, stop=stop, tile_position=tp),
                 reads, writes)

    def act(out, in_, func, reads, writes, bias=None, scale=None, accum=None):
        kw = {}
        if bias is not None:
            kw['bias'] = bias
        if scale is not None:
            kw['scale'] = scale
        if accum is not None:
            kw['accum_out'] = accum
        P.op('act', lambda: nc.scalar.activation(out=out, in_=in_, func=func, **kw), reads, writes)

    def vtt(out, in0, in1, op, reads, writes, eng='dve'):
        P.op(eng, lambda: P.E[eng].tensor_tensor(out=out, in0=in0, in1=in1, op=op), reads, writes)

    def vts(out, in0, s1, s2, op0, op1, reads, writes, eng='dve'):
        if op1 is None:
            P.op(eng, lambda: P.E[eng].tensor_scalar(out=out, in0=in0, scalar1=s1, scalar2=None, op0=op0), reads, writes)
        else:
            P.op(eng, lambda: P.E[eng].tensor_scalar(out=out, in0=in0, scalar1=s1, scalar2=s2, op0=op0, op1=op1),
                 reads, writes)

    def vstt(out, in0, scalar, in1, op0, op1, reads, writes):
        P.op('dve', lambda: nc.vector.scalar_tensor_tensor(out=out, in0=in0, scalar=scalar, in1=in1, op0=op0, op1=op1),
             reads, writes)

    def vcopy(out, in_, reads, writes, eng='dve'):
        P.op(eng, lambda: P.E[eng].tensor_copy(out=out, in_=in_), reads, writes)

    def vrecip(out, in_, reads, writes):
        P.op('dve', lambda: nc.vector.reciprocal(out=out, in_=in_), reads, writes)

    def memset(ap, val, writes, eng='dve'):
        P.op(eng, lambda: P.E[eng].memset(ap, val), (), writes)

    def bs(b):
        return slice(b * BLK, (b + 1) * BLK)

    for kc in range(8):
        dma(xT[:, kc, :], d_xT[kc * 128:(kc + 1) * 128, :], 'ld_x', writes=[('x', kc, b) for b in range(NB)])
    small_loads = [
        (cT[:], d_cT[:, :], 'cT'), (flags[:], d_flags[:, :], 'flags'),
        (fng[:], d_fg[:, :], 'fng'),
    ]
    for l in range(DEPTH):
        small_loads += [
            (adab[:, l, :], d_adab[l], 'adab'), (n1g[:, l, :], d_n1[l], 'n1g'), (n2g[:, l, :], d_n2[l], 'n2g'),
            (lcw[:, l, :], d_lcw[l], 'lcw'), (lcb[:, l, :], d_lcb[l], 'lcb'), (gb[:, l, :], d_gb[l], 'gb'),
            (lam[:, l, :], d_lam[l], 'lam'), (h0[:, l, :], d_h0[l], 'h0'), (qkg[:, l, :], d_qkg[l], 'qkg'),
            (dg[:, l, :], d_dg[l].partition_broadcast(128), 'dg'),
            (fcw[:, l, :], d_fcw[l], 'fcw'), (fcb[:, l, :], d_fcb[l], 'fcb'),
        ]
    for (o, i, k) in small_loads:
        dma(o, i, 'ld_c', writes=[k])
    CK = ['cT', 'flags', 'fng', 'adab', 'n1g', 'n2g', 'lcw', 'lcb', 'gb', 'lam', 'h0', 'qkg', 'dg', 'fcw', 'fcb']
    memset(ones[:], 1.0, ['ones'])
    memset(bones[:], 0.0, ['bones'])
    memset(bones[0:64, 0:64], 1.0, ['bones'])
    memset(bones[64:128, 64:128], 1.0, ['bones'])
    for (dst, src, k) in ((permg, d_permg, 'permg'), (permd, d_permd, 'permd')):
        dma(sstage[:], src[:, :], 'ld_c2', writes=['sstage'])
        vcopy(dst[:], sstage[:], ['sstage'], [k])
    d_ident = din("ident", [128, 128])
    dma(sstage[:], d_ident[:, :], 'ld_c2', writes=['sstage'])
    vcopy(ident[:], sstage[:], ['sstage'], ['ident'])
    dma(ADAS[1][:, 0:4, :].rearrange("p a b -> p (a b)"), d_mB[:, :], 'ld_c2', writes=['adas1'])
    vcopy(mBb[:], ADAS[1][:, 0:4, :], ['adas1'], ['mB'])
    P.barrier()
    act(siluc[:], cT[:], AF.Silu, CK, ['siluc'])
    act(lsc[:], lam[:], AF.Exp, CK, ['lsc'], scale=-1.0)
    act(lsc[:], lsc[:], AF.Ln, ['lsc'], ['lsc'], bias=1.0)
    vts(lsc2[:], lsc[:], -16.0, None, ALU.mult, None, ['lsc'], ['lsc2'])
    vts(lsc[:], lsc[:], -8.0, None, ALU.mult, None, ['lsc', 'lsc2'], ['lsc'])
    vts(lcwn[:], lcw[:], flags[:, 1:2], -1.0, ALU.mult, ALU.mult, CK, ['lcwn'])
    vts(fcwn[:], fcw[:], flags[:, 1:2], -1.0, ALU.mult, ALU.mult, CK, ['fcwn'])
    import math
    lam_init = [0.8 - 0.6 * math.exp(-0.3 * l) for l in range(DEPTH)]
    for l in range(DEPTH):
        dma(sstage[:], d_dl[l].partition_broadcast(128), 'ld_c2', writes=['sstage'])
        for k in range(2):
            vtt(dlt[:, 0:32], sstage[:, 64 * k:64 * k + 32], sstage[:, 64 * k + 32:64 * k + 64], ALU.mult, ['sstage'], ['dlt'])
            P.op('dve', lambda k=k, l=l: nc.vector.reduce_sum(out=dle[:, 2 * l + k:2 * l + k + 1], in_=dlt[:, 0:32],
                                                            axis=mybir.AxisListType.X), ['dlt'], ['dle'])
        act(dle[:, 2 * l:2 * l + 2], dle[:, 2 * l:2 * l + 2], AF.Exp, ['dle'], ['dle'])
        vts(sml[:, 0:1], dle[:, 2 * l + 1:2 * l + 2], -lam_init[l], None, ALU.add, None, ['dle'], ['sml'])
        vtt(nlam[:, l:l + 1], sml[:, 0:1], dle[:, 2 * l:2 * l + 1], ALU.subtract, ['sml', 'dle'], ['nlam'])
        vts(dg[:, l, :], dg[:, l, :], 1.0 - lam_init[l], None, ALU.mult, None, CK, ['dg'])
    P.barrier()

    ADS = [at(X + 16384 * i, [128, 8, BLK], F32) for i in range(3)]
    ADB = [at(X + 49152 + 8192 * i, [128, 8, BLK], BF16) for i in range(2)]
    ROW = at(X + 65536, [1, 6 * D_MODEL], F32)
    onef = calloc([128, 2], F32)
    mhalf = calloc([128, 4], F32)
    memset(mhalf[:], -0.5, ['mhalf'])
    silucb = calloc([128, 8], BF16)
    memset(onef[:], 1.0, ['onef'])
    vcopy(silucb[:], siluc[:], ['siluc'], ['silucb'])
    adac = [0]

    def ada(l):
        for g in range(12):
            n = adac[0]
            adac[0] += 1
            s3 = n % 3
            s = n % 2
            for hf in range(2):
                dma(ADS[s3][:, 4 * hf:4 * hf + 4, :],
                    d_adaw[l][hf * 512:(hf + 1) * 512, g * 512:(g + 1) * 512].rearrange("(kc p) n -> p kc n", p=128),
                    'ld_ada%d' % s3, writes=[('ads', s3)], eng=('sp' if hf == 0 else 'pool'))
            if n % 2 == 0:
                act(ADB[s][:], ADS[s3][:], AF.Copy, [('ads', s3)], [('adb', s)])
            else:
                vcopy(ADB[s][:], ADS[s3][:], [('ads', s3)], [('adb', s)])
            bank = g % 4
            for kc in range(8):
                mm(ps[0:1, bank, :], silucb[:, kc:kc + 1], ADB[s][:, kc, :], kc == 0, kc == 7,
                   [('adb', s), 'silucb'], [('ps', bank)])
            vcopy(ROW[0:1, g * 512:(g + 1) * 512], ps[0:1, bank, :], [('ps', bank)], ['row'])
        for col in range(48):
            mm(ps[:, 7, col:col + 1], ROW[0:1, col * 128:(col + 1) * 128], onef[0:1, 0:1], True, True,
               ['row', 'onef'], [('ps', 7)])
        vtt(mod[:, l, :], ps[:, 7, 0:48], adab[:, l, :], ALU.add, [('ps', 7)], [('mod', l)])
        vstt(gs1[:, l, :], mod[:, l, 8:16], 1.0, n1g[:, l, :], ALU.add, ALU.mult, [('mod', l)], [('gs', l)])
        vstt(gs2[:, l, :], mod[:, l, 32:40], 1.0, n2g[:, l, :], ALU.add, ALU.mult, [('mod', l)], [('gs', l)])

    ADS2 = at(X + 80320, [128, 8, 256], F32)
    ADB2 = at(X + 88512, [128, 8, 256], BF16)
    ROW2 = at(X + 92608, [1, 256], F32)
    ada1_state = [0]

    def ada1_scatter(g):
        for j in range(2):
            col = 256 + 2 * g + j
            mm(ps[:, 7, col:col + 1], ROW2[0:1, j * 128:(j + 1) * 128], onef[0:1, 0:1], True, True,
               ['row2', 'onef'], [('ps', 7)])

    def ada1_step():
        g = ada1_state[0]
        if g >= 24:
            return
        ada1_state[0] += 1
        if g > 0:
            ada1_scatter(g - 1)
        for hf in range(2):
            dma(ADS2[:, 4 * hf:4 * hf + 4, :],
                d_adaw[1][hf * 512:(hf + 1) * 512, g * 256:(g + 1) * 256].rearrange("(kc p) n -> p kc n", p=128),
                'ld_ada2', writes=['ads2'], eng=('sp' if hf == 0 else 'pool'))
        vcopy(ADB2[:], ADS2[:], ['ads2'], ['adb2'])
        for kc in range(8):
            mm(ps[0:1, 7, 0:256], silucb[:, kc:kc + 1], ADB2[:, kc, :], kc == 0, kc == 7, ['adb2', 'silucb'], [('ps', 7)])
        vcopy(ROW2[0:1, :], ps[0:1, 7, 0:256], [('ps', 7)], ['row2'])

    def ada1_finish():
        while ada1_state[0] < 24:
            ada1_step()
        ada1_scatter(23)
        vtt(mod[:, 1, :], ps[:, 7, 256:304], adab[:, 1, :], ALU.add, [('ps', 7)], [('mod', 1)])
        vstt(gs1[:, 1, :], mod[:, 1, 8:16], 1.0, n1g[:, 1, :], ALU.add, ALU.mult, [('mod', 1)], [('gs', 1)])
        vstt(gs2[:, 1, :], mod[:, 1, 32:40], 1.0, n2g[:, 1, :], ALU.add, ALU.mult, [('mod', 1)], [('gs', 1)])

    def norm_mod(gs_ap, sh_ap, rkeys, out_fn):
        for b in range(NB):
            for kc in range(8):
                if kc % 2 == 0:
                    act(SQ[kc % 2][:], xT[:, kc, bs(b)], AF.Square, [('x', kc, b)], [('sq', kc % 2)])
                else:
                    vtt(SQ[kc % 2][:], xT[:, kc, bs(b)], xT[:, kc, bs(b)], ALU.mult, [('x', kc, b)], [('sq', kc % 2)])
                mm(ps[:, b % 2, :], ones[:], SQ[kc % 2][:], kc == 0, kc == 7, [('sq', kc % 2), 'ones'], [('ps', b % 2)])
            act(RT[b % 2][:], ps[:, b % 2, :], AF.Ln, [('ps', b % 2)], [('rt', 0)], bias=EPS, scale=1.0 / D_MODEL)
            act(RT[b % 2][:], RT[b % 2][:], AF.Exp, [('rt', 0)], [('rt', 0)], scale=-0.5)
            for kc in range(8):
                vtt(T1[kc % 2][:], xT[:, kc, bs(b)], RT[b % 2][:], ALU.mult, [('x', kc, b), ('rt', 0)], [('t1', kc % 2)])
                out_fn(kc, b, T1[kc % 2], ('t1', kc % 2))

    wcount = [0]

    def load_w128(src_ap):
        i = wcount[0] % 2
        wcount[0] += 1
        dma(WS[i][:].rearrange("p kc n -> p (kc n)"), src_ap, 'ld_ws%d' % i, writes=[('ws', i)])
        vcopy(WB[i][:], WS[i][:], [('ws', i)], [('wb', i)], eng='pool')
        return i

    def layer(l):
        sh1, g1 = mod[:, l, 0:8], mod[:, l, 16:24]
        sh2, g2 = mod[:, l, 24:32], mod[:, l, 40:48]
        MK = [('mod', l), ('gs', l)]

        def o1(kc, b, t, tk):
            act(actT[:, kc, bs(b)], t[:], AF.Identity, [tk] + MK, [('h', kc, b)],
                bias=sh1[:, kc:kc + 1], scale=gs1[:, l, kc:kc + 1])
        norm_mod(None, None, None, o1)

        for i in range(8):
            dma(sstage[:], d_gw[l, i], 'ld_c2', writes=['sstage'])
            vcopy(gwb[:, i, :], sstage[:], ['sstage'], ['gwb'], eng='pool')

        pbank = [0]

        def proj_chunk(ci, post):
            wi = load_w128(d_winC[l, ci])
            pend = None
            for b in range(NB):
                pb = pbank[0] % 6
                pbank[0] += 1
                for kc in range(8):
                    mm(ps[:, pb, :], WB[wi][:, kc, :], actT[:, kc, bs(b)], kc == 0, kc == 7,
                       [('wb', wi), ('h', kc, b)], [('ps', pb)])
                nxt = post(b, ps[:, pb, :], ('ps', pb))
                if pend is not None:
                    pend()
                pend = nxt
            if pend is not None:
                pend()

        for cc in range(2):
            def post_lx(b, pt, pk, cc=cc):
                act(LRUX[:, cc, bs(b)], pt, AF.Copy, [pk], [('lrux', cc)])
            proj_chunk(cc, post_lx)
        for cc in range(2):
            def post_lg(b, pt, pk, cc=cc):
                i = b % 2
                act(QN[i][:], pt, AF.Square, [pk], [('qn', i)])
                vts(QN[i][:], QN[i][:], 0.044715, 1.0, ALU.mult, ALU.add, [('qn', i)], [('qn', i)])
                vtt(QN[i][:], QN[i][:], pt, ALU.mult, [('qn', i), pk], [('qn', i)])
                act(QN[i][:], QN[i][:], AF.Sigmoid, [('qn', i)], [('qn', i)], scale=1.5957691216057308)
                vtt(LRUG[:, cc, bs(b)], QN[i][:], pt, ALU.mult, [('qn', i), pk], [('lrug', cc)])
            proj_chunk(2 + cc, post_lg)

        lru_phase(l)

        P.barrier()
        ctx_load(l)
        memset(VG[:, :, :, 64:65], 1.0, ['vg'], eng='pool')
        memset(VD[:, :, :, 64:65], 1.0, ['vd'], eng='pool')

        auxc = [0]

        def post_qk(dst_fn, gcol, out_dram=None):
            def post(b, pt, pk):
                i = auxc[0] % 2
                auxc[0] += 1
                act(SQ[i][:], pt, AF.Square, [pk], [('sq', i)])

                def cont():
                    ab = 6 + i
                    aux, auxk = ps[:, ab, :], ('ps', ab)
                    mm(aux, bones[:], SQ[i][:], True, True, [('sq', i), 'bones'], [auxk])
                    act(RT[i][:], aux, AF.Ln, [auxk], [('rt', 0)], bias=EPS, scale=1.0 / 64)
                    act(RT[i][:], RT[i][:], AF.Exp, [('rt', 0)], [('rt', 0)], scale=-0.5)
                    if out_dram is None:
                        vstt(dst_fn(b), pt, qkg[:, l, gcol:gcol + 1], RT[i][:], ALU.mult, ALU.mult, [pk, ('rt', 0)] + CK,
                             [('qkv',)])
                    else:
                        vstt(QN[i][:], pt, qkg[:, l, gcol:gcol + 1], RT[i][:], ALU.mult, ALU.mult, [pk, ('rt', 0)] + CK,
                             [('qn', i)])
                        for kv in range(2):
                            vcopy(KZ[kv][0:64, 256 + b * BLK:256 + (b + 1) * BLK], QN[i][kv * 64:(kv + 1) * 64, :],
                                  [('qn', i)], [('qkv',)])
                        dma(out_dram(b), QN[i][:], 'st_k', reads=[('qn', i)], eng='act')
                return cont
            return post

        for c in range(4):
            proj_chunk(4 + c, post_qk(lambda b, c=c: QT[:, c, bs(b)], 0))
        proj_chunk(8, post_qk(None, 1,
                              out_dram=lambda b: o_kT[l][:, bs(b)]))
        for c in range(2):
            def post_dq(b, pt, pk, c=c):
                act(DQT[:, c, bs(b)], pt, AF.Copy, [pk], [('qkv',)])
            proj_chunk(9 + c, post_dq)
        for c in range(2):
            def post_dk(b, pt, pk, c=c):
                i = b % 2
                act(QN[i][:], pt, AF.Copy, [pk], [('qn', i)])
                for r in range(4):
                    n = 4 * c + r
                    vcopy(DKZ[n // 3][(n % 3) * 32:(n % 3) * 32 + 32, 256 + b * BLK:256 + (b + 1) * BLK],
                          QN[i][r * 32:(r + 1) * 32, :], [('qn', i)], [('qkv',)])
                dma(o_dkT[l][c * 128:(c + 1) * 128, bs(b)], QN[i][:], 'st_dk', reads=[('qn', i)], eng='act')
            proj_chunk(11 + c, post_dk)
        for vc in range(3):
            wi = load_w128(d_winC[l, 13 + vc])
            for tg in range(4):
                pbk = 4 * (vc % 2) + tg
                for tt in range(4):
                    tok = tg * 4 + tt
                    for kc in range(8):
                        mm(ps[:, pbk, tt * 128:(tt + 1) * 128], actT[:, kc, tok * 128:(tok + 1) * 128], WB[wi][:, kc, :],
                           kc == 0, kc == 7, [('wb', wi)] + [('h', kc, tok // 4)], [('ps', pbk)])
                i = tg % 2
                vk = ('vst', 0) if i == 0 else ('qn', 1)
                act(VST[i][:], ps[:, pbk, :].rearrange("p (a b) -> p a b", a=4), AF.Copy, [('ps', pbk)], [vk])
                if vc == 0:
                    vcopy(VG[:, 2 + tg * 4:2 + tg * 4 + 4, :, 0:64],
                          VST[i][:].rearrange("p a (h d) -> p a h d", h=2), [vk], [('qkv',)])
                    dma(o_v[l][tg * 512:(tg + 1) * 512, :].rearrange("(a p) n -> p a n", p=128), VST[i][:], 'st_v',
                        reads=[vk], eng='act')
                else:
                    hb = 2 * (vc - 1)
                    vcopy(VD[:, 2 + tg * 4:2 + tg * 4 + 4, hb:hb + 2, 0:64],
                          VST[i][:].rearrange("p a (h d) -> p a h d", h=2), [vk], [('qkv',)])
                    dma(o_dv[l][tg * 512:(tg + 1) * 512, hb * 64:(hb + 2) * 64].rearrange("(a p) n -> p a n", p=128),
                        VST[i][:], 'st_v', reads=[vk], eng='act')
        P.barrier()
        for b in range(NB):
            dma(ROPET[:, 0:2, :], d_ropeg[:, :, bs(b)].rearrange("a p n -> p a n"), 'ld_rope', writes=['ropet'])
            dma(ROPET[:, 2:4, :], d_roped[:, :, bs(b)].rearrange("a p n -> p a n"), 'ld_rope', writes=['ropet'])
            items = [(QT[:, c, bs(b)], permg, 0, 128) for c in range(4)]
            items += [(KZ[kv][0:64, 256 + b * BLK:256 + (b + 1) * BLK], permg, 0, 64) for kv in range(2)]
            items += [(DQT[:, c, bs(b)], permd, 2, 128) for c in range(2)]
            items += [(DKZ[t][:, 256 + b * BLK:256 + (b + 1) * BLK], permd, 2, 128) for t in range(3)]
            for n, (ap_, pm, to, np_) in enumerate(items):
                i = n % 2
                mm(ps[0:np_, i, :], pm[0:np_, 0:np_], ap_, True, True, [('rp', b, n)], [('ps', i)])
                vtt(T1[i][0:np_, :], ap_, ROPET[0:np_, to, :], ALU.mult, [('rp', b, n), 'ropet'], [('t1', i)])
                vtt(T2[i][0:np_, :], ps[0:np_, i, :], ROPET[0:np_, to + 1, :], ALU.mult, [('ps', i), 'ropet'], [('t2', i)])
                vtt(ap_, T1[i][0:np_, :], T2[i][0:np_, :], ALU.add, [('t1', i), ('t2', i)], [('rp', b, n)])
        P.barrier()
        for t in range(3):
            dma(DKZ[t][96:105, :], d_Ak[:, :], 'ld_ctx', writes=[('qkv',)])

        attention(l)
        if l == 0 and DEPTH > 1:
            ada1_finish()
        P.barrier()

        for og in range(2):
            dma(WOS[og][:], d_wout[l][:, og * 512:(og + 1) * 512].rearrange("(kc p) n -> p kc n", p=128),
                'ld_wo%d' % og, writes=[('wos', og)])
            act(WOB[og][:], WOS[og][:], AF.Copy, [('wos', og)], [('wob', og)])
        n = 0
        for og in range(2):
            for oc in range(4):
                ch = og * 4 + oc
                for b in range(NB):
                    pbk = n % 8
                    n += 1
                    for kc in range(8):
                        src = LRUOUT[:, kc, bs(b)] if kc < 2 else actT[:, kc, bs(b)]
                        mm(ps[:, pbk, :], WOB[og][:, kc, oc * 128:(oc + 1) * 128], src, kc == 0, kc == 7,
                           [('wob', og)], [('ps', pbk)])
                    vstt(xT[:, ch, bs(b)], ps[:, pbk, :], g1[:, ch:ch + 1], xT[:, ch, bs(b)], ALU.mult, ALU.add,
                         [('ps', pbk), ('x', ch, b)] + MK, [('x', ch, b)])
        P.barrier()

        def o2(kc, b, t, tk):
            act(actT[:, kc, bs(b)], t[:], AF.Identity, [tk] + MK, [('h', kc, b)],
                bias=sh2[:, kc:kc + 1], scale=gs2[:, l, kc:kc + 1])
        norm_mod(None, None, None, o2)
        ffn(l, g2, MK)
        P.barrier()

    def ctx_load(l):
        dma(T1[0][:, 0:256], d_ckT[l], 'ld_ctx', writes=[('t1', 0)])
        for kv in range(2):
            memset(KZ[kv][64:128, :], 0.0, [('qkv',)], eng='pool')
            dma(KZ[kv][64:73, :], d_Ak[:, :], 'ld_ctx', writes=[('qkv',)])
            vcopy(KZ[kv][0:64, 0:256], T1[0][kv * 64:(kv + 1) * 64, 0:256], [('t1', 0)], [('qkv',)])
        stg = [(T1[1], ('t1', 1)), (RT[0], ('rt', 0))]
        for t in range(3):
            memset(DKZ[t][96:128, :], 0.0, [('qkv',)], eng='pool')
        memset(DKZ[2][64:96, :], 0.0, [('qkv',)], eng='pool')
        for c in range(2):
            tt, tk = stg[c]
            dma(tt[:, 0:256], d_cdkT[l][c * 128:(c + 1) * 128, :], 'ld_ctx', writes=[tk])
            for r in range(4):
                n = 4 * c + r
                vcopy(DKZ[n // 3][(n % 3) * 32:(n % 3) * 32 + 32, 0:256], tt[r * 32:(r + 1) * 32, 0:256], [tk], [('qkv',)])
        dma(QN[0][:, 0:256].rearrange("p (a n) -> p a n", a=2), d_cv[l].rearrange("(a p) n -> p a n", p=128), 'ld_ctx',
            writes=[('qn', 0)])
        vcopy(VG[:, 0:2, :, 0:64], QN[0][:, 0:256].rearrange("p (a h d) -> p a h d", a=2, h=2), [('qn', 0)], [('qkv',)])
        dma(QN[1][:].rearrange("p (a n) -> p a n", a=2), d_cdv[l].rearrange("(a p) n -> p a n", p=128), 'ld_ctx',
            writes=[('qn', 1)])
        vcopy(VD[:, 0:2, :, 0:64], QN[1][:].rearrange("p (a h d) -> p a h d", a=2, h=4), [('qn', 1)], [('qkv',)])

    def lru_phase(l):
        P.barrier()
        H2 = T // 2
        for cc in range(2):
            x = LRUX[:, cc, :]
            w = lambda k: lcw[:, l, cc * 4 + k:cc * 4 + k + 1]
            wn = lambda k: lcwn[:, l, cc * 4 + k:cc * 4 + k + 1]
            XK = ['xc']
            act(L_XC[:], x, AF.Identity, [('lrux', cc)] + CK, XK, bias=lcb[:, l, cc:cc + 1], scale=w(2))
            vstt(L_XC[:, 2:T], x[:, 0:T - 2], w(0), L_XC[:, 2:T], ALU.mult, ALU.add, XK + CK, XK)
            vstt(L_XC[:, 1:T], x[:, 0:T - 1], w(1), L_XC[:, 1:T], ALU.mult, ALU.add, XK + CK, XK)
            vstt(L_XC[:, 0:T - 1], x[:, 1:T], w(3), L_XC[:, 0:T - 1], ALU.mult, ALU.add, XK + CK, XK)
            vstt(L_XC[:, 256:T:256], x[:, 254:T - 2:256], wn(0), L_XC[:, 256:T:256], ALU.mult, ALU.add, XK + ['lcwn'], XK)
            vstt(L_XC[:, 256:T:256], x[:, 255:T - 1:256], wn(1), L_XC[:, 256:T:256], ALU.mult, ALU.add, XK + ['lcwn'], XK)
            vstt(L_XC[:, 257:T:256], x[:, 255:T - 1:256], wn(0), L_XC[:, 257:T:256], ALU.mult, ALU.add, XK + ['lcwn'], XK)
            vstt(L_XC[:, 255:T - 1:256], x[:, 256:T:256], wn(3), L_XC[:, 255:T - 1:256], ALU.mult, ALU.add, XK + ['lcwn'], XK)
            vcopy(L_XCB[:], L_XC[:], XK, ['xcb'])
            for h in range(2):
                for dr in range(2):
                    for ri in range(2):
                        combo = dr * 2 + ri
                        gi = dr * 4 + cc * 2 + ri
                        for k in range(2):
                            b = 2 * h + k
                            mm(ps[:, combo * 2 + k, :], gwb[:, gi, :], L_XCB[:, bs(b)], True, True, ['xcb', 'gwb'],
                               [('ps', combo * 2 + k)])
                for dr in range(2):
                    for ri in range(2):
                        combo = dr * 2 + ri
                        gi = dr * 4 + cc * 2 + ri
                        dst = (L_R if ri == 0 else L_I)[dr]
                        dk = ('lr' if ri == 0 else 'li', dr)
                        act(dst[:, h * H2:(h + 1) * H2], ps[:, combo * 2:combo * 2 + 2, :].rearrange("p a b -> p (a b)"),
                            AF.Sigmoid, [('ps', combo * 2), ('ps', combo * 2 + 1)] + CK, [dk], bias=gb[:, l, gi:gi + 1])
            for dr in range(2):
                sc2 = lsc2[:, l, dr * 2 + cc:dr * 2 + cc + 1]
                act(L_T[dr][:], L_R[dr][:], AF.Exp, [('lr', dr), 'lsc2'], [('lt', dr)], scale=sc2)
            for dr in range(2):
                sc = lsc[:, l, dr * 2 + cc:dr * 2 + cc + 1]
                act(L_R[dr][:], L_R[dr][:], AF.Exp, [('lr', dr), 'lsc'], [('lr', dr)], scale=sc)
            for dr in range(2):
                act(L_T[dr][:], L_T[dr][:], AF.Sqrt, [('lt', dr)], [('lt', dr)], bias=1.0, scale=-1.0)
            for dr in range(2):
                A, I_, T_ = L_R[dr], L_I[dr], L_T[dr]
                ak, ik, tk = ('lr', dr), ('li', dr), ('lt', dr)
                vtt(I_[:], I_[:], L_XC[:], ALU.mult, [ik] + XK, [ik])
                vtt(I_[:], I_[:], T_[:], ALU.mult, [ik, tk], [ik])
                e = 0 if dr == 0 else T - 1
                hh = h0[:, l, dr * 2 + cc:dr * 2 + cc + 1]
                vstt(I_[:, e:e + 1], A[:, e:e + 1], hh, I_[:, e:e + 1], ALU.mult, ALU.add, [ik, ak] + CK, [ik])
                if dr == 0:
                    vts(A[:, 256:T:256], A[:, 256:T:256], flags[:, 0:1], None, ALU.mult, None, [ak] + CK, [ak])
                    P.op('dve', lambda A=A, I_=I_, T_=T_: nc.vector.tensor_tensor_scan(
                        out=T_[:], data0=A[:], data1=I_[:], initial=0.0, op0=ALU.mult, op1=ALU.add), [ak, ik, tk], [tk])
                    vcopy(st[:, (l * 4 + cc) * 8:(l * 4 + cc) * 8 + 8], T_[:, 255:T:256], [tk], ['st'])
                else:
                    vts(A[:, 255:T - 1:256], A[:, 255:T - 1:256], flags[:, 0:1], None, ALU.mult, None, [ak] + CK, [ak])
                    P.op('dve', lambda A=A, I_=I_, T_=T_: nc.vector.tensor_tensor_scan(
                        out=T_[:, ::-1], data0=A[:, ::-1], data1=I_[:, ::-1], initial=0.0, op0=ALU.mult, op1=ALU.add),
                        [ak, ik, tk], [tk])
                    vcopy(st[:, (l * 4 + 2 + cc) * 8:(l * 4 + 2 + cc) * 8 + 8], T_[:, 0:T:256], [tk], ['st'])
            vtt(L_T[0][:], L_T[0][:], L_T[1][:], ALU.add, [('lt', 0), ('lt', 1)], [('lt', 0)])
            vtt(LRUOUT[:, cc, :], L_T[0][:], LRUG[:, cc, :], ALU.mult, [('lt', 0), ('lrug', cc)], [('lruout', cc)])

    def attention(l):
        SKEW = 2
        sc_d = 32 ** -0.5
        heads = []
        pair = 0
        for c in range(4):
            for qb in range(NB):
                for hh in range(2):
                    p0 = hh * 64
                    heads.append(dict(kind='g', qrows=QT[p0:p0 + 64, c, bs(qb)], p0=p0, nk=64,
                                      kfn=(lambda kt, hh=hh: KZ[hh][:, kt * 128:(kt + 1) * 128]),
                                      vfn=(lambda kt, hh=hh: VG[:, kt, hh, 0:65]), scale=0.125, qb=qb, hh=hh,
                                      comp=0, pair=pair, last=(hh == 1), dst=2 + c))
                pair += 1
        for c in range(2):
            for qb in range(NB):
                for hh in range(2):
                    for comp in range(2):
                        p0 = hh * 64 + comp * 32
                        heads.append(dict(kind='d', qrows=DQT[p0:p0 + 32, c, bs(qb)], p0=p0, nk=32,
                                          kfn=(lambda kt, n=4 * c + 2 * hh + comp: DKZ[n // 3][:, kt * 128:(kt + 1) * 128]),
                                          slot=(4 * c + 2 * hh + comp) % 3,
                                          vfn=(lambda kt, h=2 * c + hh: VD[:, kt, h, 0:65]), scale=sc_d, qb=qb, hh=hh,
                                          comp=comp, pair=pair, last=(hh == 1 and comp == 1), dst=6 + c))
                pair += 1
        KP = KT // 2
        tiles = [(hi, kp) for hi in range(len(heads)) for kp in range(KP)]
        NTL = len(tiles)

        def accv(hi):
            ab = 4 + hi % 2
            v = ps[:, ab, 0:260].rearrange("p (j c) -> p j c", j=4)
            return ab, v

        def finish(H, hi):
            oi = H['pair'] % 2
            hh = H['hh']
            ab, v = accv(hi)
            acc4, den4 = v[:, :, 0:64], v[:, :, 64:65]
            PACC = [('ps', ab)]
            rec = sml[:, 0:4].unsqueeze(2)
            P.op('dve', lambda: nc.vector.reciprocal(out=rec, in_=den4), PACC, ['smlr'])
            recb = rec.broadcast_to([128, 4, 64])
            if H['kind'] == 'g':
                vtt(OTOK[oi][:, :, hh * 64:(hh + 1) * 64], acc4, recb, ALU.mult, PACC + ['smlr'], [('otok', oi)])
            elif H['comp'] == 0:
                vtt(DD1[:], acc4, recb, ALU.mult, PACC + ['smlr'], ['dd1'])
            else:
                vts(sml[:, 0:4], sml[:, 0:4], nlam[:, l:l + 1], None, ALU.mult, None, ['smlr', 'nlam'], ['smlr'])
                vtt(DD2[:], acc4, recb, ALU.mult, PACC + ['smlr'], ['dd2'])

        def finish_b(H):
            oi = H['pair'] % 2
            hh = H['hh']
            vtt(DD2[:], DD2[:], DD1[:], ALU.add, ['dd2', 'dd1'], ['dd2'])
            vtt(DD1[:], DD2[:], DD2[:], ALU.mult, ['dd2', 'dd1'], ['dd1'])
            P.op('dve', lambda: nc.vector.reduce_sum(out=sml[:, 8:12], in_=DD1[:], axis=mybir.AxisListType.X),
                 ['dd1'], ['smls'])
            vts(sml[:, 8:12], sml[:, 8:12], 1.0 / 64, EPS, ALU.mult, ALU.add, ['smls'], ['smls'])
            vtt(sml[:, 8:12], sml[:, 8:12], mhalf[:, 0:4], ALU.pow, ['smls', 'mhalf'], ['smls'], eng='pool')
            vtt(DD2[:], DD2[:], sml[:, 8:12].unsqueeze(2).broadcast_to([128, 4, 64]), ALU.mult, ['dd2', 'smls'], ['dd2'])
            vtt(OTOK[oi][:, :, hh * 64:(hh + 1) * 64], DD2[:], dg[:, l, :].unsqueeze(1).broadcast_to([128, 4, 64]),
                ALU.mult, ['dd2', 'dg'], [('otok', oi)])

        def transposes(H):
            oi = H['pair'] % 2
            tpb = ps[:, 6, :].bitcast(BF16)
            for j in range(4):
                P.op('pe', lambda j=j: nc.tensor.transpose(tpb[:, j * 128:(j + 1) * 128], OTOK[oi][:, j, :], ident[:]),
                     [('otok', oi), 'ident'], [('ps', 6)])
            vcopy(actT[:, H['dst'], bs(H['qb'])], tpb[:, 0:BLK], [('ps', 6)], [('h', H['dst'], H['qb'])])

        SKEW = 1
        deferred = []
        for i in range(NTL + SKEW):
            if l == 0 and DEPTH > 1 and i % 24 == 12:
                ada1_step()
            if i < NTL:
                hi, kp = tiles[i]
                H = heads[hi]
                zi = hi % 2
                s = i % 2
                if kp == 0:
                    memset(QZ[zi][:], 0.0, [('qz', zi)], eng='pool')
                    if H['kind'] == 'g':
                        vcopy(QZ[zi][0:64, :], H['qrows'], [('qkv',)], [('qz', zi)], eng='pool')
                        vcopy(QZ[zi][64:73, :], mBb[64:73, H['qb'], :], ['mB'], [('qz', zi)], eng='pool')
                    else:
                        r0 = H['slot'] * 32
                        vcopy(QZ[zi][r0:r0 + 32, :], H['qrows'], [('qkv',)], [('qz', zi)], eng='pool')
                        vcopy(QZ[zi][96:105, :], mBb[96:105, H['qb'], :], ['mB'], [('qz', zi)], eng='pool')
                for t in range(2):
                    kt = 2 * kp + t
                    bk = 2 * s + t
                    mm(ps[:, bk, :], H['kfn'](kt), QZ[zi][:], True, True, [('qkv',), ('qz', zi)], [('ps', bk)])
                act(PTt[s][:], ps[:, 2 * s:2 * s + 2, :].rearrange("p a b -> p (a b)"), AF.Exp,
                    [('ps', 2 * s), ('ps', 2 * s + 1)], [('pt', s)], scale=H['scale'])
            if i >= SKEW:
                hi, kp = tiles[i - SKEW]
                H = heads[hi]
                s = (i - SKEW) % 2
                ab, v = accv(hi)
                for t in range(2):
                    kt = 2 * kp + t
                    for j in range(4):
                        first = (kt == 0 and j == 0)
                        P.op('pe', lambda t=t, j=j, kt=kt, first=first, v=v, s=s, H=H: nc.tensor.matmul(
                            v[:, j, :], lhsT=PTt[s][:, t * BLK + j * 128:t * BLK + (j + 1) * 128], rhs=H['vfn'](kt),
                            start=first, stop=(kt == KT - 1), skip_group_check=True),
                            [('pt', s), ('qkv',)], [('ps', ab)])
                if kp == KP - 1:
                    finish(H, hi)
                    if H['kind'] == 'd' and H['comp'] == 1:
                        deferred.append((i + 2, 'b', H))
                        if H['last']:
                            deferred.append((i + 4, 't', H))
                    elif H['last']:
                        deferred.append((i + 1, 't', H))
            deferred.sort(key=lambda d: d[0])
            while deferred and deferred[0][0] <= i:
                d = deferred.pop(0)
                (finish_b if d[1] == 'b' else transposes)(d[2])
        while deferred:
            d = deferred.pop(0)
            (finish_b if d[1] == 'b' else transposes)(d[2])

    def ffn(l, g2, MK):
        H2 = T // 2
        nps = [0]
        nfo = [0]
        nhp = [0]

        def load_up(g):
            wi = g % 2
            dma(FWS[:], d_wup[l][:, g * 512:(g + 1) * 512].rearrange("(kc p) n -> p kc n", p=128), 'ld_fu',
                writes=['fws'])
            vcopy(FWB[wi][:], FWS[:], ['fws'], [('fwb', wi)])

        def load_dn(g):
            wi = g % 2
            dma(FDS[:], d_wdn[l][g * 256:(g + 1) * 256, :].rearrange("(j p) n -> p j n", p=128), 'ld_fd', writes=['fds'])
            vcopy(FDB[wi][:], FDS[:], ['fds'], [('fdb', wi)])

        def down_piece(g, k0, k1):
            wi = g % 2
            for k in range(k0, k1):
                oc, b = k // NB, k % NB
                n = nps[0]
                nps[0] += 1
                pbk = 4 + n % 4
                for jj in range(2):
                    mm(ps[:, pbk, :], FDB[wi][:, jj, oc * 128:(oc + 1) * 128], FACT[wi][:, jj, bs(b)], jj == 0, jj == 1,
                       [('fdb', wi), ('fact', wi)], [('ps', pbk)])
                if n % 8 not in (1, 4, 6):
                    vstt(xT[:, oc, bs(b)], ps[:, pbk, :], g2[:, oc:oc + 1], xT[:, oc, bs(b)], ALU.mult, ALU.add,
                         [('ps', pbk), ('x', oc, b)] + MK, [('x', oc, b)])
                else:
                    fi = nfo[0] % 2
                    nfo[0] += 1
                    act(FOUT[fi][:], ps[:, pbk, :], AF.Identity, [('ps', pbk)] + MK, [('fout', fi)], scale=g2[:, oc:oc + 1])
                    vtt(xT[:, oc, bs(b)], xT[:, oc, bs(b)], FOUT[fi][:], ALU.add, [('fout', fi), ('x', oc, b)],
                        [('x', oc, b)], eng='pool')

        def up(g, gd):
            wi = g % 2
            piece = 0
            for jj in range(2):
                j = 2 * g + jj
                for part in range(2):
                    col = (part * 2 + jj) * 128
                    fc = part * 22 + j
                    cw = lambda k, fc=fc: fcw[:, l, fc * 3 + k:fc * 3 + k + 1]
                    Y, U0, U2 = FY[part], FU0[part], FU2[part]
                    for h in range(2):
                        pr = nhp[0] % 2
                        nhp[0] += 1
                        pb = 2 * pr
                        for k in range(2):
                            b = 2 * h + k
                            for kc in range(8):
                                mm(ps[:, pb + k, :], FWB[wi][:, kc, col:col + 128], actT[:, kc, bs(b)], kc == 0, kc == 7,
                                   [('fwb', wi), ('h', kc, b)], [('ps', pb + k)])
                        PK = [('ps', pb), ('ps', pb + 1)]
                        p2 = ps[:, pb:pb + 2, :].rearrange("p a b -> p (a b)")
                        t0 = h * H2
                        yk, u0k, u2k = ('fy', part, h), ('fu0', part), ('fu2', part)
                        act(Y[:, t0:t0 + H2], p2, AF.Identity, PK + CK, [yk], bias=fcb[:, l, fc:fc + 1], scale=cw(1))
                        fl = flags[:, 0:1]
                        y0, y1 = ('fy', part, 0), ('fy', part, 1)
                        if h == 0:
                            act(U0[:, 1:H2 + 1], p2, AF.Copy, PK + CK, [u0k], scale=cw(0))
                            act(U2[:, 0:H2 - 1], p2[:, 1:H2], AF.Copy, PK + CK, [u2k], scale=cw(2))
                            vts(U0[:, 256:H2 + 1:256], U0[:, 256:H2 + 1:256], fl, None, ALU.mult, None, [u0k] + CK, [u0k])
                            vts(U2[:, 255:H2 - 1:256], U2[:, 255:H2 - 1:256], fl, None, ALU.mult, None, [u2k] + CK, [u2k])
                            vtt(Y[:, 0:H2], Y[:, 0:H2], U0[:, 0:H2], ALU.add, [y0, u0k], [y0])
                            vtt(Y[:, 0:H2 - 2], Y[:, 0:H2 - 2], U2[:, 0:H2 - 2], ALU.add, [y0, u2k], [y0])
                        else:
                            act(U0[:, H2 + 1:T], p2[:, 0:H2 - 1], AF.Copy, PK + CK, [u0k], scale=cw(0))
                            act(U2[:, H2 - 1:T - 1], p2, AF.Copy, PK + CK, [u2k], scale=cw(2))
                            vts(U0[:, H2 + 256:T:256], U0[:, H2 + 256:T:256], fl, None, ALU.mult, None, [u0k] + CK, [u0k])
                            vts(U2[:, H2 - 1:T - 1:256], U2[:, H2 - 1:T - 1:256], fl, None, ALU.mult, None, [u2k] + CK, [u2k])
                            vtt(Y[:, H2:T], Y[:, H2:T], U0[:, H2:T], ALU.add, [y1, u0k], [y1])
                            vtt(Y[:, H2 - 2:T], Y[:, H2 - 2:T], U2[:, H2 - 2:T], ALU.add, [y0, y1, u2k], [y0, y1])
                        if gd is not None:
                            down_piece(gd, piece * 4, piece * 4 + 4)
                        piece += 1
                        if piece == 4 and g + 1 < 11:
                            load_up(g + 1)
                YK = [('fy', p_, h_) for p_ in range(2) for h_ in range(2)]
                act(FY[1][:], FY[1][:], AF.Silu, YK, [('fy', 1, 0), ('fy', 1, 1)])
                vtt(FACT[wi][:, jj, :], FY[1][:], FY[0][:], ALU.mult, YK, [('fact', wi)])
            if g + 1 < 11:
                load_dn(g + 1)

        memset(FU0[0][:, 0:1], 0.0, [('fu0', 0)])
        memset(FU0[1][:, 0:1], 0.0, [('fu0', 1)])
        memset(FU2[0][:, T - 1:T], 0.0, [('fu2', 0)])
        memset(FU2[1][:, T - 1:T], 0.0, [('fu2', 1)])
        load_up(0)
        load_dn(0)
        up(0, None)
        for g in range(11):
            if g + 1 < 11:
                up(g + 1, g)
            else:
                down_piece(g, 0, 32)

    ada(0)
    P.barrier()
    for l in range(n_layers):
        layer(l)

    def of(kc, b, t, tk):
        i = (kc + b) % 2
        act(QN[i][:], t[:], AF.Identity, [tk, 'fng'], [('qn', i)], scale=fng[:, kc:kc + 1])
        dma(o_yT[kc * 128:(kc + 1) * 128, bs(b)], QN[i][:], 'st_y', reads=[('qn', i)])
    norm_mod(None, None, None, of)
    dma(o_st[:, :], st[:], 'st_s', reads=['st'])
    P.emit()
    return nc


def _rope_tables(dim, is_sample):
    if not is_sample:
        return np.stack([np.ones((128, T), np.float32), np.zeros((128, T), np.float32)])
    GRID_W = 64
    rows = T // GRID_W
    t_row = np.repeat(np.arange(rows, dtype=np.float32), GRID_W)
    t_col = np.tile(np.arange(GRID_W, dtype=np.float32), rows)
    axis_dim = dim // 2
    inv = (np.float32(10000.0) ** (-np.arange(0, axis_dim, 2, dtype=np.float32) / np.float32(axis_dim))).astype(np.float32)
    ar = t_row[:, None] * inv
    ac = t_col[:, None] * inv
    ang = np.concatenate([ar, ar, ac, ac], axis=-1)
    cos = np.cos(ang).astype(np.float32).T
    sin = np.sin(ang).astype(np.float32).T
    rep = 128 // dim
    return np.stack([np.tile(cos, (rep, 1)), np.tile(sin, (rep, 1))]).astype(np.float32)


def _perm_matrix(dim):
    q = dim // 4
    Pm = np.zeros((128, 128), np.float32)
    for hb in range(0, 128, dim):
        for d in range(dim):
            quarter = d // q
            m = hb + d
            if quarter % 2 == 0:
                Pm[hb + d + q, m] = -1.0
            else:
                Pm[hb + d - q, m] = 1.0
    return Pm


_CACHE = {}


def kernel(**inp):
    f32 = np.float32
    g = {k: np.asarray(v) for k, v in inp.items()}
    L = DEPTH

    def c(a):
        return np.ascontiguousarray(a, dtype=f32)

    def pk8(v):
        return c(v.reshape(8, 128).T)

    w_in = g['w_in']
    colsF = []
    colsF += list(range(0, 256))
    colsF += list(range(256, 512))
    for cch in range(4):
        for h in (cch, cch + 4):
            colsF += list(range(512 + 64 * h, 512 + 64 * h + 64))
    colsF += list(range(1024, 1152))
    colsF += list(range(1280, 1536))
    colsF += list(range(1536, 1792))
    colsV = list(range(1152, 1280)) + list(range(1792, 2048))
    w_inF = c(w_in[:, :, colsF])
    w_inV = c(w_in[:, :, colsV])
    w_all = np.concatenate([w_inF, w_inV], axis=2)
    w_inC = c(w_all.reshape(L, 8, 128, 16, 128).transpose(0, 3, 2, 1, 4).reshape(L, 16, 128, 1024))
    rows_o = list(range(0, 256))
    for cch in range(4):
        for h in (cch, cch + 4):
            rows_o += list(range(256 + 64 * h, 256 + 64 * h + 64))
    rows_o += list(range(768, 1024))
    w_outP = c(g['w_out'][:, rows_o, :])
    up = g['ffn_w_up']
    cols_up = []
    for grp in range(11):
        for part in range(2):
            for jj in range(2):
                j = 2 * grp + jj
                cols_up += list(range(part * D_FF + 128 * j, part * D_FF + 128 * j + 128))
    w_upP = c(up[:, :, cols_up])
    shared = {
        'ada_w': c(g['ada_w']),
        'ada_b': c(g['ada_b'].reshape(L, 48, 128).transpose(0, 2, 1)),
        'n1g': c(g['norm1_g'].reshape(L, 8, 128).transpose(0, 2, 1)),
        'n2g': c(g['norm2_g'].reshape(L, 8, 128).transpose(0, 2, 1)),
        'fng': pk8(g['final_norm_g']),
        'w_inC': w_inC, 'w_outP': w_outP, 'w_upP': w_upP, 'w_dn': c(g['ffn_w_down']),
        'lcw': c(g['lru_conv_w'].reshape(L, 4, 2, 128).transpose(0, 3, 2, 1).reshape(L, 128, 8)),
        'lcb': c(g['lru_conv_b'].reshape(L, 2, 128).transpose(0, 2, 1)),
        'lam': c(g['lru_lambda'].reshape(L, 2, 2, 128).transpose(0, 3, 1, 2).reshape(L, 128, 4)),
        'qkg': c(np.stack([np.tile(g['gqa_q_norm_g'], (1, 2)), np.tile(g['gqa_k_norm_g'], (1, 2))], axis=-1)),
        'dl': c(g['diff_lambda'].reshape(L, 128)),
        'dg': c(g['diff_norm_g']),
        'fcw': c(g['ffn_conv_w'].reshape(L, 3, 44, 128).transpose(0, 3, 2, 1).reshape(L, 128, 132)),
        'fcb': c(g['ffn_conv_b'].reshape(L, 44, 128).transpose(0, 2, 1)),
        'permg': _perm_matrix(64), 'permd': _perm_matrix(32),
        'ident': np.eye(128, dtype=f32),
    }
    gwsrc = g['lru_gate_w']
    gw = np.zeros((L, 8, 128, 128), f32)
    gbsrc = g['lru_gate_b']
    gbv = np.zeros((L, 128, 8), f32)
    for l in range(L):
        for dr in range(2):
            for cc in range(2):
                for ri in range(2):
                    idx = dr * 4 + cc * 2 + ri
                    for bl in range(2):
                        blk = 2 * cc + bl
                        gw[l, idx, bl * 64:(bl + 1) * 64, bl * 64:(bl + 1) * 64] = gwsrc[l, dr, blk, :, ri * 64:(ri + 1) * 64]
                        gbv[l, bl * 64:(bl + 1) * 64, idx] = gbsrc[l, dr, blk, ri * 64:(ri + 1) * 64]
    shared['gw'] = gw
    shared['gb'] = gbv

    mA4 = np.zeros((128, 9 * 128), f32)
    for j in range(9):
        mA4[j, j * 128:(j + 1) * 128] = 1.0

    def mB4_for(is_sample):
        m = np.zeros((128, 4 * BLK), f32)
        if not is_sample:
            qseg = np.arange(T) // 256 + 1
            for j in range(9):
                m[j, :] = np.where(qseg == j, 0.0, NEG)
                m[64 + j, :] = m[j, :]
                m[96 + j, :] = m[j, :]
        return m

    import ml_dtypes
    Ak = np.zeros((9, TK), f32)
    for j in range(9):
        Ak[j, j * 256:(j + 1) * 256] = 1.0
    Ak = Ak.astype(ml_dtypes.bfloat16)

    ropeg_s, roped_s = _rope_tables(64, True), _rope_tables(32, True)
    ropeg_p, roped_p = _rope_tables(64, False), _rope_tables(32, False)
    mB_s, mB_p = mB4_for(True), mB4_for(False)

    xs, xp = g['x_sample'], g['x_prompt']
    in_maps = []
    for core in range(8):
        m = dict(shared)
        m['Ak'] = Ak
        if core < 4:
            b = core
            m['xT'] = c(xs[b].T)
            m['cT'] = pk8(g['c'][b])
            m['flags'] = c(np.tile(np.array([[1.0, 0.0]], f32), (128, 1)))
            m['mB4'] = mB_s
            m['ropeg'], m['roped'] = ropeg_s, roped_s
            m['ckT'] = c(g['cache_gqa_k'][b].reshape(L, 256, 128).transpose(0, 2, 1))
            m['cv'] = c(g['cache_gqa_v'][b].reshape(L, 256, 128))
            m['cdkT'] = c(g['cache_diff_k'][b].reshape(L, 256, 256).transpose(0, 2, 1))
            m['cdv'] = c(g['cache_diff_v'][b].reshape(L, 256, 256))
            m['h0'] = c(g['state_lru'][b].reshape(L, 2, 2, 128).transpose(0, 3, 1, 2).reshape(L, 128, 4))
        else:
            i = (core - 4) % 2
            m['xT'] = c(xp[8 * i:8 * i + 8].reshape(T, D_MODEL).T)
            m['cT'] = pk8(g['c_ctx'])
            m['flags'] = c(np.tile(np.array([[0.0, 1.0]], f32), (128, 1)))
            m['mB4'] = mB_p
            m['ropeg'], m['roped'] = ropeg_p, roped_p
            m['ckT'] = np.zeros((L, 128, 256), f32)
            m['cv'] = np.zeros((L, 256, 128), f32)
            m['cdkT'] = np.zeros((L, 256, 256), f32)
            m['cdv'] = np.zeros((L, 256, 256), f32)
            m['h0'] = np.zeros((L, 128, 4), f32)
        in_maps.append(m)

    if 'nc' not in _CACHE:
        _CACHE['nc'] = build_program()
    nc = _CACHE['nc']
    res = run_bass_kernel_spmd(nc, in_maps, core_ids=list(range(8)))
    R = res.results

    y_sample = np.stack([R[b]['yT'].T for b in range(4)]).astype(f32)
    yp, ks, vs, dks, dvs, sts = [], [], [], [], [], []
    for i in range(2):
        r = R[4 + i]
        yT = r['yT']
        for s in range(8):
            sl = slice(256 * s, 256 * s + 256)
            yp.append(yT[:, sl].T)
            ks.append(np.stack([r['okT'][l][:, sl].T.reshape(256, 2, 64) for l in range(L)]))
            vs.append(np.stack([r['ov'][l][sl, :].reshape(256, 2, 64) for l in range(L)]))
            dks.append(np.stack([r['odkT'][l][:, sl].T.reshape(256, 4, 2, 32) for l in range(L)]))
            dvs.append(np.stack([r['odv'][l][sl, :].reshape(256, 4, 64) for l in range(L)]))
            o = r['ost'].reshape(128, L, 2, 2, 8)[:, :, :, :, s]
            sts.append(o.transpose(1, 2, 3, 0).reshape(L, 2, 256))
    out = (np.stack(yp).astype(f32), y_sample, np.stack(ks).astype(f32), np.stack(vs).astype(f32),
           np.stack(dks).astype(f32), np.stack(dvs).astype(f32), np.stack(sts).astype(f32))
    return tuple(np.ascontiguousarray(o) for o in out)
```

```python
import contextlib
import numpy as np
import concourse.bass as bass
import concourse.mybir as mybir
from concourse.bass_utils import run_bass_kernel_spmd

F32 = mybir.dt.float32
BF16 = mybir.dt.bfloat16
ALU = mybir.AluOpType
AF = mybir.ActivationFunctionType

D_MODEL = 1024
T = 2048
BLK = 512
NB = 4
NT = 16
KT = 18
TK = 2304
DEPTH = 2
D_FF = 2816
EPS = 1e-6
NEG = -30000.0


class Prog:
    def __init__(self, nc):
        self.nc = nc
        self.E = {'pe': nc.tensor, 'act': nc.scalar, 'dve': nc.vector, 'pool': nc.gpsimd, 'sp': nc.sync}
        self.ops = []

    def op(self, eng, fn, reads=(), writes=(), dma=None):
        self.ops.append((eng, fn, tuple(reads), tuple(writes), dma))

    def barrier(self):
        self.ops.append(None)

    def emit(self, final_eng='sp'):
        nc = self.nc
        ops = self.ops
        n = len(ops)
        last_w = {}
        readers = {}
        deps = [None] * n
        marked = [False] * n
        last_eng = {}
        pend_dma = []
        bar_set = None
        bar_done = set()
        for i, rec in enumerate(ops):
            if rec is None:
                s = set(last_eng.values()) | set(pend_dma)
                if bar_set is not None:
                    s |= bar_set
                bar_set = s
                bar_done = set()
                pend_dma = []
                continue
            eng, fn, rd, wr, dma = rec
            d = set()
            if bar_set is not None and eng not in bar_done:
                d |= bar_set
                bar_done.add(eng)
            for t in rd:
                j = last_w.get(t)
                if j is not None:
                    d.add(j)
            for t in wr:
                j = last_w.get(t)
                if j is not None:
                    d.add(j)
                r = readers.get(t)
                if r:
                    d.update(r[0].values())
                    d.update(r[1])
            d.discard(i)
            dd = []
            for j in d:
                ej, _, _, _, dmaj = ops[j]
                if dmaj is None and dma is None and ej == 'pe' and eng == 'pe':
                    continue
                dd.append(j)
                marked[j] = True
            deps[i] = dd
            for t in rd:
                r = readers.get(t)
                if r is None:
                    r = readers[t] = ({}, [])
                if dma is None:
                    r[0][eng] = i
                else:
                    r[1].append(i)
            for t in wr:
                last_w[t] = i
                readers[t] = ({}, [])
            if dma is None:
                last_eng[eng] = i
            else:
                pend_dma.append(i)
        cnt = {}
        mile = [None] * n
        for i, rec in enumerate(ops):
            if rec is None:
                continue
            eng, fn, rd, wr, dma = rec
            if dma is not None:
                cnt[dma] = cnt.get(dma, 0) + 16
                mile[i] = (dma, cnt[dma])
            elif marked[i]:
                k = 'E_' + eng
                cnt[k] = cnt.get(k, 0) + 1
                mile[i] = (k, cnt[k])
        sems = {}
        with contextlib.ExitStack() as st:
            for k in cnt:
                sems[k] = st.enter_context(nc.semaphore(k))
            waited = {e: {} for e in self.E}
            for i, rec in enumerate(ops):
                if rec is None:
                    continue
                eng, fn, rd, wr, dma = rec
                need = {}
                for j in deps[i]:
                    s, v = mile[j]
                    if need.get(s, 0) < v:
                        need[s] = v
                for s, v in need.items():
                    if waited[eng].get(s, 0) < v:
                        self.E[eng].wait_ge(sems[s], v)
                        waited[eng][s] = v
                ins = fn()
                if mile[i] is not None:
                    s, v = mile[i]
                    ins.then_inc(sems[s], 16 if dma is not None else 1)
            for k, v in cnt.items():
                if waited[final_eng].get(k, 0) < v:
                    self.E[final_eng].wait_ge(sems[k], v)
        return cnt


def build_program(n_layers=DEPTH, debug=False):
    nc = bass.Bass("TRN2", target_bir_lowering=False)
    P = Prog(nc)

    def din(name, shape):
        return nc.dram_tensor(name, list(shape), F32, kind="ExternalInput").ap()

    def dout(name, shape):
        return nc.dram_tensor(name, list(shape), F32, kind="ExternalOutput").ap()

    d_xT = din("xT", [D_MODEL, T])
    d_cT = din("cT", [128, 8])
    d_flags = din("flags", [128, 2])
    d_mB = din("mB4", [128, 4 * BLK])
    d_Ak = nc.dram_tensor("Ak", [9, TK], BF16, kind="ExternalInput").ap()
    d_ropeg = din("ropeg", [2, 128, T])
    d_roped = din("roped", [2, 128, T])
    d_permg = din("permg", [128, 128])
    d_permd = din("permd", [128, 128])
    d_ckT = din("ckT", [DEPTH, 128, 256])
    d_cv = din("cv", [DEPTH, 256, 128])
    d_cdkT = din("cdkT", [DEPTH, 256, 256])
    d_cdv = din("cdv", [DEPTH, 256, 256])
    d_h0 = din("h0", [DEPTH, 128, 4])
    d_n1 = din("n1g", [DEPTH, 128, 8])
    d_n2 = din("n2g", [DEPTH, 128, 8])
    d_fg = din("fng", [128, 8])
    d_adaw = din("ada_w", [DEPTH, D_MODEL, 6 * D_MODEL])
    d_adab = din("ada_b", [DEPTH, 128, 48])
    d_winC = din("w_inC", [DEPTH, 16, 128, 8 * 128])
    d_wout = din("w_outP", [DEPTH, D_MODEL, D_MODEL])
    d_lcw = din("lcw", [DEPTH, 128, 8])
    d_lcb = din("lcb", [DEPTH, 128, 2])
    d_gw = din("gw", [DEPTH, 8, 128, 128])
    d_gb = din("gb", [DEPTH, 128, 8])
    d_lam = din("lam", [DEPTH, 128, 4])
    d_qkg = din("qkg", [DEPTH, 128, 2])
    d_dl = din("dl", [DEPTH, 128])
    d_dg = din("dg", [DEPTH, 64])
    d_wup = din("w_upP", [DEPTH, D_MODEL, 11 * 512])
    d_fcw = din("fcw", [DEPTH, 128, 44 * 3])
    d_fcb = din("fcb", [DEPTH, 128, 44])
    d_wdn = din("w_dn", [DEPTH, D_FF, D_MODEL])

    o_yT = dout("yT", [D_MODEL, T])
    o_kT = dout("okT", [DEPTH, 128, T])
    o_v = dout("ov", [DEPTH, T, 128])
    o_dkT = dout("odkT", [DEPTH, 256, T])
    o_dv = dout("odv", [DEPTH, T, 256])
    o_st = dout("ost", [128, 64])

    base0 = (nc.sbuf_base + 31) // 32 * 32
    ARENA = 212800
    arena = nc.alloc_sbuf_tensor("arena", [128, ARENA], mybir.dt.uint8)
    cnt_name = [0]

    def at(off, shape, dt):
        cnt_name[0] += 1
        assert off % 32 == 0 and off >= 0, off
        nbytes = int(np.prod(shape[1:])) * (4 if dt == F32 else 2)
        assert off + nbytes <= ARENA, (off, nbytes)
        return nc.alloc_sbuf_tensor_at("t%d" % cnt_name[0], list(shape), dt, offset=base0 + off)

    OFF_X, OFF_ACT, OFF_C, OFF_XR = 0, 65536, 98304, 115456
    xT = at(OFF_X, [128, 8, T], F32)
    actT = at(OFF_ACT, [128, 8, T], BF16)

    cptr = [OFF_C]

    def calloc(shape, dt):
        nbytes = int(np.prod(shape[1:])) * (4 if dt == F32 else 2)
        nbytes = (nbytes + 31) // 32 * 32
        t = at(cptr[0], shape, dt)
        cptr[0] += nbytes
        assert cptr[0] <= OFF_XR
        return t

    ident = calloc([128, 128], BF16)
    ones = calloc([128, 128], BF16)
    bones = calloc([128, 128], BF16)
    permg = calloc([128, 128], BF16)
    permd = calloc([128, 128], BF16)
    gwb = calloc([128, 8, 128], BF16)
    mBb = calloc([128, 4, BLK], BF16)
    mod = calloc([128, 2, 48], F32)
    adab = calloc([128, 2, 48], F32)
    n1g = calloc([128, 2, 8], F32)
    n2g = calloc([128, 2, 8], F32)
    fng = calloc([128, 8], F32)
    gs1 = calloc([128, 2, 8], F32)
    gs2 = calloc([128, 2, 8], F32)
    cT = calloc([128, 8], F32)
    siluc = calloc([128, 8], F32)
    flags = calloc([128, 2], F32)
    lcw = calloc([128, 2, 8], F32)
    lcwn = calloc([128, 2, 8], F32)
    lcb = calloc([128, 2, 2], F32)
    gb = calloc([128, 2, 8], F32)
    lam = calloc([128, 2, 4], F32)
    lsc = calloc([128, 2, 4], F32)
    lsc2 = calloc([128, 2, 4], F32)
    h0 = calloc([128, 2, 4], F32)
    qkg = calloc([128, 2, 2], F32)
    dlt = calloc([128, 64], F32)
    dle = calloc([128, 4], F32)
    nlam = calloc([128, 2], F32)
    dg = calloc([128, 2, 64], F32)
    fcw = calloc([128, 2, 44 * 3], F32)
    fcwn = calloc([128, 2, 44 * 3], F32)
    fcb = calloc([128, 2, 44], F32)
    st = calloc([128, 64], F32)
    sml = calloc([128, 16], F32)
    sstage = calloc([128, 128], F32)

    X = OFF_XR
    LRUOUT = at(X + 0, [128, 2, T], BF16)
    QT = at(X + 8192, [128, 4, T], BF16)
    DQT = at(X + 24576, [128, 2, T], BF16)
    KZ = [at(X + 32768 + 4608 * i, [128, TK], BF16) for i in range(2)]
    DKZ = [at(X + 41984 + 4608 * i, [128, TK], BF16) for i in range(3)]
    VG = at(X + 55808, [128, KT, 2, 66], BF16)
    VD = at(X + 60576, [128, KT, 4, 66], BF16)
    WS = [at(X + 70080 + 4096 * i, [128, 8, 128], F32) for i in range(2)]
    WB = [at(X + 78272 + 2048 * i, [128, 8, 128], BF16) for i in range(2)]
    PT0 = X + 82368
    SQ = [at(PT0 + 1024 * i, [128, BLK], BF16) for i in range(2)]
    RT = [at(PT0 + 2048, [128, BLK], F32) for i in range(2)]
    QN = [at(PT0 + 4096 + 2048 * i, [128, BLK], F32) for i in range(2)]
    T1 = [at(PT0 + 8192 + 2048 * i, [128, BLK], F32) for i in range(2)]
    VST = [at(PT0 + 12288, [128, 4, 128], F32), at(PT0 + 4096 + 2048, [128, 4, 128], F32)]
    ROPET = at(X + 70080, [128, 4, BLK], F32)
    T2 = [at(X + 78272 + 2048 * i, [128, BLK], F32) for i in range(2)]
    LRUX = at(X + 8192, [128, 2, T], BF16)
    LRUG = at(X + 16384, [128, 2, T], BF16)
    L_R = [at(X + 24576 + 24576 * d, [128, T], F32) for d in range(2)]
    L_I = [at(X + 32768 + 24576 * d, [128, T], F32) for d in range(2)]
    L_T = [at(X + 40960 + 24576 * d, [128, T], F32) for d in range(2)]
    L_XC = at(X + 73728, [128, T], F32)
    L_XCB = at(X + 81920, [128, T], BF16)
    PTt = [at(X + 70080 + 2048 * i, [128, 2 * BLK], BF16) for i in range(2)]
    OTOK = [at(X + 74176 + 1024 * i, [128, 4, 128], BF16) for i in range(2)]
    DD1 = at(X + 76224, [128, 4, 64], F32)
    DD2 = at(X + 77248, [128, 4, 64], F32)
    QZ = [at(X + 78272 + 1024 * i, [128, BLK], BF16) for i in range(2)]
    ADAS = [at(X + 16384 * i, [128, 8, BLK], F32) for i in range(2)]
    WOS = [at(X + 8192 + 16384 * i, [128, 8, BLK], F32) for i in range(2)]
    WOB = [at(X + 40960 + 8192 * i, [128, 8, BLK], BF16) for i in range(2)]
    FWS = at(X + 0, [128, 8, BLK], F32)
    FWB = [at(X + 16384 + 8192 * i, [128, 8, BLK], BF16) for i in range(2)]
    FDS = at(X + 32768, [128, 2, D_MODEL], F32)
    FDB = [at(X + 40960 + 4096 * i, [128, 2, D_MODEL], BF16) for i in range(2)]
    FY = [at(X + 49152 + 4096 * i, [128, T], BF16) for i in range(2)]
    FU0 = [at(X + 57344 + 4096 * i, [128, T], BF16) for i in range(2)]
    FU2 = [at(X + 65536 + 4096 * i, [128, T], BF16) for i in range(2)]
    FACT = [at(X + 73728 + 8192 * i, [128, 2, T], BF16) for i in range(2)]
    FOUT = [at(X + 90112 + 2048 * i, [128, BLK], F32) for i in range(2)]

    ps = nc.alloc_psum_tensor("ps", [128, 8, BLK], F32)

    def dma(out, in_, key, reads=(), writes=(), eng='sp'):
        P.op(eng, lambda: P.E[eng].dma_start(out=out, in_=in_), reads, writes, dma=key)

    def mm(out, lhsT, rhs, start, stop, reads, writes, tp=None):
        if tp is None:
            P.op('pe', lambda: nc.tensor.matmul(out, lhsT=lhsT, rhs=rhs, start=start, stop=stop), reads, writes)
        else:
            P.op('pe', lambda: nc.tensor.matmul(out, lhsT=lhsT, rhs=rhs, start=start, stop=stop, tile_position=tp),
                 reads, writes)

    def act(out, in_, func, reads, writes, bias=None, scale=None, accum=None):
        kw = {}
        if bias is not None:
            kw['bias'] = bias
        if scale is not None:
            kw['scale'] = scale
        if accum is not None:
            kw['accum_out'] = accum
        P.op('act', lambda: nc.scalar.activation(out=out, in_=in_, func=func, **kw), reads, writes)

    def vtt(out, in0, in1, op, reads, writes, eng='dve'):
        P.op(eng, lambda: P.E[eng].tensor_tensor(out=out, in0=in0, in1=in1, op=op), reads, writes)

    def vts(out, in0, s1, s2, op0, op1, reads, writes, eng='dve'):
        if op1 is None:
            P.op(eng, lambda: P.E[eng].tensor_scalar(out=out, in0=in0, scalar1=s1, scalar2=None, op0=op0), reads, writes)
        else:
            P.op(eng, lambda: P.E[eng].tensor_scalar(out=out, in0=in0, scalar1=s1, scalar2=s2, op0=op0, op1=op1),
                 reads, writes)

    def vstt(out, in0, scalar, in1, op0, op1, reads, writes):
        P.op('dve', lambda: nc.vector.scalar_tensor_tensor(out=out, in0=in0, scalar=scalar, in1=in1, op0=op0, op1=op1),
             reads, writes)

    def vcopy(out, in_, reads, writes, eng='dve'):
        P.op(eng, lambda: P.E[eng].tensor_copy(out=out, in_=in_), reads, writes)

    def vrecip(out, in_, reads, writes):
        P.op('dve', lambda: nc.vector.reciprocal(out=out, in_=in_), reads, writes)

    def memset(ap, val, writes, eng='dve'):
        P.op(eng, lambda: P.E[eng].memset(ap, val), (), writes)

    def bs(b):
        return slice(b * BLK, (b + 1) * BLK)

    for kc in range(8):
        dma(xT[:, kc, :], d_xT[kc * 128:(kc + 1) * 128, :], 'ld_x', writes=[('x', kc, b) for b in range(NB)])
    small_loads = [
        (cT[:], d_cT[:, :], 'cT'), (flags[:], d_flags[:, :], 'flags'),
        (fng[:], d_fg[:, :], 'fng'),
    ]
    for l in range(DEPTH):
        small_loads += [
            (adab[:, l, :], d_adab[l], 'adab'), (n1g[:, l, :], d_n1[l], 'n1g'), (n2g[:, l, :], d_n2[l], 'n2g'),
            (lcw[:, l, :], d_lcw[l], 'lcw'), (lcb[:, l, :], d_lcb[l], 'lcb'), (gb[:, l, :], d_gb[l], 'gb'),
            (lam[:, l, :], d_lam[l], 'lam'), (h0[:, l, :], d_h0[l], 'h0'), (qkg[:, l, :], d_qkg[l], 'qkg'),
            (dg[:, l, :], d_dg[l].partition_broadcast(128), 'dg'),
            (fcw[:, l, :], d_fcw[l], 'fcw'), (fcb[:, l, :], d_fcb[l], 'fcb'),
        ]
    for (o, i, k) in small_loads:
        dma(o, i, 'ld_c', writes=[k])
    CK = ['cT', 'flags', 'fng', 'adab', 'n1g', 'n2g', 'lcw', 'lcb', 'gb', 'lam', 'h0', 'qkg', 'dg', 'fcw', 'fcb']
    memset(ones[:], 1.0, ['ones'])
    memset(bones[:], 0.0, ['bones'])
    memset(bones[0:64, 0:64], 1.0, ['bones'])
    memset(bones[64:128, 64:128], 1.0, ['bones'])
    for (dst, src, k) in ((permg, d_permg, 'permg'), (permd, d_permd, 'permd')):
        dma(sstage[:], src[:, :], 'ld_c2', writes=['sstage'])
        vcopy(dst[:], sstage[:], ['sstage'], [k])
    d_ident = din("ident", [128, 128])
    dma(sstage[:], d_ident[:, :], 'ld_c2', writes=['sstage'])
    vcopy(ident[:], sstage[:], ['sstage'], ['ident'])
    dma(ADAS[1][:, 0:4, :].rearrange("p a b -> p (a b)"), d_mB[:, :], 'ld_c2', writes=['adas1'])
    vcopy(mBb[:], ADAS[1][:, 0:4, :], ['adas1'], ['mB'])
    P.barrier()
    act(siluc[:], cT[:], AF.Silu, CK, ['siluc'])
    act(lsc[:], lam[:], AF.Exp, CK, ['lsc'], scale=-1.0)
    act(lsc[:], lsc[:], AF.Ln, ['lsc'], ['lsc'], bias=1.0)
    vts(lsc2[:], lsc[:], -16.0, None, ALU.mult, None, ['lsc'], ['lsc2'])
    vts(lsc[:], lsc[:], -8.0, None, ALU.mult, None, ['lsc', 'lsc2'], ['lsc'])
    vts(lcwn[:], lcw[:], flags[:, 1:2], -1.0, ALU.mult, ALU.mult, CK, ['lcwn'])
    vts(fcwn[:], fcw[:], flags[:, 1:2], -1.0, ALU.mult, ALU.mult, CK, ['fcwn'])
    import math
    lam_init = [0.8 - 0.6 * math.exp(-0.3 * l) for l in range(DEPTH)]
    for l in range(DEPTH):
        dma(sstage[:], d_dl[l].partition_broadcast(128), 'ld_c2', writes=['sstage'])
        for k in range(2):
            vtt(dlt[:, 0:32], sstage[:, 64 * k:64 * k + 32], sstage[:, 64 * k + 32:64 * k + 64], ALU.mult, ['sstage'], ['dlt'])
            P.op('dve', lambda k=k, l=l: nc.vector.reduce_sum(out=dle[:, 2 * l + k:2 * l + k + 1], in_=dlt[:, 0:32],
                                                            axis=mybir.AxisListType.X), ['dlt'], ['dle'])
        act(dle[:, 2 * l:2 * l + 2], dle[:, 2 * l:2 * l + 2], AF.Exp, ['dle'], ['dle'])
        vts(sml[:, 0:1], dle[:, 2 * l + 1:2 * l + 2], -lam_init[l], None, ALU.add, None, ['dle'], ['sml'])
        vtt(nlam[:, l:l + 1], sml[:, 0:1], dle[:, 2 * l:2 * l + 1], ALU.subtract, ['sml', 'dle'], ['nlam'])
        vts(dg[:, l, :], dg[:, l, :], 1.0 - lam_init[l], None, ALU.mult, None, CK, ['dg'])
    P.barrier()

    ADS = [at(X + 16384 * i, [128, 8, BLK], F32) for i in range(3)]
    ADB = [at(X + 49152 + 8192 * i, [128, 8, BLK], BF16) for i in range(2)]
    ROW = at(X + 65536, [1, 6 * D_MODEL], F32)
    onef = calloc([128, 2], F32)
    mhalf = calloc([128, 4], F32)
    memset(mhalf[:], -0.5, ['mhalf'])
    silucb = calloc([128, 8], BF16)
    memset(onef[:], 1.0, ['onef'])
    vcopy(silucb[:], siluc[:], ['siluc'], ['silucb'])
    adac = [0]

    def ada(l):
        for g in range(12):
            n = adac[0]
            adac[0] += 1
            s3 = n % 3
            s = n % 2
            for hf in range(2):
                dma(ADS[s3][:, 4 * hf:4 * hf + 4, :],
                    d_adaw[l][hf * 512:(hf + 1) * 512, g * 512:(g + 1) * 512].rearrange("(kc p) n -> p kc n", p=128),
                    'ld_ada%d' % s3, writes=[('ads', s3)], eng=('sp' if hf == 0 else 'pool'))
            if n % 2 == 0:
                act(ADB[s][:], ADS[s3][:], AF.Copy, [('ads', s3)], [('adb', s)])
            else:
                vcopy(ADB[s][:], ADS[s3][:], [('ads', s3)], [('adb', s)])
            bank = g % 4
            for kc in range(8):
                mm(ps[0:1, bank, :], silucb[:, kc:kc + 1], ADB[s][:, kc, :], kc == 0, kc == 7,
                   [('adb', s), 'silucb'], [('ps', bank)])
            vcopy(ROW[0:1, g * 512:(g + 1) * 512], ps[0:1, bank, :], [('ps', bank)], ['row'])
        for col in range(48):
            mm(ps[:, 7, col:col + 1], ROW[0:1, col * 128:(col + 1) * 128], onef[0:1, 0:1], True, True,
               ['row', 'onef'], [('ps', 7)])
        vtt(mod[:, l, :], ps[:, 7, 0:48], adab[:, l, :], ALU.add, [('ps', 7)], [('mod', l)])
        vstt(gs1[:, l, :], mod[:, l, 8:16], 1.0, n1g[:, l, :], ALU.add, ALU.mult, [('mod', l)], [('gs', l)])
        vstt(gs2[:, l, :], mod[:, l, 32:40], 1.0, n2g[:, l, :], ALU.add, ALU.mult, [('mod', l)], [('gs', l)])

    ADS2 = at(X + 80320, [128, 8, 256], F32)
    ADB2 = at(X + 88512, [128, 8, 256], BF16)
    ROW2 = at(X + 92608, [1, 256], F32)
    ada1_state = [0]

    def ada1_scatter(g):
        for j in range(2):
            col = 256 + 2 * g + j
            mm(ps[:, 7, col:col + 1], ROW2[0:1, j * 128:(j + 1) * 128], onef[0:1, 0:1], True, True,
               ['row2', 'onef'], [('ps', 7)])

    def ada1_step():
        g = ada1_state[0]
        if g >= 24:
            return
        ada1_state[0] += 1
        if g > 0:
            ada1_scatter(g - 1)
        for hf in range(2):
            dma(ADS2[:, 4 * hf:4 * hf + 4, :],
                d_adaw[1][hf * 512:(hf + 1) * 512, g * 256:(g + 1) * 256].rearrange("(kc p) n -> p kc n", p=128),
                'ld_ada2', writes=['ads2'])
        vcopy(ADB2[:], ADS2[:], ['ads2'], ['adb2'])
        for kc in range(8):
            mm(ps[0:1, 7, 0:256], silucb[:, kc:kc + 1], ADB2[:, kc, :], kc == 0, kc == 7, ['adb2', 'silucb'], [('ps', 7)])
        vcopy(ROW2[0:1, :], ps[0:1, 7, 0:256], [('ps', 7)], ['row2'])

    def ada1_finish():
        while ada1_state[0] < 24:
            ada1_step()
        ada1_scatter(23)
        vtt(mod[:, 1, :], ps[:, 7, 256:304], adab[:, 1, :], ALU.add, [('ps', 7)], [('mod', 1)])
        vstt(gs1[:, 1, :], mod[:, 1, 8:16], 1.0, n1g[:, 1, :], ALU.add, ALU.mult, [('mod', 1)], [('gs', 1)])
        vstt(gs2[:, 1, :], mod[:, 1, 32:40], 1.0, n2g[:, 1, :], ALU.add, ALU.mult, [('mod', 1)], [('gs', 1)])

    def norm_mod(gs_ap, sh_ap, rkeys, out_fn):
        for b in range(NB):
            for kc in range(8):
                if kc % 2 == 0:
                    act(SQ[kc % 2][:], xT[:, kc, bs(b)], AF.Square, [('x', kc, b)], [('sq', kc % 2)])
                else:
                    vtt(SQ[kc % 2][:], xT[:, kc, bs(b)], xT[:, kc, bs(b)], ALU.mult, [('x', kc, b)], [('sq', kc % 2)])
                mm(ps[:, b % 2, :], ones[:], SQ[kc % 2][:], kc == 0, kc == 7, [('sq', kc % 2), 'ones'], [('ps', b % 2)])
            act(RT[b % 2][:], ps[:, b % 2, :], AF.Ln, [('ps', b % 2)], [('rt', 0)], bias=EPS, scale=1.0 / D_MODEL)
            act(RT[b % 2][:], RT[b % 2][:], AF.Exp, [('rt', 0)], [('rt', 0)], scale=-0.5)
            for kc in range(8):
                vtt(T1[kc % 2][:], xT[:, kc, bs(b)], RT[b % 2][:], ALU.mult, [('x', kc, b), ('rt', 0)], [('t1', kc % 2)])
                out_fn(kc, b, T1[kc % 2], ('t1', kc % 2))

    wcount = [0]

    def load_w128(src_ap):
        i = wcount[0] % 2
        wcount[0] += 1
        dma(WS[i][:].rearrange("p kc n -> p (kc n)"), src_ap, 'ld_ws%d' % i, writes=[('ws', i)])
        vcopy(WB[i][:], WS[i][:], [('ws', i)], [('wb', i)], eng='pool')
        return i

    def layer(l):
        sh1, g1 = mod[:, l, 0:8], mod[:, l, 16:24]
        sh2, g2 = mod[:, l, 24:32], mod[:, l, 40:48]
        MK = [('mod', l), ('gs', l)]

        def o1(kc, b, t, tk):
            act(actT[:, kc, bs(b)], t[:], AF.Identity, [tk] + MK, [('h', kc, b)],
                bias=sh1[:, kc:kc + 1], scale=gs1[:, l, kc:kc + 1])
        norm_mod(None, None, None, o1)

        for i in range(8):
            dma(sstage[:], d_gw[l, i], 'ld_gw', writes=['sstage'], eng='pool')
            vcopy(gwb[:, i, :], sstage[:], ['sstage'], ['gwb'], eng='pool')

        pbank = [0]

        def proj_chunk(ci, post):
            wi = load_w128(d_winC[l, ci])
            pend = None
            for b in range(NB):
                pb = pbank[0] % 6
                pbank[0] += 1
                for kc in range(8):
                    mm(ps[:, pb, :], WB[wi][:, kc, :], actT[:, kc, bs(b)], kc == 0, kc == 7,
                       [('wb', wi), ('h', kc, b)], [('ps', pb)])
                nxt = post(b, ps[:, pb, :], ('ps', pb))
                if pend is not None:
                    pend()
                pend = nxt
            if pend is not None:
                pend()

        for cc in range(2):
            def post_lx(b, pt, pk, cc=cc):
                act(LRUX[:, cc, bs(b)], pt, AF.Copy, [pk], [('lrux', cc)])
            proj_chunk(cc, post_lx)
        for cc in range(2):
            def post_lg(b, pt, pk, cc=cc):
                i = b % 2
                act(QN[i][:], pt, AF.Square, [pk], [('qn', i)])
                vts(QN[i][:], QN[i][:], 0.044715, 1.0, ALU.mult, ALU.add, [('qn', i)], [('qn', i)])
                vtt(QN[i][:], QN[i][:], pt, ALU.mult, [('qn', i), pk], [('qn', i)])
                act(QN[i][:], QN[i][:], AF.Sigmoid, [('qn', i)], [('qn', i)], scale=1.5957691216057308)
                vtt(LRUG[:, cc, bs(b)], QN[i][:], pt, ALU.mult, [('qn', i), pk], [('lrug', cc)])
            proj_chunk(2 + cc, post_lg)

        lru_phase(l)

        P.barrier()
        ctx_load(l)
        memset(VG[:, :, :, 64:65], 1.0, ['vg'], eng='pool')
        memset(VD[:, :, :, 64:65], 1.0, ['vd'], eng='pool')

        auxc = [0]

        def post_qk(dst_fn, gcol, out_dram=None):
            def post(b, pt, pk):
                i = auxc[0] % 2
                auxc[0] += 1
                act(SQ[i][:], pt, AF.Square, [pk], [('sq', i)])

                def cont():
                    ab = 6 + i
                    aux, auxk = ps[:, ab, :], ('ps', ab)
                    mm(aux, bones[:], SQ[i][:], True, True, [('sq', i), 'bones'], [auxk])
                    act(RT[i][:], aux, AF.Ln, [auxk], [('rt', 0)], bias=EPS, scale=1.0 / 64)
                    act(RT[i][:], RT[i][:], AF.Exp, [('rt', 0)], [('rt', 0)], scale=-0.5)
                    if out_dram is None:
                        vstt(dst_fn(b), pt, qkg[:, l, gcol:gcol + 1], RT[i][:], ALU.mult, ALU.mult, [pk, ('rt', 0)] + CK,
                             [('qkv',)])
                    else:
                        vstt(QN[i][:], pt, qkg[:, l, gcol:gcol + 1], RT[i][:], ALU.mult, ALU.mult, [pk, ('rt', 0)] + CK,
                             [('qn', i)])
                        for kv in range(2):
                            vcopy(KZ[kv][0:64, 256 + b * BLK:256 + (b + 1) * BLK], QN[i][kv * 64:(kv + 1) * 64, :],
                                  [('qn', i)], [('qkv',)])
                        dma(out_dram(b), QN[i][:], 'st_k', reads=[('qn', i)], eng='act')
                return cont
            return post

        for c in range(4):
            proj_chunk(4 + c, post_qk(lambda b, c=c: QT[:, c, bs(b)], 0))
        proj_chunk(8, post_qk(None, 1,
                              out_dram=lambda b: o_kT[l][:, bs(b)]))
        for c in range(2):
            def post_dq(b, pt, pk, c=c):
                act(DQT[:, c, bs(b)], pt, AF.Copy, [pk], [('qkv',)])
            proj_chunk(9 + c, post_dq)
        for c in range(2):
            def post_dk(b, pt, pk, c=c):
                i = b % 2
                act(QN[i][:], pt, AF.Copy, [pk], [('qn', i)])
                for r in range(4):
                    n = 4 * c + r
                    vcopy(DKZ[n // 3][(n % 3) * 32:(n % 3) * 32 + 32, 256 + b * BLK:256 + (b + 1) * BLK],
                          QN[i][r * 32:(r + 1) * 32, :], [('qn', i)], [('qkv',)])
                dma(o_dkT[l][c * 128:(c + 1) * 128, bs(b)], QN[i][:], 'st_dk', reads=[('qn', i)], eng='act')
            proj_chunk(11 + c, post_dk)
        for vc in range(3):
            wi = load_w128(d_winC[l, 13 + vc])
            for tg in range(4):
                pbk = 4 * (vc % 2) + tg
                for tt in range(4):
                    tok = tg * 4 + tt
                    for kc in range(8):
                        mm(ps[:, pbk, tt * 128:(tt + 1) * 128], actT[:, kc, tok * 128:(tok + 1) * 128], WB[wi][:, kc, :],
                           kc == 0, kc == 7, [('wb', wi)] + [('h', kc, tok // 4)], [('ps', pbk)])
                i = tg % 2
                vk = ('vst', 0) if i == 0 else ('qn', 1)
                act(VST[i][:], ps[:, pbk, :].rearrange("p (a b) -> p a b", a=4), AF.Copy, [('ps', pbk)], [vk])
                if vc == 0:
                    vcopy(VG[:, 2 + tg * 4:2 + tg * 4 + 4, :, 0:64],
                          VST[i][:].rearrange("p a (h d) -> p a h d", h=2), [vk], [('qkv',)])
                    dma(o_v[l][tg * 512:(tg + 1) * 512, :].rearrange("(a p) n -> p a n", p=128), VST[i][:], 'st_v',
                        reads=[vk], eng='act')
                else:
                    hb = 2 * (vc - 1)
                    vcopy(VD[:, 2 + tg * 4:2 + tg * 4 + 4, hb:hb + 2, 0:64],
                          VST[i][:].rearrange("p a (h d) -> p a h d", h=2), [vk], [('qkv',)])
                    dma(o_dv[l][tg * 512:(tg + 1) * 512, hb * 64:(hb + 2) * 64].rearrange("(a p) n -> p a n", p=128),
                        VST[i][:], 'st_v', reads=[vk], eng='act')
        P.barrier()
        for b in range(NB):
            dma(ROPET[:, 0:2, :], d_ropeg[:, :, bs(b)].rearrange("a p n -> p a n"), 'ld_rope', writes=['ropet'])
            dma(ROPET[:, 2:4, :], d_roped[:, :, bs(b)].rearrange("a p n -> p a n"), 'ld_rope', writes=['ropet'])
            items = [(QT[:, c, bs(b)], permg, 0, 128) for c in range(4)]
            items += [(KZ[kv][0:64, 256 + b * BLK:256 + (b + 1) * BLK], permg, 0, 64) for kv in range(2)]
            items += [(DQT[:, c, bs(b)], permd, 2, 128) for c in range(2)]
            items += [(DKZ[t][:, 256 + b * BLK:256 + (b + 1) * BLK], permd, 2, 128) for t in range(3)]
            for n, (ap_, pm, to, np_) in enumerate(items):
                i = n % 2
                mm(ps[0:np_, i, :], pm[0:np_, 0:np_], ap_, True, True, [('rp', b, n)], [('ps', i)])
                vtt(T1[i][0:np_, :], ap_, ROPET[0:np_, to, :], ALU.mult, [('rp', b, n), 'ropet'], [('t1', i)])
                vtt(T2[i][0:np_, :], ps[0:np_, i, :], ROPET[0:np_, to + 1, :], ALU.mult, [('ps', i), 'ropet'], [('t2', i)])
                vtt(ap_, T1[i][0:np_, :], T2[i][0:np_, :], ALU.add, [('t1', i), ('t2', i)], [('rp', b, n)])
        P.barrier()
        for t in range(3):
            dma(DKZ[t][96:105, :], d_Ak[:, :], 'ld_ctx', writes=[('qkv',)])

        attention(l)
        if l == 0 and DEPTH > 1:
            ada1_finish()
        P.barrier()

        for og in range(2):
            dma(WOS[og][:], d_wout[l][:, og * 512:(og + 1) * 512].rearrange("(kc p) n -> p kc n", p=128),
                'ld_wo%d' % og, writes=[('wos', og)])
            act(WOB[og][:], WOS[og][:], AF.Copy, [('wos', og)], [('wob', og)])
        n = 0
        for og in range(2):
            for oc in range(4):
                ch = og * 4 + oc
                for b in range(NB):
                    pbk = n % 8
                    n += 1
                    for kc in range(8):
                        src = LRUOUT[:, kc, bs(b)] if kc < 2 else actT[:, kc, bs(b)]
                        mm(ps[:, pbk, :], WOB[og][:, kc, oc * 128:(oc + 1) * 128], src, kc == 0, kc == 7,
                           [('wob', og)], [('ps', pbk)])
                    vstt(xT[:, ch, bs(b)], ps[:, pbk, :], g1[:, ch:ch + 1], xT[:, ch, bs(b)], ALU.mult, ALU.add,
                         [('ps', pbk), ('x', ch, b)] + MK, [('x', ch, b)])
        P.barrier()

        def o2(kc, b, t, tk):
            act(actT[:, kc, bs(b)], t[:], AF.Identity, [tk] + MK, [('h', kc, b)],
                bias=sh2[:, kc:kc + 1], scale=gs2[:, l, kc:kc + 1])
        norm_mod(None, None, None, o2)
        ffn(l, g2, MK)
        P.barrier()

    def ctx_load(l):
        dma(T1[0][:, 0:256], d_ckT[l], 'ld_ctx', writes=[('t1', 0)])
        for kv in range(2):
            memset(KZ[kv][64:128, :], 0.0, [('qkv',)], eng='pool')
            dma(KZ[kv][64:73, :], d_Ak[:, :], 'ld_ctx', writes=[('qkv',)])
            vcopy(KZ[kv][0:64, 0:256], T1[0][kv * 64:(kv + 1) * 64, 0:256], [('t1', 0)], [('qkv',)])
        stg = [(T1[1], ('t1', 1)), (RT[0], ('rt', 0))]
        for t in range(3):
            memset(DKZ[t][96:128, :], 0.0, [('qkv',)], eng='pool')
        memset(DKZ[2][64:96, :], 0.0, [('qkv',)], eng='pool')
        for c in range(2):
            tt, tk = stg[c]
            dma(tt[:, 0:256], d_cdkT[l][c * 128:(c + 1) * 128, :], 'ld_ctx', writes=[tk])
            for r in range(4):
                n = 4 * c + r
                vcopy(DKZ[n // 3][(n % 3) * 32:(n % 3) * 32 + 32, 0:256], tt[r * 32:(r + 1) * 32, 0:256], [tk], [('qkv',)])
        dma(QN[0][:, 0:256].rearrange("p (a n) -> p a n", a=2), d_cv[l].rearrange("(a p) n -> p a n", p=128), 'ld_ctx',
            writes=[('qn', 0)])
        vcopy(VG[:, 0:2, :, 0:64], QN[0][:, 0:256].rearrange("p (a h d) -> p a h d", a=2, h=2), [('qn', 0)], [('qkv',)])
        dma(QN[1][:].rearrange("p (a n) -> p a n", a=2), d_cdv[l].rearrange("(a p) n -> p a n", p=128), 'ld_ctx',
            writes=[('qn', 1)])
        vcopy(VD[:, 0:2, :, 0:64], QN[1][:].rearrange("p (a h d) -> p a h d", a=2, h=4), [('qn', 1)], [('qkv',)])

    def lru_phase(l):
        P.barrier()
        H2 = T // 2
        for cc in range(2):
            x = LRUX[:, cc, :]
            w = lambda k: lcw[:, l, cc * 4 + k:cc * 4 + k + 1]
            wn = lambda k: lcwn[:, l, cc * 4 + k:cc * 4 + k + 1]
            XK = ['xc']
            act(L_XC[:], x, AF.Identity, [('lrux', cc)] + CK, XK, bias=lcb[:, l, cc:cc + 1], scale=w(2))
            vstt(L_XC[:, 2:T], x[:, 0:T - 2], w(0), L_XC[:, 2:T], ALU.mult, ALU.add, XK + CK, XK)
            vstt(L_XC[:, 1:T], x[:, 0:T - 1], w(1), L_XC[:, 1:T], ALU.mult, ALU.add, XK + CK, XK)
            vstt(L_XC[:, 0:T - 1], x[:, 1:T], w(3), L_XC[:, 0:T - 1], ALU.mult, ALU.add, XK + CK, XK)
            vstt(L_XC[:, 256:T:256], x[:, 254:T - 2:256], wn(0), L_XC[:, 256:T:256], ALU.mult, ALU.add, XK + ['lcwn'], XK)
            vstt(L_XC[:, 256:T:256], x[:, 255:T - 1:256], wn(1), L_XC[:, 256:T:256], ALU.mult, ALU.add, XK + ['lcwn'], XK)
            vstt(L_XC[:, 257:T:256], x[:, 255:T - 1:256], wn(0), L_XC[:, 257:T:256], ALU.mult, ALU.add, XK + ['lcwn'], XK)
            vstt(L_XC[:, 255:T - 1:256], x[:, 256:T:256], wn(3), L_XC[:, 255:T - 1:256], ALU.mult, ALU.add, XK + ['lcwn'], XK)
            vcopy(L_XCB[:], L_XC[:], XK, ['xcb'])
            for h in range(2):
                for dr in range(2):
                    for ri in range(2):
                        combo = dr * 2 + ri
                        gi = dr * 4 + cc * 2 + ri
                        for k in range(2):
                            b = 2 * h + k
                            mm(ps[:, combo * 2 + k, :], gwb[:, gi, :], L_XCB[:, bs(b)], True, True, ['xcb', 'gwb'],
                               [('ps', combo * 2 + k)])
                for dr in range(2):
                    for ri in range(2):
                        combo = dr * 2 + ri
                        gi = dr * 4 + cc * 2 + ri
                        dst = (L_R if ri == 0 else L_I)[dr]
                        dk = ('lr' if ri == 0 else 'li', dr)
                        act(dst[:, h * H2:(h + 1) * H2], ps[:, combo * 2:combo * 2 + 2, :].rearrange("p a b -> p (a b)"),
                            AF.Sigmoid, [('ps', combo * 2), ('ps', combo * 2 + 1)] + CK, [dk], bias=gb[:, l, gi:gi + 1])
            for dr in range(2):
                sc2 = lsc2[:, l, dr * 2 + cc:dr * 2 + cc + 1]
                act(L_T[dr][:], L_R[dr][:], AF.Exp, [('lr', dr), 'lsc2'], [('lt', dr)], scale=sc2)
            for dr in range(2):
                sc = lsc[:, l, dr * 2 + cc:dr * 2 + cc + 1]
                act(L_R[dr][:], L_R[dr][:], AF.Exp, [('lr', dr), 'lsc'], [('lr', dr)], scale=sc)
            for dr in range(2):
                act(L_T[dr][:], L_T[dr][:], AF.Sqrt, [('lt', dr)], [('lt', dr)], bias=1.0, scale=-1.0)
            for dr in range(2):
                A, I_, T_ = L_R[dr], L_I[dr], L_T[dr]
                ak, ik, tk = ('lr', dr), ('li', dr), ('lt', dr)
                vtt(I_[:], I_[:], L_XC[:], ALU.mult, [ik] + XK, [ik])
                vtt(I_[:], I_[:], T_[:], ALU.mult, [ik, tk], [ik])
                e = 0 if dr == 0 else T - 1
                hh = h0[:, l, dr * 2 + cc:dr * 2 + cc + 1]
                vstt(I_[:, e:e + 1], A[:, e:e + 1], hh, I_[:, e:e + 1], ALU.mult, ALU.add, [ik, ak] + CK, [ik])
                if dr == 0:
                    vts(A[:, 256:T:256], A[:, 256:T:256], flags[:, 0:1], None, ALU.mult, None, [ak] + CK, [ak])
                    P.op('dve', lambda A=A, I_=I_, T_=T_: nc.vector.tensor_tensor_scan(
                        out=T_[:], data0=A[:], data1=I_[:], initial=0.0, op0=ALU.mult, op1=ALU.add), [ak, ik, tk], [tk])
                    vcopy(st[:, (l * 4 + cc) * 8:(l * 4 + cc) * 8 + 8], T_[:, 255:T:256], [tk], ['st'])
                else:
                    vts(A[:, 255:T - 1:256], A[:, 255:T - 1:256], flags[:, 0:1], None, ALU.mult, None, [ak] + CK, [ak])
                    P.op('dve', lambda A=A, I_=I_, T_=T_: nc.vector.tensor_tensor_scan(
                        out=T_[:, ::-1], data0=A[:, ::-1], data1=I_[:, ::-1], initial=0.0, op0=ALU.mult, op1=ALU.add),
                        [ak, ik, tk], [tk])
                    vcopy(st[:, (l * 4 + 2 + cc) * 8:(l * 4 + 2 + cc) * 8 + 8], T_[:, 0:T:256], [tk], ['st'])
            vtt(L_T[0][:], L_T[0][:], L_T[1][:], ALU.add, [('lt', 0), ('lt', 1)], [('lt', 0)])
            vtt(LRUOUT[:, cc, :], L_T[0][:], LRUG[:, cc, :], ALU.mult, [('lt', 0), ('lrug', cc)], [('lruout', cc)])

    def attention(l):
        SKEW = 2
        sc_d = 32 ** -0.5
        heads = []
        pair = 0
        for c in range(4):
            for qb in range(NB):
                for hh in range(2):
                    p0 = hh * 64
                    heads.append(dict(kind='g', qrows=QT[p0:p0 + 64, c, bs(qb)], p0=p0, nk=64,
                                      kfn=(lambda kt, hh=hh: KZ[hh][:, kt * 128:(kt + 1) * 128]),
                                      vfn=(lambda kt, hh=hh: VG[:, kt, hh, 0:65]), scale=0.125, qb=qb, hh=hh,
                                      comp=0, pair=pair, last=(hh == 1), dst=2 + c))
                pair += 1
        for c in range(2):
            for qb in range(NB):
                for hh in range(2):
                    for comp in range(2):
                        p0 = hh * 64 + comp * 32
                        heads.append(dict(kind='d', qrows=DQT[p0:p0 + 32, c, bs(qb)], p0=p0, nk=32,
                                          kfn=(lambda kt, n=4 * c + 2 * hh + comp: DKZ[n // 3][:, kt * 128:(kt + 1) * 128]),
                                          slot=(4 * c + 2 * hh + comp) % 3,
                                          vfn=(lambda kt, h=2 * c + hh: VD[:, kt, h, 0:65]), scale=sc_d, qb=qb, hh=hh,
                                          comp=comp, pair=pair, last=(hh == 1 and comp == 1), dst=6 + c))
                pair += 1
        KP = KT // 2
        tiles = [(hi, kp) for hi in range(len(heads)) for kp in range(KP)]
        NTL = len(tiles)

        def accv(hi):
            ab = 4 + hi % 2
            v = ps[:, ab, 0:260].rearrange("p (j c) -> p j c", j=4)
            return ab, v

        def finish(H, hi):
            oi = H['pair'] % 2
            hh = H['hh']
            ab, v = accv(hi)
            acc4, den4 = v[:, :, 0:64], v[:, :, 64:65]
            PACC = [('ps', ab)]
            rec = sml[:, 0:4].unsqueeze(2)
            P.op('dve', lambda: nc.vector.reciprocal(out=rec, in_=den4), PACC, ['smlr'])
            recb = rec.broadcast_to([128, 4, 64])
            if H['kind'] == 'g':
                vtt(OTOK[oi][:, :, hh * 64:(hh + 1) * 64], acc4, recb, ALU.mult, PACC + ['smlr'], [('otok', oi)])
            elif H['comp'] == 0:
                vtt(DD1[:], acc4, recb, ALU.mult, PACC + ['smlr'], ['dd1'])
            else:
                vts(sml[:, 0:4], sml[:, 0:4], nlam[:, l:l + 1], None, ALU.mult, None, ['smlr', 'nlam'], ['smlr'])
                vtt(DD2[:], acc4, recb, ALU.mult, PACC + ['smlr'], ['dd2'])

        def finish_b(H):
            oi = H['pair'] % 2
            hh = H['hh']
            vtt(DD2[:], DD2[:], DD1[:], ALU.add, ['dd2', 'dd1'], ['dd2'])
            vtt(DD1[:], DD2[:], DD2[:], ALU.mult, ['dd2', 'dd1'], ['dd1'])
            P.op('dve', lambda: nc.vector.reduce_sum(out=sml[:, 8:12], in_=DD1[:], axis=mybir.AxisListType.X),
                 ['dd1'], ['smls'])
            vts(sml[:, 8:12], sml[:, 8:12], 1.0 / 64, EPS, ALU.mult, ALU.add, ['smls'], ['smls'])
            vtt(sml[:, 8:12], sml[:, 8:12], mhalf[:, 0:4], ALU.pow, ['smls', 'mhalf'], ['smls'], eng='pool')
            vtt(DD2[:], DD2[:], sml[:, 8:12].unsqueeze(2).broadcast_to([128, 4, 64]), ALU.mult, ['dd2', 'smls'], ['dd2'])
            vtt(OTOK[oi][:, :, hh * 64:(hh + 1) * 64], DD2[:], dg[:, l, :].unsqueeze(1).broadcast_to([128, 4, 64]),
                ALU.mult, ['dd2', 'dg'], [('otok', oi)])

        def transposes(H):
            oi = H['pair'] % 2
            tpb = ps[:, 6, :].bitcast(BF16)
            for j in range(4):
                P.op('pe', lambda j=j: nc.tensor.transpose(tpb[:, j * 128:(j + 1) * 128], OTOK[oi][:, j, :], ident[:]),
                     [('otok', oi), 'ident'], [('ps', 6)])
            vcopy(actT[:, H['dst'], bs(H['qb'])], tpb[:, 0:BLK], [('ps', 6)], [('h', H['dst'], H['qb'])])

        SKEW = 1
        deferred = []
        for i in range(NTL + SKEW):
            if l == 0 and DEPTH > 1 and i % 24 == 12:
                ada1_step()
            if i < NTL:
                hi, kp = tiles[i]
                H = heads[hi]
                zi = hi % 2
                s = i % 2
                if kp == 0:
                    memset(QZ[zi][:], 0.0, [('qz', zi)], eng='pool')
                    if H['kind'] == 'g':
                        vcopy(QZ[zi][0:64, :], H['qrows'], [('qkv',)], [('qz', zi)], eng='pool')
                        vcopy(QZ[zi][64:73, :], mBb[64:73, H['qb'], :], ['mB'], [('qz', zi)], eng='pool')
                    else:
                        r0 = H['slot'] * 32
                        vcopy(QZ[zi][r0:r0 + 32, :], H['qrows'], [('qkv',)], [('qz', zi)], eng='pool')
                        vcopy(QZ[zi][96:105, :], mBb[96:105, H['qb'], :], ['mB'], [('qz', zi)], eng='pool')
                for t in range(2):
                    kt = 2 * kp + t
                    bk = 2 * s + t
                    mm(ps[:, bk, :], H['kfn'](kt), QZ[zi][:], True, True, [('qkv',), ('qz', zi)], [('ps', bk)])
                act(PTt[s][:], ps[:, 2 * s:2 * s + 2, :].rearrange("p a b -> p (a b)"), AF.Exp,
                    [('ps', 2 * s), ('ps', 2 * s + 1)], [('pt', s)], scale=H['scale'])
            if i >= SKEW:
                hi, kp = tiles[i - SKEW]
                H = heads[hi]
                s = (i - SKEW) % 2
                ab, v = accv(hi)
                for t in range(2):
                    kt = 2 * kp + t
                    for j in range(4):
                        first = (kt == 0 and j == 0)
                        P.op('pe', lambda t=t, j=j, kt=kt, first=first, v=v, s=s, H=H: nc.tensor.matmul(
                            v[:, j, :], lhsT=PTt[s][:, t * BLK + j * 128:t * BLK + (j + 1) * 128], rhs=H['vfn'](kt),
                            start=first, stop=(kt == KT - 1), skip_group_check=True),
                            [('pt', s), ('qkv',)], [('ps', ab)])
                if kp == KP - 1:
                    finish(H, hi)
                    if H['kind'] == 'd' and H['comp'] == 1:
                        deferred.append((i + 2, 'b', H))
                        if H['last']:
                            deferred.append((i + 4, 't', H))
                    elif H['last']:
                        deferred.append((i + 1, 't', H))
            deferred.sort(key=lambda d: d[0])
            while deferred and deferred[0][0] <= i:
                d = deferred.pop(0)
                (finish_b if d[1] == 'b' else transposes)(d[2])
        while deferred:
            d = deferred.pop(0)
            (finish_b if d[1] == 'b' else transposes)(d[2])

    def ffn(l, g2, MK):
        H2 = T // 2
        nps = [0]
        nfo = [0]
        nhp = [0]

        def load_up(g):
            wi = g % 2
            dma(FWS[:], d_wup[l][:, g * 512:(g + 1) * 512].rearrange("(kc p) n -> p kc n", p=128), 'ld_fu',
                writes=['fws'])
            vcopy(FWB[wi][:], FWS[:], ['fws'], [('fwb', wi)])

        def load_dn(g):
            wi = g % 2
            dma(FDS[:], d_wdn[l][g * 256:(g + 1) * 256, :].rearrange("(j p) n -> p j n", p=128), 'ld_fd', writes=['fds'])
            vcopy(FDB[wi][:], FDS[:], ['fds'], [('fdb', wi)])

        def down_piece(g, k0, k1):
            wi = g % 2
            for k in range(k0, k1):
                oc, b = k // NB, k % NB
                n = nps[0]
                nps[0] += 1
                pbk = 4 + n % 4
                for jj in range(2):
                    mm(ps[:, pbk, :], FDB[wi][:, jj, oc * 128:(oc + 1) * 128], FACT[wi][:, jj, bs(b)], jj == 0, jj == 1,
                       [('fdb', wi), ('fact', wi)], [('ps', pbk)])
                if n % 8 not in (1, 4, 6):
                    vstt(xT[:, oc, bs(b)], ps[:, pbk, :], g2[:, oc:oc + 1], xT[:, oc, bs(b)], ALU.mult, ALU.add,
                         [('ps', pbk), ('x', oc, b)] + MK, [('x', oc, b)])
                else:
                    fi = nfo[0] % 2
                    nfo[0] += 1
                    act(FOUT[fi][:], ps[:, pbk, :], AF.Identity, [('ps', pbk)] + MK, [('fout', fi)], scale=g2[:, oc:oc + 1])
                    vtt(xT[:, oc, bs(b)], xT[:, oc, bs(b)], FOUT[fi][:], ALU.add, [('fout', fi), ('x', oc, b)],
                        [('x', oc, b)], eng='pool')

        def up(g, gd):
            wi = g % 2
            piece = 0
            for jj in range(2):
                j = 2 * g + jj
                for part in range(2):
                    col = (part * 2 + jj) * 128
                    fc = part * 22 + j
                    cw = lambda k, fc=fc: fcw[:, l, fc * 3 + k:fc * 3 + k + 1]
                    Y, U0, U2 = FY[part], FU0[part], FU2[part]
                    for h in range(2):
                        pr = nhp[0] % 2
                        nhp[0] += 1
                        pb = 2 * pr
                        for k in range(2):
                            b = 2 * h + k
                            for kc in range(8):
                                mm(ps[:, pb + k, :], FWB[wi][:, kc, col:col + 128], actT[:, kc, bs(b)], kc == 0, kc == 7,
                                   [('fwb', wi), ('h', kc, b)], [('ps', pb + k)])
                        PK = [('ps', pb), ('ps', pb + 1)]
                        p2 = ps[:, pb:pb + 2, :].rearrange("p a b -> p (a b)")
                        t0 = h * H2
                        yk, u0k, u2k = ('fy', part, h), ('fu0', part), ('fu2', part)
                        act(Y[:, t0:t0 + H2], p2, AF.Identity, PK + CK, [yk], bias=fcb[:, l, fc:fc + 1], scale=cw(1))
                        fl = flags[:, 0:1]
                        y0, y1 = ('fy', part, 0), ('fy', part, 1)
                        if h == 0:
                            act(U0[:, 1:H2 + 1], p2, AF.Copy, PK + CK, [u0k], scale=cw(0))
                            act(U2[:, 0:H2 - 1], p2[:, 1:H2], AF.Copy, PK + CK, [u2k], scale=cw(2))
                            vts(U0[:, 256:H2 + 1:256], U0[:, 256:H2 + 1:256], fl, None, ALU.mult, None, [u0k] + CK, [u0k])
                            vts(U2[:, 255:H2 - 1:256], U2[:, 255:H2 - 1:256], fl, None, ALU.mult, None, [u2k] + CK, [u2k])
                            vtt(Y[:, 0:H2], Y[:, 0:H2], U0[:, 0:H2], ALU.add, [y0, u0k], [y0])
                            vtt(Y[:, 0:H2 - 2], Y[:, 0:H2 - 2], U2[:, 0:H2 - 2], ALU.add, [y0, u2k], [y0])
                        else:
                            act(U0[:, H2 + 1:T], p2[:, 0:H2 - 1], AF.Copy, PK + CK, [u0k], scale=cw(0))
                            act(U2[:, H2 - 1:T - 1], p2, AF.Copy, PK + CK, [u2k], scale=cw(2))
                            vts(U0[:, H2 + 256:T:256], U0[:, H2 + 256:T:256], fl, None, ALU.mult, None, [u0k] + CK, [u0k])
                            vts(U2[:, H2 - 1:T - 1:256], U2[:, H2 - 1:T - 1:256], fl, None, ALU.mult, None, [u2k] + CK, [u2k])
                            vtt(Y[:, H2:T], Y[:, H2:T], U0[:, H2:T], ALU.add, [y1, u0k], [y1])
                            vtt(Y[:, H2 - 2:T], Y[:, H2 - 2:T], U2[:, H2 - 2:T], ALU.add, [y0, y1, u2k], [y0, y1])
                        if gd is not None:
                            down_piece(gd, piece * 4, piece * 4 + 4)
                        piece += 1
                        if piece == 4 and g + 1 < 11:
                            load_up(g + 1)
                YK = [('fy', p_, h_) for p_ in range(2) for h_ in range(2)]
                act(FY[1][:], FY[1][:], AF.Silu, YK, [('fy', 1, 0), ('fy', 1, 1)])
                vtt(FACT[wi][:, jj, :], FY[1][:], FY[0][:], ALU.mult, YK, [('fact', wi)])
            if g + 1 < 11:
                load_dn(g + 1)

        memset(FU0[0][:, 0:1], 0.0, [('fu0', 0)])
        memset(FU0[1][:, 0:1], 0.0, [('fu0', 1)])
        memset(FU2[0][:, T - 1:T], 0.0, [('fu2', 0)])
        memset(FU2[1][:, T - 1:T], 0.0, [('fu2', 1)])
        load_up(0)
        load_dn(0)
        up(0, None)
        for g in range(11):
            if g + 1 < 11:
                up(g + 1, g)
            else:
                down_piece(g, 0, 32)

    ada(0)
    P.barrier()
    for l in range(n_layers):
        layer(l)

    def of(kc, b, t, tk):
        i = (kc + b) % 2
        act(QN[i][:], t[:], AF.Identity, [tk, 'fng'], [('qn', i)], scale=fng[:, kc:kc + 1])
        dma(o_yT[kc * 128:(kc + 1) * 128, bs(b)], QN[i][:], 'st_y', reads=[('qn', i)])
    norm_mod(None, None, None, of)
    dma(o_st[:, :], st[:], 'st_s', reads=['st'])
    P.emit()
    return nc


def _rope_tables(dim, is_sample):
    if not is_sample:
        return np.stack([np.ones((128, T), np.float32), np.zeros((128, T), np.float32)])
    GRID_W = 64
    rows = T // GRID_W
    t_row = np.repeat(np.arange(rows, dtype=np.float32), GRID_W)
    t_col = np.tile(np.arange(GRID_W, dtype=np.float32), rows)
    axis_dim = dim // 2
    inv = (np.float32(10000.0) ** (-np.arange(0, axis_dim, 2, dtype=np.float32) / np.float32(axis_dim))).astype(np.float32)
    ar = t_row[:, None] * inv
    ac = t_col[:, None] * inv
    ang = np.concatenate([ar, ar, ac, ac], axis=-1)
    cos = np.cos(ang).astype(np.float32).T
    sin = np.sin(ang).astype(np.float32).T
    rep = 128 // dim
    return np.stack([np.tile(cos, (rep, 1)), np.tile(sin, (rep, 1))]).astype(np.float32)


def _perm_matrix(dim):
    q = dim // 4
    Pm = np.zeros((128, 128), np.float32)
    for hb in range(0, 128, dim):
        for d in range(dim):
            quarter = d // q
            m = hb + d
            if quarter % 2 == 0:
                Pm[hb + d + q, m] = -1.0
            else:
                Pm[hb + d - q, m] = 1.0
    return Pm


_CACHE = {}


def kernel(**inp):
    f32 = np.float32
    g = {k: np.asarray(v) for k, v in inp.items()}
    L = DEPTH

    def c(a):
        return np.ascontiguousarray(a, dtype=f32)

    def pk8(v):
        return c(v.reshape(8, 128).T)

    w_in = g['w_in']
    colsF = []
    colsF += list(range(0, 256))
    colsF += list(range(256, 512))
    for cch in range(4):
        for h in (cch, cch + 4):
            colsF += list(range(512 + 64 * h, 512 + 64 * h + 64))
    colsF += list(range(1024, 1152))
    colsF += list(range(1280, 1536))
    colsF += list(range(1536, 1792))
    colsV = list(range(1152, 1280)) + list(range(1792, 2048))
    w_inF = c(w_in[:, :, colsF])
    w_inV = c(w_in[:, :, colsV])
    w_all = np.concatenate([w_inF, w_inV], axis=2)
    w_inC = c(w_all.reshape(L, 8, 128, 16, 128).transpose(0, 3, 2, 1, 4).reshape(L, 16, 128, 1024))
    rows_o = list(range(0, 256))
    for cch in range(4):
        for h in (cch, cch + 4):
            rows_o += list(range(256 + 64 * h, 256 + 64 * h + 64))
    rows_o += list(range(768, 1024))
    w_outP = c(g['w_out'][:, rows_o, :])
    up = g['ffn_w_up']
    cols_up = []
    for grp in range(11):
        for part in range(2):
            for jj in range(2):
                j = 2 * grp + jj
                cols_up += list(range(part * D_FF + 128 * j, part * D_FF + 128 * j + 128))
    w_upP = c(up[:, :, cols_up])
    shared = {
        'ada_w': c(g['ada_w']),
        'ada_b': c(g['ada_b'].reshape(L, 48, 128).transpose(0, 2, 1)),
        'n1g': c(g['norm1_g'].reshape(L, 8, 128).transpose(0, 2, 1)),
        'n2g': c(g['norm2_g'].reshape(L, 8, 128).transpose(0, 2, 1)),
        'fng': pk8(g['final_norm_g']),
        'w_inC': w_inC, 'w_outP': w_outP, 'w_upP': w_upP, 'w_dn': c(g['ffn_w_down']),
        'lcw': c(g['lru_conv_w'].reshape(L, 4, 2, 128).transpose(0, 3, 2, 1).reshape(L, 128, 8)),
        'lcb': c(g['lru_conv_b'].reshape(L, 2, 128).transpose(0, 2, 1)),
        'lam': c(g['lru_lambda'].reshape(L, 2, 2, 128).transpose(0, 3, 1, 2).reshape(L, 128, 4)),
        'qkg': c(np.stack([np.tile(g['gqa_q_norm_g'], (1, 2)), np.tile(g['gqa_k_norm_g'], (1, 2))], axis=-1)),
        'dl': c(g['diff_lambda'].reshape(L, 128)),
        'dg': c(g['diff_norm_g']),
        'fcw': c(g['ffn_conv_w'].reshape(L, 3, 44, 128).transpose(0, 3, 2, 1).reshape(L, 128, 132)),
        'fcb': c(g['ffn_conv_b'].reshape(L, 44, 128).transpose(0, 2, 1)),
        'permg': _perm_matrix(64), 'permd': _perm_matrix(32),
        'ident': np.eye(128, dtype=f32),
    }
    gwsrc = g['lru_gate_w']
    gw = np.zeros((L, 8, 128, 128), f32)
    gbsrc = g['lru_gate_b']
    gbv = np.zeros((L, 128, 8), f32)
    for l in range(L):
        for dr in range(2):
            for cc in range(2):
                for ri in range(2):
                    idx = dr * 4 + cc * 2 + ri
                    for bl in range(2):
                        blk = 2 * cc + bl
                        gw[l, idx, bl * 64:(bl + 1) * 64, bl * 64:(bl + 1) * 64] = gwsrc[l, dr, blk, :, ri * 64:(ri + 1) * 64]
                        gbv[l, bl * 64:(bl + 1) * 64, idx] = gbsrc[l, dr, blk, ri * 64:(ri + 1) * 64]
    shared['gw'] = gw
    shared['gb'] = gbv

    mA4 = np.zeros((128, 9 * 128), f32)
    for j in range(9):
        mA4[j, j * 128:(j + 1) * 128] = 1.0

    def mB4_for(is_sample):
        m = np.zeros((128, 4 * BLK), f32)
        if not is_sample:
            qseg = np.arange(T) // 256 + 1
            for j in range(9):
                m[j, :] = np.where(qseg == j, 0.0, NEG)
                m[64 + j, :] = m[j, :]
                m[96 + j, :] = m[j, :]
        return m

    import ml_dtypes
    Ak = np.zeros((9, TK), f32)
    for j in range(9):
        Ak[j, j * 256:(j + 1) * 256] = 1.0
    Ak = Ak.astype(ml_dtypes.bfloat16)

    ropeg_s, roped_s = _rope_tables(64, True), _rope_tables(32, True)
    ropeg_p, roped_p = _rope_tables(64, False), _rope_tables(32, False)
    mB_s, mB_p = mB4_for(True), mB4_for(False)

    xs, xp = g['x_sample'], g['x_prompt']
    in_maps = []
    for core in range(8):
        m = dict(shared)
        m['Ak'] = Ak
        if core < 4:
            b = core
            m['xT'] = c(xs[b].T)
            m['cT'] = pk8(g['c'][b])
            m['flags'] = c(np.tile(np.array([[1.0, 0.0]], f32), (128, 1)))
            m['mB4'] = mB_s
            m['ropeg'], m['roped'] = ropeg_s, roped_s
            m['ckT'] = c(g['cache_gqa_k'][b].reshape(L, 256, 128).transpose(0, 2, 1))
            m['cv'] = c(g['cache_gqa_v'][b].reshape(L, 256, 128))
            m['cdkT'] = c(g['cache_diff_k'][b].reshape(L, 256, 256).transpose(0, 2, 1))
            m['cdv'] = c(g['cache_diff_v'][b].reshape(L, 256, 256))
            m['h0'] = c(g['state_lru'][b].reshape(L, 2, 2, 128).transpose(0, 3, 1, 2).reshape(L, 128, 4))
        else:
            i = (core - 4) % 2
            m['xT'] = c(xp[8 * i:8 * i + 8].reshape(T, D_MODEL).T)
            m['cT'] = pk8(g['c_ctx'])
            m['flags'] = c(np.tile(np.array([[0.0, 1.0]], f32), (128, 1)))
            m['mB4'] = mB_p
            m['ropeg'], m['roped'] = ropeg_p, roped_p
            m['ckT'] = np.zeros((L, 128, 256), f32)
            m['cv'] = np.zeros((L, 256, 128), f32)
            m['cdkT'] = np.zeros((L, 256, 256), f32)
            m['cdv'] = np.zeros((L, 256, 256), f32)
            m['h0'] = np.zeros((L, 128, 4), f32)
        in_maps.append(m)

    if 'nc' not in _CACHE:
        _CACHE['nc'] = build_program()
    nc = _CACHE['nc']
    res = run_bass_kernel_spmd(nc, in_maps, core_ids=list(range(8)))
    R = res.results

    y_sample = np.stack([R[b]['yT'].T for b in range(4)]).astype(f32)
    yp, ks, vs, dks, dvs, sts = [], [], [], [], [], []
    for i in range(2):
        r = R[4 + i]
        yT = r['yT']
        for s in range(8):
            sl = slice(256 * s, 256 * s + 256)
            yp.append(yT[:, sl].T)
            ks.append(np.stack([r['okT'][l][:, sl].T.reshape(256, 2, 64) for l in range(L)]))
            vs.append(np.stack([r['ov'][l][sl, :].reshape(256, 2, 64) for l in range(L)]))
            dks.append(np.stack([r['odkT'][l][:, sl].T.reshape(256, 4, 2, 32) for l in range(L)]))
            dvs.append(np.stack([r['odv'][l][sl, :].reshape(256, 4, 64) for l in range(L)]))
            o = r['ost'].reshape(128, L, 2, 2, 8)[:, :, :, :, s]
            sts.append(o.transpose(1, 2, 3, 0).reshape(L, 2, 256))
    out = (np.stack(yp).astype(f32), y_sample, np.stack(ks).astype(f32), np.stack(vs).astype(f32),
           np.stack(dks).astype(f32), np.stack(dvs).astype(f32), np.stack(sts).astype(f32))
    return tuple(np.ascontiguousarray(o) for o in out)
```

```python
import contextlib
import numpy as np
import concourse.bass as bass
import concourse.mybir as mybir
from concourse.bass_utils import run_bass_kernel_spmd

F32 = mybir.dt.float32
BF16 = mybir.dt.bfloat16
ALU = mybir.AluOpType
AF = mybir.ActivationFunctionType

D_MODEL = 1024
T = 2048
BLK = 512
NB = 4
NT = 16
KT = 18
TK = 2304
DEPTH = 2
D_FF = 2816
EPS = 1e-6
NEG = -30000.0


class Prog:
    def __init__(self, nc):
        self.nc = nc
        self.E = {'pe': nc.tensor, 'act': nc.scalar, 'dve': nc.vector, 'pool': nc.gpsimd, 'sp': nc.sync}
        self.ops = []

    def op(self, eng, fn, reads=(), writes=(), dma=None):
        self.ops.append((eng, fn, tuple(reads), tuple(writes), dma))

    def barrier(self):
        self.ops.append(None)

    def emit(self, final_eng='sp'):
        nc = self.nc
        ops = self.ops
        n = len(ops)
        last_w = {}
        readers = {}
        deps = [None] * n
        marked = [False] * n
        last_eng = {}
        pend_dma = []
        bar_set = None
        bar_done = set()
        for i, rec in enumerate(ops):
            if rec is None:
                s = set(last_eng.values()) | set(pend_dma)
                if bar_set is not None:
                    s |= bar_set
                bar_set = s
                bar_done = set()
                pend_dma = []
                continue
            eng, fn, rd, wr, dma = rec
            d = set()
            if bar_set is not None and eng not in bar_done:
                d |= bar_set
                bar_done.add(eng)
            for t in rd:
                j = last_w.get(t)
                if j is not None:
                    d.add(j)
            for t in wr:
                j = last_w.get(t)
                if j is not None:
                    d.add(j)
                r = readers.get(t)
                if r:
                    d.update(r[0].values())
                    d.update(r[1])
            d.discard(i)
            dd = []
            for j in d:
                ej, _, _, _, dmaj = ops[j]
                if dmaj is None and dma is None and ej == 'pe' and eng == 'pe':
                    continue
                dd.append(j)
                marked[j] = True
            deps[i] = dd
            for t in rd:
                r = readers.get(t)
                if r is None:
                    r = readers[t] = ({}, [])
                if dma is None:
                    r[0][eng] = i
                else:
                    r[1].append(i)
            for t in wr:
                last_w[t] = i
                readers[t] = ({}, [])
            if dma is None:
                last_eng[eng] = i
            else:
                pend_dma.append(i)
        cnt = {}
        mile = [None] * n
        for i, rec in enumerate(ops):
            if rec is None:
                continue
            eng, fn, rd, wr, dma = rec
            if dma is not None:
                cnt[dma] = cnt.get(dma, 0) + 16
                mile[i] = (dma, cnt[dma])
            elif marked[i]:
                k = 'E_' + eng
                cnt[k] = cnt.get(k, 0) + 1
                mile[i] = (k, cnt[k])
        sems = {}
        with contextlib.ExitStack() as st:
            for k in cnt:
                sems[k] = st.enter_context(nc.semaphore(k))
            waited = {e: {} for e in self.E}
            for i, rec in enumerate(ops):
                if rec is None:
                    continue
                eng, fn, rd, wr, dma = rec
                need = {}
                for j in deps[i]:
                    s, v = mile[j]
                    if need.get(s, 0) < v:
                        need[s] = v
                for s, v in need.items():
                    if waited[eng].get(s, 0) < v:
                        self.E[eng].wait_ge(sems[s], v)
                        waited[eng][s] = v
                ins = fn()
                if mile[i] is not None:
                    s, v = mile[i]
                    ins.then_inc(sems[s], 16 if dma is not None else 1)
            for k, v in cnt.items():
                if waited[final_eng].get(k, 0) < v:
                    self.E[final_eng].wait_ge(sems[k], v)
        return cnt


def build_program(n_layers=DEPTH, debug=False):
    nc = bass.Bass("TRN2", target_bir_lowering=False)
    P = Prog(nc)

    def din(name, shape):
        return nc.dram_tensor(name, list(shape), F32, kind="ExternalInput").ap()

    def dout(name, shape):
        return nc.dram_tensor(name, list(shape), F32, kind="ExternalOutput").ap()

    d_xT = din("xT", [D_MODEL, T])
    d_cT = din("cT", [128, 8])
    d_flags = din("flags", [128, 2])
    d_mB = din("mB4", [128, 4 * BLK])
    d_Ak = nc.dram_tensor("Ak", [9, TK], BF16, kind="ExternalInput").ap()
    d_ropeg = din("ropeg", [2, 128, T])
    d_roped = din("roped", [2, 128, T])
    d_permg = din("permg", [128, 128])
    d_permd = din("permd", [128, 128])
    d_ckT = din("ckT", [DEPTH, 128, 256])
    d_cv = din("cv", [DEPTH, 256, 128])
    d_cdkT = din("cdkT", [DEPTH, 256, 256])
    d_cdv = din("cdv", [DEPTH, 256, 256])
    d_h0 = din("h0", [DEPTH, 128, 4])
    d_n1 = din("n1g", [DEPTH, 128, 8])
    d_n2 = din("n2g", [DEPTH, 128, 8])
    d_fg = din("fng", [128, 8])
    d_adaw = din("ada_w", [DEPTH, D_MODEL, 6 * D_MODEL])
    d_adab = din("ada_b", [DEPTH, 128, 48])
    d_winC = din("w_inC", [DEPTH, 16, 128, 8 * 128])
    d_wout = din("w_outC", [DEPTH, 2, 128, 8 * 512])
    d_lcw = din("lcw", [DEPTH, 128, 8])
    d_lcb = din("lcb", [DEPTH, 128, 2])
    d_gw = din("gw", [DEPTH, 8, 128, 128])
    d_gb = din("gb", [DEPTH, 128, 8])
    d_lam = din("lam", [DEPTH, 128, 4])
    d_qkg = din("qkg", [DEPTH, 128, 2])
    d_dl = din("dl", [DEPTH, 128])
    d_dg = din("dg", [DEPTH, 64])
    d_wup = din("w_upC", [DEPTH, 11, 128, 8 * 512])
    d_fcw = din("fcw", [DEPTH, 128, 44 * 3])
    d_fcb = din("fcb", [DEPTH, 128, 44])
    d_wdn = din("w_dnC", [DEPTH, 11, 128, 2 * D_MODEL])

    o_yT = dout("yT", [D_MODEL, T])
    o_kT = dout("okT", [DEPTH, 128, T])
    o_v = dout("ov", [DEPTH, T, 128])
    o_dkT = dout("odkT", [DEPTH, 256, T])
    o_dv = dout("odv", [DEPTH, T, 256])
    o_st = dout("ost", [128, 64])

    base0 = (nc.sbuf_base + 31) // 32 * 32
    ARENA = 212800
    arena = nc.alloc_sbuf_tensor("arena", [128, ARENA], mybir.dt.uint8)
    cnt_name = [0]

    def at(off, shape, dt):
        cnt_name[0] += 1
        assert off % 32 == 0 and off >= 0, off
        nbytes = int(np.prod(shape[1:])) * (4 if dt == F32 else 2)
        assert off + nbytes <= ARENA, (off, nbytes)
        return nc.alloc_sbuf_tensor_at("t%d" % cnt_name[0], list(shape), dt, offset=base0 + off)

    OFF_X, OFF_ACT, OFF_C, OFF_XR = 0, 65536, 98304, 115456
    xT = at(OFF_X, [128, 8, T], F32)
    actT = at(OFF_ACT, [128, 8, T], BF16)

    cptr = [OFF_C]

    def calloc(shape, dt):
        nbytes = int(np.prod(shape[1:])) * (4 if dt == F32 else 2)
        nbytes = (nbytes + 31) // 32 * 32
        t = at(cptr[0], shape, dt)
        cptr[0] += nbytes
        assert cptr[0] <= OFF_XR
        return t

    ident = calloc([128, 128], BF16)
    ones = calloc([128, 128], BF16)
    bones = calloc([128, 128], BF16)
    permg = calloc([128, 128], BF16)
    permd = calloc([128, 128], BF16)
    gwb = calloc([128, 8, 128], BF16)
    mBb = calloc([128, 4, BLK], BF16)
    mod = calloc([128, 2, 48], F32)
    adab = calloc([128, 2, 48], F32)
    n1g = calloc([128, 2, 8], F32)
    n2g = calloc([128, 2, 8], F32)
    fng = calloc([128, 8], F32)
    gs1 = calloc([128, 2, 8], F32)
    gs2 = calloc([128, 2, 8], F32)
    cT = calloc([128, 8], F32)
    siluc = calloc([128, 8], F32)
    flags = calloc([128, 2], F32)
    lcw = calloc([128, 2, 8], F32)
    lcwn = calloc([128, 2, 8], F32)
    lcb = calloc([128, 2, 2], F32)
    gb = calloc([128, 2, 8], F32)
    lam = calloc([128, 2, 4], F32)
    lsc = calloc([128, 2, 4], F32)
    lsc2 = calloc([128, 2, 4], F32)
    h0 = calloc([128, 2, 4], F32)
    qkg = calloc([128, 2, 2], F32)
    dlt = calloc([128, 64], F32)
    dle = calloc([128, 4], F32)
    nlam = calloc([128, 2], F32)
    dg = calloc([128, 2, 64], F32)
    fcw = calloc([128, 2, 44 * 3], F32)
    fcwn = calloc([128, 2, 44 * 3], F32)
    fcb = calloc([128, 2, 44], F32)
    st = calloc([128, 64], F32)
    sml = calloc([128, 16], F32)
    sstage = calloc([128, 128], F32)

    X = OFF_XR
    LRUOUT = at(X + 0, [128, 2, T], BF16)
    QT = at(X + 8192, [128, 4, T], BF16)
    DQT = at(X + 24576, [128, 2, T], BF16)
    KZ = [at(X + 32768 + 4608 * i, [128, TK], BF16) for i in range(2)]
    DKZ = [at(X + 41984 + 4608 * i, [128, TK], BF16) for i in range(3)]
    VG = at(X + 55808, [128, KT, 2, 66], BF16)
    VD = at(X + 60576, [128, KT, 4, 66], BF16)
    WS = [at(X + 70080 + 4096 * i, [128, 8, 128], F32) for i in range(2)]
    WB = [at(X + 78272 + 2048 * i, [128, 8, 128], BF16) for i in range(2)]
    PT0 = X + 82368
    SQ = [at(PT0 + 1024 * i, [128, BLK], BF16) for i in range(2)]
    RT = [at(PT0 + 2048, [128, BLK], F32) for i in range(2)]
    QN = [at(PT0 + 4096 + 2048 * i, [128, BLK], F32) for i in range(2)]
    T1 = [at(PT0 + 8192 + 2048 * i, [128, BLK], F32) for i in range(2)]
    VST = [at(PT0 + 12288, [128, 4, 128], F32), at(PT0 + 4096 + 2048, [128, 4, 128], F32)]
    ROPET = at(X + 70080, [128, 4, BLK], F32)
    T2 = [at(X + 78272 + 2048 * i, [128, BLK], F32) for i in range(2)]
    LRUX = at(X + 8192, [128, 2, T], BF16)
    LRUG = at(X + 16384, [128, 2, T], BF16)
    L_R = [at(X + 24576 + 24576 * d, [128, T], F32) for d in range(2)]
    L_I = [at(X + 32768 + 24576 * d, [128, T], F32) for d in range(2)]
    L_T = [at(X + 40960 + 24576 * d, [128, T], F32) for d in range(2)]
    L_XC = at(X + 73728, [128, T], F32)
    L_XCB = at(X + 81920, [128, T], BF16)
    PTt = [at(X + 70080 + 2048 * i, [128, 2 * BLK], BF16) for i in range(2)]
    OTOK = [at(X + 74176 + 1024 * i, [128, 4, 128], BF16) for i in range(2)]
    DD1 = at(X + 76224, [128, 4, 64], F32)
    DD2 = at(X + 77248, [128, 4, 64], F32)
    QZ = [at(X + 78272 + 1024 * i, [128, BLK], BF16) for i in range(2)]
    ADAS = [at(X + 16384 * i, [128, 8, BLK], F32) for i in range(2)]
    WOS = [at(X + 8192 + 16384 * i, [128, 8, BLK], F32) for i in range(2)]
    WOB = [at(X + 40960 + 8192 * i, [128, 8, BLK], BF16) for i in range(2)]
    FWS = at(X + 0, [128, 8, BLK], F32)
    FWB = [at(X + 16384 + 8192 * i, [128, 8, BLK], BF16) for i in range(2)]
    FDS = at(X + 32768, [128, 2, D_MODEL], F32)
    FDB = [at(X + 40960 + 4096 * i, [128, 2, D_MODEL], BF16) for i in range(2)]
    FY = [at(X + 49152 + 4096 * i, [128, T], BF16) for i in range(2)]
    FU0 = [at(X + 57344 + 4096 * i, [128, T], BF16) for i in range(2)]
    FU2 = [at(X + 65536 + 4096 * i, [128, T], BF16) for i in range(2)]
    FACT = [at(X + 73728 + 8192 * i, [128, 2, T], BF16) for i in range(2)]
    FOUT = [at(X + 90112 + 2048 * i, [128, BLK], F32) for i in range(2)]

    ps = nc.alloc_psum_tensor("ps", [128, 8, BLK], F32)

    def dma(out, in_, key, reads=(), writes=(), eng='sp'):
        P.op(eng, lambda: P.E[eng].dma_start(out=out, in_=in_), reads, writes, dma=key)

    def mm(out, lhsT, rhs, start, stop, reads, writes, tp=None):
        if tp is None:
            P.op('pe', lambda: nc.tensor.matmul(out, lhsT=lhsT, rhs=rhs, start=start, stop=stop), reads, writes)
        else:
            P.op('pe', lambda: nc.tensor.matmul(out, lhsT=lhsT, rhs=rhs, start=start, stop=stop, tile_position=tp),
                 reads, writes)

    def act(out, in_, func, reads, writes, bias=None, scale=None, accum=None):
        kw = {}
        if bias is not None:
            kw['bias'] = bias
        if scale is not None:
            kw['scale'] = scale
        if accum is not None:
            kw['accum_out'] = accum
        P.op('act', lambda: nc.scalar.activation(out=out, in_=in_, func=func, **kw), reads, writes)

    def vtt(out, in0, in1, op, reads, writes, eng='dve'):
        P.op(eng, lambda: P.E[eng].tensor_tensor(out=out, in0=in0, in1=in1, op=op), reads, writes)

    def vts(out, in0, s1, s2, op0, op1, reads, writes, eng='dve'):
        if op1 is None:
            P.op(eng, lambda: P.E[eng].tensor_scalar(out=out, in0=in0, scalar1=s1, scalar2=None, op0=op0), reads, writes)
        else:
            P.op(eng, lambda: P.E[eng].tensor_scalar(out=out, in0=in0, scalar1=s1, scalar2=s2, op0=op0, op1=op1),
                 reads, writes)

    def vstt(out, in0, scalar, in1, op0, op1, reads, writes):
        P.op('dve', lambda: nc.vector.scalar_tensor_tensor(out=out, in0=in0, scalar=scalar, in1=in1, op0=op0, op1=op1),
             reads, writes)

    def vcopy(out, in_, reads, writes, eng='dve'):
        P.op(eng, lambda: P.E[eng].tensor_copy(out=out, in_=in_), reads, writes)

    def vrecip(out, in_, reads, writes):
        P.op('dve', lambda: nc.vector.reciprocal(out=out, in_=in_), reads, writes)

    def memset(ap, val, writes, eng='dve'):
        P.op(eng, lambda: P.E[eng].memset(ap, val), (), writes)

    def bs(b):
        return slice(b * BLK, (b + 1) * BLK)

    for kc in range(8):
        dma(xT[:, kc, :], d_xT[kc * 128:(kc + 1) * 128, :], 'ld_x', writes=[('x', kc, b) for b in range(NB)])
    small_loads = [
        (cT[:], d_cT[:, :], 'cT'), (flags[:], d_flags[:, :], 'flags'),
        (fng[:], d_fg[:, :], 'fng'),
    ]
    for l in range(DEPTH):
        small_loads += [
            (adab[:, l, :], d_adab[l], 'adab'), (n1g[:, l, :], d_n1[l], 'n1g'), (n2g[:, l, :], d_n2[l], 'n2g'),
            (lcw[:, l, :], d_lcw[l], 'lcw'), (lcb[:, l, :], d_lcb[l], 'lcb'), (gb[:, l, :], d_gb[l], 'gb'),
            (lam[:, l, :], d_lam[l], 'lam'), (h0[:, l, :], d_h0[l], 'h0'), (qkg[:, l, :], d_qkg[l], 'qkg'),
            (dg[:, l, :], d_dg[l].partition_broadcast(128), 'dg'),
            (fcw[:, l, :], d_fcw[l], 'fcw'), (fcb[:, l, :], d_fcb[l], 'fcb'),
        ]
    for (o, i, k) in small_loads:
        dma(o, i, 'ld_c', writes=[k])
    CK = ['cT', 'flags', 'fng', 'adab', 'n1g', 'n2g', 'lcw', 'lcb', 'gb', 'lam', 'h0', 'qkg', 'dg', 'fcw', 'fcb']
    memset(ones[:], 1.0, ['ones'])
    memset(bones[:], 0.0, ['bones'])
    memset(bones[0:64, 0:64], 1.0, ['bones'])
    memset(bones[64:128, 64:128], 1.0, ['bones'])
    for (dst, src, k) in ((permg, d_permg, 'permg'), (permd, d_permd, 'permd')):
        dma(sstage[:], src[:, :], 'ld_c2', writes=['sstage'])
        vcopy(dst[:], sstage[:], ['sstage'], [k])
    d_ident = din("ident", [128, 128])
    dma(sstage[:], d_ident[:, :], 'ld_c2', writes=['sstage'])
    vcopy(ident[:], sstage[:], ['sstage'], ['ident'])
    dma(ADAS[1][:, 0:4, :].rearrange("p a b -> p (a b)"), d_mB[:, :], 'ld_c2', writes=['adas1'])
    vcopy(mBb[:], ADAS[1][:, 0:4, :], ['adas1'], ['mB'])
    P.barrier()
    act(siluc[:], cT[:], AF.Silu, CK, ['siluc'])
    act(lsc[:], lam[:], AF.Exp, CK, ['lsc'], scale=-1.0)
    act(lsc[:], lsc[:], AF.Ln, ['lsc'], ['lsc'], bias=1.0)
    vts(lsc2[:], lsc[:], -16.0, None, ALU.mult, None, ['lsc'], ['lsc2'])
    vts(lsc[:], lsc[:], -8.0, None, ALU.mult, None, ['lsc', 'lsc2'], ['lsc'])
    vts(lcwn[:], lcw[:], flags[:, 1:2], -1.0, ALU.mult, ALU.mult, CK, ['lcwn'])
    vts(fcwn[:], fcw[:], flags[:, 1:2], -1.0, ALU.mult, ALU.mult, CK, ['fcwn'])
    import math
    lam_init = [0.8 - 0.6 * math.exp(-0.3 * l) for l in range(DEPTH)]
    for l in range(DEPTH):
        dma(sstage[:], d_dl[l].partition_broadcast(128), 'ld_c2', writes=['sstage'])
        for k in range(2):
            vtt(dlt[:, 0:32], sstage[:, 64 * k:64 * k + 32], sstage[:, 64 * k + 32:64 * k + 64], ALU.mult, ['sstage'], ['dlt'])
            P.op('dve', lambda k=k, l=l: nc.vector.reduce_sum(out=dle[:, 2 * l + k:2 * l + k + 1], in_=dlt[:, 0:32],
                                                            axis=mybir.AxisListType.X), ['dlt'], ['dle'])
        act(dle[:, 2 * l:2 * l + 2], dle[:, 2 * l:2 * l + 2], AF.Exp, ['dle'], ['dle'])
        vts(sml[:, 0:1], dle[:, 2 * l + 1:2 * l + 2], -lam_init[l], None, ALU.add, None, ['dle'], ['sml'])
        vtt(nlam[:, l:l + 1], sml[:, 0:1], dle[:, 2 * l:2 * l + 1], ALU.subtract, ['sml', 'dle'], ['nlam'])
        vts(dg[:, l, :], dg[:, l, :], 1.0 - lam_init[l], None, ALU.mult, None, CK, ['dg'])
    P.barrier()

    ADS = [at(X + 16384 * i, [128, 8, BLK], F32) for i in range(3)]
    ADB = [at(X + 49152 + 8192 * i, [128, 8, BLK], BF16) for i in range(2)]
    ROW = at(X + 65536, [1, 6 * D_MODEL], F32)
    onef = calloc([128, 2], F32)
    mhalf = calloc([128, 4], F32)
    memset(mhalf[:], -0.5, ['mhalf'])
    silucb = calloc([128, 8], BF16)
    memset(onef[:], 1.0, ['onef'])
    vcopy(silucb[:], siluc[:], ['siluc'], ['silucb'])
    adac = [0]

    def ada(l):
        for g in range(12):
            n = adac[0]
            adac[0] += 1
            s3 = n % 3
            s = n % 2
            for hf in range(2):
                dma(ADS[s3][:, 4 * hf:4 * hf + 4, :],
                    d_adaw[l][hf * 512:(hf + 1) * 512, g * 512:(g + 1) * 512].rearrange("(kc p) n -> p kc n", p=128),
                    'ld_ada%d' % s3, writes=[('ads', s3)], eng=('sp' if hf == 0 else 'pool'))
            if n % 2 == 0:
                act(ADB[s][:], ADS[s3][:], AF.Copy, [('ads', s3)], [('adb', s)])
            else:
                vcopy(ADB[s][:], ADS[s3][:], [('ads', s3)], [('adb', s)])
            bank = g % 4
            for kc in range(8):
                mm(ps[0:1, bank, :], silucb[:, kc:kc + 1], ADB[s][:, kc, :], kc == 0, kc == 7,
                   [('adb', s), 'silucb'], [('ps', bank)])
            vcopy(ROW[0:1, g * 512:(g + 1) * 512], ps[0:1, bank, :], [('ps', bank)], ['row'])
        for col in range(48):
            mm(ps[:, 7, col:col + 1], ROW[0:1, col * 128:(col + 1) * 128], onef[0:1, 0:1], True, True,
               ['row', 'onef'], [('ps', 7)])
        vtt(mod[:, l, :], ps[:, 7, 0:48], adab[:, l, :], ALU.add, [('ps', 7)], [('mod', l)])
        vstt(gs1[:, l, :], mod[:, l, 8:16], 1.0, n1g[:, l, :], ALU.add, ALU.mult, [('mod', l)], [('gs', l)])
        vstt(gs2[:, l, :], mod[:, l, 32:40], 1.0, n2g[:, l, :], ALU.add, ALU.mult, [('mod', l)], [('gs', l)])

    ADS2 = at(X + 80320, [128, 8, 256], F32)
    ADB2 = at(X + 88512, [128, 8, 256], BF16)
    ROW2 = at(X + 92608, [1, 256], F32)
    ada1_state = [0]

    def ada1_scatter(g):
        for j in range(2):
            col = 256 + 2 * g + j
            mm(ps[:, 7, col:col + 1], ROW2[0:1, j * 128:(j + 1) * 128], onef[0:1, 0:1], True, True,
               ['row2', 'onef'], [('ps', 7)])

    def ada1_step():
        g = ada1_state[0]
        if g >= 24:
            return
        ada1_state[0] += 1
        if g > 0:
            ada1_scatter(g - 1)
        for hf in range(2):
            dma(ADS2[:, 4 * hf:4 * hf + 4, :],
                d_adaw[1][hf * 512:(hf + 1) * 512, g * 256:(g + 1) * 256].rearrange("(kc p) n -> p kc n", p=128),
                'ld_ada2', writes=['ads2'])
        vcopy(ADB2[:], ADS2[:], ['ads2'], ['adb2'])
        for kc in range(8):
            mm(ps[0:1, 7, 0:256], silucb[:, kc:kc + 1], ADB2[:, kc, :], kc == 0, kc == 7, ['adb2', 'silucb'], [('ps', 7)])
        vcopy(ROW2[0:1, :], ps[0:1, 7, 0:256], [('ps', 7)], ['row2'])

    def ada1_finish():
        while ada1_state[0] < 24:
            ada1_step()
        ada1_scatter(23)
        vtt(mod[:, 1, :], ps[:, 7, 256:304], adab[:, 1, :], ALU.add, [('ps', 7)], [('mod', 1)])
        vstt(gs1[:, 1, :], mod[:, 1, 8:16], 1.0, n1g[:, 1, :], ALU.add, ALU.mult, [('mod', 1)], [('gs', 1)])
        vstt(gs2[:, 1, :], mod[:, 1, 32:40], 1.0, n2g[:, 1, :], ALU.add, ALU.mult, [('mod', 1)], [('gs', 1)])

    def norm_mod(gs_ap, sh_ap, rkeys, out_fn):
        for b in range(NB):
            for kc in range(8):
                if kc % 2 == 0:
                    act(SQ[kc % 2][:], xT[:, kc, bs(b)], AF.Square, [('x', kc, b)], [('sq', kc % 2)])
                else:
                    vtt(SQ[kc % 2][:], xT[:, kc, bs(b)], xT[:, kc, bs(b)], ALU.mult, [('x', kc, b)], [('sq', kc % 2)])
                mm(ps[:, b % 2, :], ones[:], SQ[kc % 2][:], kc == 0, kc == 7, [('sq', kc % 2), 'ones'], [('ps', b % 2)])
            act(RT[b % 2][:], ps[:, b % 2, :], AF.Ln, [('ps', b % 2)], [('rt', 0)], bias=EPS, scale=1.0 / D_MODEL)
            act(RT[b % 2][:], RT[b % 2][:], AF.Exp, [('rt', 0)], [('rt', 0)], scale=-0.5)
            for kc in range(8):
                vtt(T1[kc % 2][:], xT[:, kc, bs(b)], RT[b % 2][:], ALU.mult, [('x', kc, b), ('rt', 0)], [('t1', kc % 2)])
                out_fn(kc, b, T1[kc % 2], ('t1', kc % 2))

    wcount = [0]

    def load_w128(src_ap):
        i = wcount[0] % 2
        wcount[0] += 1
        dma(WS[i][:].rearrange("p kc n -> p (kc n)"), src_ap, 'ld_ws%d' % i, writes=[('ws', i)])
        vcopy(WB[i][:], WS[i][:], [('ws', i)], [('wb', i)], eng='pool')
        return i

    def layer(l):
        sh1, g1 = mod[:, l, 0:8], mod[:, l, 16:24]
        sh2, g2 = mod[:, l, 24:32], mod[:, l, 40:48]
        MK = [('mod', l), ('gs', l)]

        def o1(kc, b, t, tk):
            act(actT[:, kc, bs(b)], t[:], AF.Identity, [tk] + MK, [('h', kc, b)],
                bias=sh1[:, kc:kc + 1], scale=gs1[:, l, kc:kc + 1])
        norm_mod(None, None, None, o1)

        for i in range(8):
            dma(sstage[:], d_gw[l, i], 'ld_gw', writes=['sstage'], eng='pool')
            vcopy(gwb[:, i, :], sstage[:], ['sstage'], ['gwb'], eng='pool')

        pbank = [0]

        def proj_chunk(ci, post):
            wi = load_w128(d_winC[l, ci])
            pend = None
            for b in range(NB):
                pb = pbank[0] % 6
                pbank[0] += 1
                for kc in range(8):
                    mm(ps[:, pb, :], WB[wi][:, kc, :], actT[:, kc, bs(b)], kc == 0, kc == 7,
                       [('wb', wi), ('h', kc, b)], [('ps', pb)])
                nxt = post(b, ps[:, pb, :], ('ps', pb))
                if pend is not None:
                    pend()
                pend = nxt
            if pend is not None:
                pend()

        for cc in range(2):
            def post_lx(b, pt, pk, cc=cc):
                act(LRUX[:, cc, bs(b)], pt, AF.Copy, [pk], [('lrux', cc)])
            proj_chunk(cc, post_lx)
        for cc in range(2):
            def post_lg(b, pt, pk, cc=cc):
                i = b % 2
                act(QN[i][:], pt, AF.Square, [pk], [('qn', i)])
                vts(QN[i][:], QN[i][:], 0.044715, 1.0, ALU.mult, ALU.add, [('qn', i)], [('qn', i)])
                vtt(QN[i][:], QN[i][:], pt, ALU.mult, [('qn', i), pk], [('qn', i)])
                act(QN[i][:], QN[i][:], AF.Sigmoid, [('qn', i)], [('qn', i)], scale=1.5957691216057308)
                vtt(LRUG[:, cc, bs(b)], QN[i][:], pt, ALU.mult, [('qn', i), pk], [('lrug', cc)])
            proj_chunk(2 + cc, post_lg)

        lru_phase(l)

        P.barrier()
        ctx_load(l)
        memset(VG[:, :, :, 64:65], 1.0, ['vg'], eng='pool')
        memset(VD[:, :, :, 64:65], 1.0, ['vd'], eng='pool')

        auxc = [0]

        def post_qk(dst_fn, gcol, out_dram=None):
            def post(b, pt, pk):
                i = auxc[0] % 2
                auxc[0] += 1
                act(SQ[i][:], pt, AF.Square, [pk], [('sq', i)])

                def cont():
                    ab = 6 + i
                    aux, auxk = ps[:, ab, :], ('ps', ab)
                    mm(aux, bones[:], SQ[i][:], True, True, [('sq', i), 'bones'], [auxk])
                    act(RT[i][:], aux, AF.Ln, [auxk], [('rt', 0)], bias=EPS, scale=1.0 / 64)
                    act(RT[i][:], RT[i][:], AF.Exp, [('rt', 0)], [('rt', 0)], scale=-0.5)
                    if out_dram is None:
                        vstt(dst_fn(b), pt, qkg[:, l, gcol:gcol + 1], RT[i][:], ALU.mult, ALU.mult, [pk, ('rt', 0)] + CK,
                             [('qkv',)])
                    else:
                        vstt(QN[i][:], pt, qkg[:, l, gcol:gcol + 1], RT[i][:], ALU.mult, ALU.mult, [pk, ('rt', 0)] + CK,
                             [('qn', i)])
                        for kv in range(2):
                            vcopy(KZ[kv][0:64, 256 + b * BLK:256 + (b + 1) * BLK], QN[i][kv * 64:(kv + 1) * 64, :],
                                  [('qn', i)], [('qkv',)])
                        dma(out_dram(b), QN[i][:], 'st_k', reads=[('qn', i)], eng='act')
                return cont
            return post

        for c in range(4):
            proj_chunk(4 + c, post_qk(lambda b, c=c: QT[:, c, bs(b)], 0))
        proj_chunk(8, post_qk(None, 1,
                              out_dram=lambda b: o_kT[l][:, bs(b)]))
        for c in range(2):
            def post_dq(b, pt, pk, c=c):
                act(DQT[:, c, bs(b)], pt, AF.Copy, [pk], [('qkv',)])
            proj_chunk(9 + c, post_dq)
        for c in range(2):
            def post_dk(b, pt, pk, c=c):
                i = b % 2
                act(QN[i][:], pt, AF.Copy, [pk], [('qn', i)])
                for r in range(4):
                    n = 4 * c + r
                    vcopy(DKZ[n // 3][(n % 3) * 32:(n % 3) * 32 + 32, 256 + b * BLK:256 + (b + 1) * BLK],
                          QN[i][r * 32:(r + 1) * 32, :], [('qn', i)], [('qkv',)])
                dma(o_dkT[l][c * 128:(c + 1) * 128, bs(b)], QN[i][:], 'st_dk', reads=[('qn', i)], eng='act')
            proj_chunk(11 + c, post_dk)
        for vc in range(3):
            wi = load_w128(d_winC[l, 13 + vc])
            for tg in range(4):
                pbk = 4 * (vc % 2) + tg
                for tt in range(4):
                    tok = tg * 4 + tt
                    for kc in range(8):
                        mm(ps[:, pbk, tt * 128:(tt + 1) * 128], actT[:, kc, tok * 128:(tok + 1) * 128], WB[wi][:, kc, :],
                           kc == 0, kc == 7, [('wb', wi)] + [('h', kc, tok // 4)], [('ps', pbk)])
                i = tg % 2
                vk = ('vst', 0) if i == 0 else ('qn', 1)
                act(VST[i][:], ps[:, pbk, :].rearrange("p (a b) -> p a b", a=4), AF.Copy, [('ps', pbk)], [vk])
                if vc == 0:
                    vcopy(VG[:, 2 + tg * 4:2 + tg * 4 + 4, :, 0:64],
                          VST[i][:].rearrange("p a (h d) -> p a h d", h=2), [vk], [('qkv',)])
                    dma(o_v[l][tg * 512:(tg + 1) * 512, :].rearrange("(a p) n -> p a n", p=128), VST[i][:], 'st_v',
                        reads=[vk], eng='act')
                else:
                    hb = 2 * (vc - 1)
                    vcopy(VD[:, 2 + tg * 4:2 + tg * 4 + 4, hb:hb + 2, 0:64],
                          VST[i][:].rearrange("p a (h d) -> p a h d", h=2), [vk], [('qkv',)])
                    dma(o_dv[l][tg * 512:(tg + 1) * 512, hb * 64:(hb + 2) * 64].rearrange("(a p) n -> p a n", p=128),
                        VST[i][:], 'st_v', reads=[vk], eng='act')
        P.barrier()
        for b in range(NB):
            dma(ROPET[:, 0:2, :], d_ropeg[:, :, bs(b)].rearrange("a p n -> p a n"), 'ld_rope', writes=['ropet'])
            dma(ROPET[:, 2:4, :], d_roped[:, :, bs(b)].rearrange("a p n -> p a n"), 'ld_rope', writes=['ropet'])
            items = [(QT[:, c, bs(b)], permg, 0, 128) for c in range(4)]
            items += [(KZ[kv][0:64, 256 + b * BLK:256 + (b + 1) * BLK], permg, 0, 64) for kv in range(2)]
            items += [(DQT[:, c, bs(b)], permd, 2, 128) for c in range(2)]
            items += [(DKZ[t][:, 256 + b * BLK:256 + (b + 1) * BLK], permd, 2, 128) for t in range(3)]
            for n, (ap_, pm, to, np_) in enumerate(items):
                i = n % 2
                mm(ps[0:np_, i, :], pm[0:np_, 0:np_], ap_, True, True, [('rp', b, n)], [('ps', i)])
                vtt(T1[i][0:np_, :], ap_, ROPET[0:np_, to, :], ALU.mult, [('rp', b, n), 'ropet'], [('t1', i)])
                vtt(T2[i][0:np_, :], ps[0:np_, i, :], ROPET[0:np_, to + 1, :], ALU.mult, [('ps', i), 'ropet'], [('t2', i)])
                vtt(ap_, T1[i][0:np_, :], T2[i][0:np_, :], ALU.add, [('t1', i), ('t2', i)], [('rp', b, n)])
        P.barrier()
        for t in range(3):
            dma(DKZ[t][96:105, :], d_Ak[:, :], 'ld_ctx', writes=[('qkv',)])

        attention(l)
        if l == 0 and DEPTH > 1:
            ada1_finish()
        P.barrier()

        for og in range(2):
            dma(WOS[og][:].rearrange("p kc n -> p (kc n)"), d_wout[l, og], 'ld_wo%d' % og, writes=[('wos', og)])
            act(WOB[og][:], WOS[og][:], AF.Copy, [('wos', og)], [('wob', og)])
        n = 0
        for og in range(2):
            for oc in range(4):
                ch = og * 4 + oc
                for b in range(NB):
                    pbk = n % 8
                    n += 1
                    for kc in range(8):
                        src = LRUOUT[:, kc, bs(b)] if kc < 2 else actT[:, kc, bs(b)]
                        mm(ps[:, pbk, :], WOB[og][:, kc, oc * 128:(oc + 1) * 128], src, kc == 0, kc == 7,
                           [('wob', og)], [('ps', pbk)])
                    vstt(xT[:, ch, bs(b)], ps[:, pbk, :], g1[:, ch:ch + 1], xT[:, ch, bs(b)], ALU.mult, ALU.add,
                         [('ps', pbk), ('x', ch, b)] + MK, [('x', ch, b)])
        P.barrier()

        def o2(kc, b, t, tk):
            act(actT[:, kc, bs(b)], t[:], AF.Identity, [tk] + MK, [('h', kc, b)],
                bias=sh2[:, kc:kc + 1], scale=gs2[:, l, kc:kc + 1])
        norm_mod(None, None, None, o2)
        ffn(l, g2, MK)
        P.barrier()

    def ctx_load(l):
        dma(T1[0][:, 0:256], d_ckT[l], 'ld_ctx', writes=[('t1', 0)])
        for kv in range(2):
            memset(KZ[kv][64:128, :], 0.0, [('qkv',)], eng='pool')
            dma(KZ[kv][64:73, :], d_Ak[:, :], 'ld_ctx', writes=[('qkv',)])
            vcopy(KZ[kv][0:64, 0:256], T1[0][kv * 64:(kv + 1) * 64, 0:256], [('t1', 0)], [('qkv',)])
        stg = [(T1[1], ('t1', 1)), (RT[0], ('rt', 0))]
        for t in range(3):
            memset(DKZ[t][96:128, :], 0.0, [('qkv',)], eng='pool')
        memset(DKZ[2][64:96, :], 0.0, [('qkv',)], eng='pool')
        for c in range(2):
            tt, tk = stg[c]
            dma(tt[:, 0:256], d_cdkT[l][c * 128:(c + 1) * 128, :], 'ld_ctx', writes=[tk])
            for r in range(4):
                n = 4 * c + r
                vcopy(DKZ[n // 3][(n % 3) * 32:(n % 3) * 32 + 32, 0:256], tt[r * 32:(r + 1) * 32, 0:256], [tk], [('qkv',)])
        dma(QN[0][:, 0:256].rearrange("p (a n) -> p a n", a=2), d_cv[l].rearrange("(a p) n -> p a n", p=128), 'ld_ctx',
            writes=[('qn', 0)])
        vcopy(VG[:, 0:2, :, 0:64], QN[0][:, 0:256].rearrange("p (a h d) -> p a h d", a=2, h=2), [('qn', 0)], [('qkv',)])
        dma(QN[1][:].rearrange("p (a n) -> p a n", a=2), d_cdv[l].rearrange("(a p) n -> p a n", p=128), 'ld_ctx',
            writes=[('qn', 1)])
        vcopy(VD[:, 0:2, :, 0:64], QN[1][:].rearrange("p (a h d) -> p a h d", a=2, h=4), [('qn', 1)], [('qkv',)])

    def lru_phase(l):
        P.barrier()
        H2 = T // 2
        for cc in range(2):
            x = LRUX[:, cc, :]
            w = lambda k: lcw[:, l, cc * 4 + k:cc * 4 + k + 1]
            wn = lambda k: lcwn[:, l, cc * 4 + k:cc * 4 + k + 1]
            XK = ['xc']
            act(L_XC[:], x, AF.Identity, [('lrux', cc)] + CK, XK, bias=lcb[:, l, cc:cc + 1], scale=w(2))
            vstt(L_XC[:, 2:T], x[:, 0:T - 2], w(0), L_XC[:, 2:T], ALU.mult, ALU.add, XK + CK, XK)
            vstt(L_XC[:, 1:T], x[:, 0:T - 1], w(1), L_XC[:, 1:T], ALU.mult, ALU.add, XK + CK, XK)
            vstt(L_XC[:, 0:T - 1], x[:, 1:T], w(3), L_XC[:, 0:T - 1], ALU.mult, ALU.add, XK + CK, XK)
            vstt(L_XC[:, 256:T:256], x[:, 254:T - 2:256], wn(0), L_XC[:, 256:T:256], ALU.mult, ALU.add, XK + ['lcwn'], XK)
            vstt(L_XC[:, 256:T:256], x[:, 255:T - 1:256], wn(1), L_XC[:, 256:T:256], ALU.mult, ALU.add, XK + ['lcwn'], XK)
            vstt(L_XC[:, 257:T:256], x[:, 255:T - 1:256], wn(0), L_XC[:, 257:T:256], ALU.mult, ALU.add, XK + ['lcwn'], XK)
            vstt(L_XC[:, 255:T - 1:256], x[:, 256:T:256], wn(3), L_XC[:, 255:T - 1:256], ALU.mult, ALU.add, XK + ['lcwn'], XK)
            vcopy(L_XCB[:], L_XC[:], XK, ['xcb'])
            for h in range(2):
                for dr in range(2):
                    for ri in range(2):
                        combo = dr * 2 + ri
                        gi = dr * 4 + cc * 2 + ri
                        for k in range(2):
                            b = 2 * h + k
                            mm(ps[:, combo * 2 + k, :], gwb[:, gi, :], L_XCB[:, bs(b)], True, True, ['xcb', 'gwb'],
                               [('ps', combo * 2 + k)])
                for dr in range(2):
                    for ri in range(2):
                        combo = dr * 2 + ri
                        gi = dr * 4 + cc * 2 + ri
                        dst = (L_R if ri == 0 else L_I)[dr]
                        dk = ('lr' if ri == 0 else 'li', dr)
                        act(dst[:, h * H2:(h + 1) * H2], ps[:, combo * 2:combo * 2 + 2, :].rearrange("p a b -> p (a b)"),
                            AF.Sigmoid, [('ps', combo * 2), ('ps', combo * 2 + 1)] + CK, [dk], bias=gb[:, l, gi:gi + 1])
            for dr in range(2):
                sc2 = lsc2[:, l, dr * 2 + cc:dr * 2 + cc + 1]
                act(L_T[dr][:], L_R[dr][:], AF.Exp, [('lr', dr), 'lsc2'], [('lt', dr)], scale=sc2)
            for dr in range(2):
                sc = lsc[:, l, dr * 2 + cc:dr * 2 + cc + 1]
                act(L_R[dr][:], L_R[dr][:], AF.Exp, [('lr', dr), 'lsc'], [('lr', dr)], scale=sc)
            for dr in range(2):
                act(L_T[dr][:], L_T[dr][:], AF.Sqrt, [('lt', dr)], [('lt', dr)], bias=1.0, scale=-1.0)
            for dr in range(2):
                A, I_, T_ = L_R[dr], L_I[dr], L_T[dr]
                ak, ik, tk = ('lr', dr), ('li', dr), ('lt', dr)
                vtt(I_[:], I_[:], L_XC[:], ALU.mult, [ik] + XK, [ik])
                vtt(I_[:], I_[:], T_[:], ALU.mult, [ik, tk], [ik])
                e = 0 if dr == 0 else T - 1
                hh = h0[:, l, dr * 2 + cc:dr * 2 + cc + 1]
                vstt(I_[:, e:e + 1], A[:, e:e + 1], hh, I_[:, e:e + 1], ALU.mult, ALU.add, [ik, ak] + CK, [ik])
                if dr == 0:
                    vts(A[:, 256:T:256], A[:, 256:T:256], flags[:, 0:1], None, ALU.mult, None, [ak] + CK, [ak])
                    P.op('dve', lambda A=A, I_=I_, T_=T_: nc.vector.tensor_tensor_scan(
                        out=T_[:], data0=A[:], data1=I_[:], initial=0.0, op0=ALU.mult, op1=ALU.add), [ak, ik, tk], [tk])
                    vcopy(st[:, (l * 4 + cc) * 8:(l * 4 + cc) * 8 + 8], T_[:, 255:T:256], [tk], ['st'])
                else:
                    vts(A[:, 255:T - 1:256], A[:, 255:T - 1:256], flags[:, 0:1], None, ALU.mult, None, [ak] + CK, [ak])
                    P.op('dve', lambda A=A, I_=I_, T_=T_: nc.vector.tensor_tensor_scan(
                        out=T_[:, ::-1], data0=A[:, ::-1], data1=I_[:, ::-1], initial=0.0, op0=ALU.mult, op1=ALU.add),
                        [ak, ik, tk], [tk])
                    vcopy(st[:, (l * 4 + 2 + cc) * 8:(l * 4 + 2 + cc) * 8 + 8], T_[:, 0:T:256], [tk], ['st'])
            vtt(L_T[0][:], L_T[0][:], L_T[1][:], ALU.add, [('lt', 0), ('lt', 1)], [('lt', 0)])
            vtt(LRUOUT[:, cc, :], L_T[0][:], LRUG[:, cc, :], ALU.mult, [('lt', 0), ('lrug', cc)], [('lruout', cc)])

    def attention(l):
        SKEW = 2
        sc_d = 32 ** -0.5
        heads = []
        pair = 0
        for c in range(4):
            for qb in range(NB):
                for hh in range(2):
                    p0 = hh * 64
                    heads.append(dict(kind='g', qrows=QT[p0:p0 + 64, c, bs(qb)], p0=p0, nk=64,
                                      kfn=(lambda kt, hh=hh: KZ[hh][:, kt * 128:(kt + 1) * 128]),
                                      vfn=(lambda kt, hh=hh: VG[:, kt, hh, 0:65]), scale=0.125, qb=qb, hh=hh,
                                      comp=0, pair=pair, last=(hh == 1), dst=2 + c))
                pair += 1
        for c in range(2):
            for qb in range(NB):
                for hh in range(2):
                    for comp in range(2):
                        p0 = hh * 64 + comp * 32
                        heads.append(dict(kind='d', qrows=DQT[p0:p0 + 32, c, bs(qb)], p0=p0, nk=32,
                                          kfn=(lambda kt, n=4 * c + 2 * hh + comp: DKZ[n // 3][:, kt * 128:(kt + 1) * 128]),
                                          slot=(4 * c + 2 * hh + comp) % 3,
                                          vfn=(lambda kt, h=2 * c + hh: VD[:, kt, h, 0:65]), scale=sc_d, qb=qb, hh=hh,
                                          comp=comp, pair=pair, last=(hh == 1 and comp == 1), dst=6 + c))
                pair += 1
        KP = KT // 2
        tiles = [(hi, kp) for hi in range(len(heads)) for kp in range(KP)]
        NTL = len(tiles)

        def accv(hi):
            ab = 4 + hi % 2
            v = ps[:, ab, 0:260].rearrange("p (j c) -> p j c", j=4)
            return ab, v

        def finish(H, hi):
            oi = H['pair'] % 2
            hh = H['hh']
            ab, v = accv(hi)
            acc4, den4 = v[:, :, 0:64], v[:, :, 64:65]
            PACC = [('ps', ab)]
            rec = sml[:, 0:4].unsqueeze(2)
            P.op('dve', lambda: nc.vector.reciprocal(out=rec, in_=den4), PACC, ['smlr'])
            recb = rec.broadcast_to([128, 4, 64])
            if H['kind'] == 'g':
                vtt(OTOK[oi][:, :, hh * 64:(hh + 1) * 64], acc4, recb, ALU.mult, PACC + ['smlr'], [('otok', oi)])
            elif H['comp'] == 0:
                vtt(DD1[:], acc4, recb, ALU.mult, PACC + ['smlr'], ['dd1'])
            else:
                vts(sml[:, 0:4], sml[:, 0:4], nlam[:, l:l + 1], None, ALU.mult, None, ['smlr', 'nlam'], ['smlr'])
                vtt(DD2[:], acc4, recb, ALU.mult, PACC + ['smlr'], ['dd2'])

        def finish_b(H):
            oi = H['pair'] % 2
            hh = H['hh']
            vtt(DD2[:], DD2[:], DD1[:], ALU.add, ['dd2', 'dd1'], ['dd2'])
            vtt(DD1[:], DD2[:], DD2[:], ALU.mult, ['dd2', 'dd1'], ['dd1'])
            P.op('dve', lambda: nc.vector.reduce_sum(out=sml[:, 8:12], in_=DD1[:], axis=mybir.AxisListType.X),
                 ['dd1'], ['smls'])
            vts(sml[:, 8:12], sml[:, 8:12], 1.0 / 64, EPS, ALU.mult, ALU.add, ['smls'], ['smls'])
            vtt(sml[:, 8:12], sml[:, 8:12], mhalf[:, 0:4], ALU.pow, ['smls', 'mhalf'], ['smls'], eng='pool')
            vtt(DD2[:], DD2[:], sml[:, 8:12].unsqueeze(2).broadcast_to([128, 4, 64]), ALU.mult, ['dd2', 'smls'], ['dd2'])
            vtt(OTOK[oi][:, :, hh * 64:(hh + 1) * 64], DD2[:], dg[:, l, :].unsqueeze(1).broadcast_to([128, 4, 64]),
                ALU.mult, ['dd2', 'dg'], [('otok', oi)])

        def transposes(H):
            oi = H['pair'] % 2
            tpb = ps[:, 6, :].bitcast(BF16)
            for j in range(4):
                P.op('pe', lambda j=j: nc.tensor.transpose(tpb[:, j * 128:(j + 1) * 128], OTOK[oi][:, j, :], ident[:]),
                     [('otok', oi), 'ident'], [('ps', 6)])
            vcopy(actT[:, H['dst'], bs(H['qb'])], tpb[:, 0:BLK], [('ps', 6)], [('h', H['dst'], H['qb'])])

        SKEW = 1
        deferred = []
        for i in range(NTL + SKEW):
            if l == 0 and DEPTH > 1 and i % 24 == 12:
                ada1_step()
            if i < NTL:
                hi, kp = tiles[i]
                H = heads[hi]
                zi = hi % 2
                s = i % 2
                if kp == 0:
                    memset(QZ[zi][:], 0.0, [('qz', zi)], eng='pool')
                    if H['kind'] == 'g':
                        vcopy(QZ[zi][0:64, :], H['qrows'], [('qkv',)], [('qz', zi)], eng='pool')
                        vcopy(QZ[zi][64:73, :], mBb[64:73, H['qb'], :], ['mB'], [('qz', zi)], eng='pool')
                    else:
                        r0 = H['slot'] * 32
                        vcopy(QZ[zi][r0:r0 + 32, :], H['qrows'], [('qkv',)], [('qz', zi)], eng='pool')
                        vcopy(QZ[zi][96:105, :], mBb[96:105, H['qb'], :], ['mB'], [('qz', zi)], eng='pool')
                for t in range(2):
                    kt = 2 * kp + t
                    bk = 2 * s + t
                    mm(ps[:, bk, :], H['kfn'](kt), QZ[zi][:], True, True, [('qkv',), ('qz', zi)], [('ps', bk)])
                act(PTt[s][:], ps[:, 2 * s:2 * s + 2, :].rearrange("p a b -> p (a b)"), AF.Exp,
                    [('ps', 2 * s), ('ps', 2 * s + 1)], [('pt', s)], scale=H['scale'])
            if i >= SKEW:
                hi, kp = tiles[i - SKEW]
                H = heads[hi]
                s = (i - SKEW) % 2
                ab, v = accv(hi)
                for t in range(2):
                    kt = 2 * kp + t
                    for j in range(4):
                        first = (kt == 0 and j == 0)
                        P.op('pe', lambda t=t, j=j, kt=kt, first=first, v=v, s=s, H=H: nc.tensor.matmul(
                            v[:, j, :], lhsT=PTt[s][:, t * BLK + j * 128:t * BLK + (j + 1) * 128], rhs=H['vfn'](kt),
                            start=first, stop=(kt == KT - 1), skip_group_check=True),
                            [('pt', s), ('qkv',)], [('ps', ab)])
                if kp == KP - 1:
                    finish(H, hi)
                    if H['kind'] == 'd' and H['comp'] == 1:
                        deferred.append((i + 2, 'b', H))
                        if H['last']:
                            deferred.append((i + 4, 't', H))
                    elif H['last']:
                        deferred.append((i + 1, 't', H))
            deferred.sort(key=lambda d: d[0])
            while deferred and deferred[0][0] <= i:
                d = deferred.pop(0)
                (finish_b if d[1] == 'b' else transposes)(d[2])
        while deferred:
            d = deferred.pop(0)
            (finish_b if d[1] == 'b' else transposes)(d[2])

    def ffn(l, g2, MK):
        H2 = T // 2
        nps = [0]
        nfo = [0]
        nhp = [0]

        def load_up(g):
            wi = g % 2
            dma(FWS[:].rearrange("p kc n -> p (kc n)"), d_wup[l, g], 'ld_fu', writes=['fws'])
            vcopy(FWB[wi][:], FWS[:], ['fws'], [('fwb', wi)])

        def load_dn(g):
            wi = g % 2
            dma(FDS[:].rearrange("p j n -> p (j n)"), d_wdn[l, g], 'ld_fd', writes=['fds'])
            vcopy(FDB[wi][:], FDS[:], ['fds'], [('fdb', wi)])

        def down_piece(g, k0, k1):
            wi = g % 2
            for k in range(k0, k1):
                oc, b = k // NB, k % NB
                n = nps[0]
                nps[0] += 1
                pbk = 4 + n % 4
                for jj in range(2):
                    mm(ps[:, pbk, :], FDB[wi][:, jj, oc * 128:(oc + 1) * 128], FACT[wi][:, jj, bs(b)], jj == 0, jj == 1,
                       [('fdb', wi), ('fact', wi)], [('ps', pbk)])
                if n % 8 not in (1, 4, 6):
                    vstt(xT[:, oc, bs(b)], ps[:, pbk, :], g2[:, oc:oc + 1], xT[:, oc, bs(b)], ALU.mult, ALU.add,
                         [('ps', pbk), ('x', oc, b)] + MK, [('x', oc, b)])
                else:
                    fi = nfo[0] % 2
                    nfo[0] += 1
                    act(FOUT[fi][:], ps[:, pbk, :], AF.Identity, [('ps', pbk)] + MK, [('fout', fi)], scale=g2[:, oc:oc + 1])
                    vtt(xT[:, oc, bs(b)], xT[:, oc, bs(b)], FOUT[fi][:], ALU.add, [('fout', fi), ('x', oc, b)],
                        [('x', oc, b)], eng='pool')

        def up(g, gd):
            wi = g % 2
            piece = 0
            for jj in range(2):
                j = 2 * g + jj
                for part in range(2):
                    col = (part * 2 + jj) * 128
                    fc = part * 22 + j
                    cw = lambda k, fc=fc: fcw[:, l, fc * 3 + k:fc * 3 + k + 1]
                    Y, U0, U2 = FY[part], FU0[part], FU2[part]
                    for h in range(2):
                        pr = nhp[0] % 2
                        nhp[0] += 1
                        pb = 2 * pr
                        for k in range(2):
                            b = 2 * h + k
                            for kc in range(8):
                                mm(ps[:, pb + k, :], FWB[wi][:, kc, col:col + 128], actT[:, kc, bs(b)], kc == 0, kc == 7,
                                   [('fwb', wi), ('h', kc, b)], [('ps', pb + k)])
                        PK = [('ps', pb), ('ps', pb + 1)]
                        p2 = ps[:, pb:pb + 2, :].rearrange("p a b -> p (a b)")
                        t0 = h * H2
                        yk, u0k, u2k = ('fy', part, h), ('fu0', part), ('fu2', part)
                        act(Y[:, t0:t0 + H2], p2, AF.Identity, PK + CK, [yk], bias=fcb[:, l, fc:fc + 1], scale=cw(1))
                        fl = flags[:, 0:1]
                        y0, y1 = ('fy', part, 0), ('fy', part, 1)
                        if h == 0:
                            act(U0[:, 1:H2 + 1], p2, AF.Copy, PK + CK, [u0k], scale=cw(0))
                            act(U2[:, 0:H2 - 1], p2[:, 1:H2], AF.Copy, PK + CK, [u2k], scale=cw(2))
                            vts(U0[:, 256:H2 + 1:256], U0[:, 256:H2 + 1:256], fl, None, ALU.mult, None, [u0k] + CK, [u0k])
                            vts(U2[:, 255:H2 - 1:256], U2[:, 255:H2 - 1:256], fl, None, ALU.mult, None, [u2k] + CK, [u2k])
                            vtt(Y[:, 0:H2], Y[:, 0:H2], U0[:, 0:H2], ALU.add, [y0, u0k], [y0])
                            vtt(Y[:, 0:H2 - 2], Y[:, 0:H2 - 2], U2[:, 0:H2 - 2], ALU.add, [y0, u2k], [y0])
                        else:
                            act(U0[:, H2 + 1:T], p2[:, 0:H2 - 1], AF.Copy, PK + CK, [u0k], scale=cw(0))
                            act(U2[:, H2 - 1:T - 1], p2, AF.Copy, PK + CK, [u2k], scale=cw(2))
                            vts(U0[:, H2 + 256:T:256], U0[:, H2 + 256:T:256], fl, None, ALU.mult, None, [u0k] + CK, [u0k])
                            vts(U2[:, H2 - 1:T - 1:256], U2[:, H2 - 1:T - 1:256], fl, None, ALU.mult, None, [u2k] + CK, [u2k])
                            vtt(Y[:, H2:T], Y[:, H2:T], U0[:, H2:T], ALU.add, [y1, u0k], [y1])
                            vtt(Y[:, H2 - 2:T], Y[:, H2 - 2:T], U2[:, H2 - 2:T], ALU.add, [y0, y1, u2k], [y0, y1])
                        if gd is not None:
                            down_piece(gd, piece * 4, piece * 4 + 4)
                        piece += 1
                        if piece == 4 and g + 1 < 11:
                            load_up(g + 1)
                YK = [('fy', p_, h_) for p_ in range(2) for h_ in range(2)]
                act(FY[1][:], FY[1][:], AF.Silu, YK, [('fy', 1, 0), ('fy', 1, 1)])
                vtt(FACT[wi][:, jj, :], FY[1][:], FY[0][:], ALU.mult, YK, [('fact', wi)])
            if g + 1 < 11:
                load_dn(g + 1)

        memset(FU0[0][:, 0:1], 0.0, [('fu0', 0)])
        memset(FU0[1][:, 0:1], 0.0, [('fu0', 1)])
        memset(FU2[0][:, T - 1:T], 0.0, [('fu2', 0)])
        memset(FU2[1][:, T - 1:T], 0.0, [('fu2', 1)])
        load_up(0)
        load_dn(0)
        up(0, None)
        for g in range(11):
            if g + 1 < 11:
                up(g + 1, g)
            else:
                down_piece(g, 0, 32)

    ada(0)
    P.barrier()
    for l in range(n_layers):
        layer(l)

    def of(kc, b, t, tk):
        i = (kc + b) % 2
        act(QN[i][:], t[:], AF.Identity, [tk, 'fng'], [('qn', i)], scale=fng[:, kc:kc + 1])
        dma(o_yT[kc * 128:(kc + 1) * 128, bs(b)], QN[i][:], 'st_y', reads=[('qn', i)])
    norm_mod(None, None, None, of)
    dma(o_st[:, :], st[:], 'st_s', reads=['st'])
    P.emit()
    return nc


def _rope_tables(dim, is_sample):
    if not is_sample:
        return np.stack([np.ones((128, T), np.float32), np.zeros((128, T), np.float32)])
    GRID_W = 64
    rows = T // GRID_W
    t_row = np.repeat(np.arange(rows, dtype=np.float32), GRID_W)
    t_col = np.tile(np.arange(GRID_W, dtype=np.float32), rows)
    axis_dim = dim // 2
    inv = (np.float32(10000.0) ** (-np.arange(0, axis_dim, 2, dtype=np.float32) / np.float32(axis_dim))).astype(np.float32)
    ar = t_row[:, None] * inv
    ac = t_col[:, None] * inv
    ang = np.concatenate([ar, ar, ac, ac], axis=-1)
    cos = np.cos(ang).astype(np.float32).T
    sin = np.sin(ang).astype(np.float32).T
    rep = 128 // dim
    return np.stack([np.tile(cos, (rep, 1)), np.tile(sin, (rep, 1))]).astype(np.float32)


def _perm_matrix(dim):
    q = dim // 4
    Pm = np.zeros((128, 128), np.float32)
    for hb in range(0, 128, dim):
        for d in range(dim):
            quarter = d // q
            m = hb + d
            if quarter % 2 == 0:
                Pm[hb + d + q, m] = -1.0
            else:
                Pm[hb + d - q, m] = 1.0
    return Pm


_CACHE = {}


def kernel(**inp):
    f32 = np.float32
    g = {k: np.asarray(v) for k, v in inp.items()}
    L = DEPTH

    def c(a):
        return np.ascontiguousarray(a, dtype=f32)

    def pk8(v):
        return c(v.reshape(8, 128).T)

    w_in = g['w_in']
    colsF = []
    colsF += list(range(0, 256))
    colsF += list(range(256, 512))
    for cch in range(4):
        for h in (cch, cch + 4):
            colsF += list(range(512 + 64 * h, 512 + 64 * h + 64))
    colsF += list(range(1024, 1152))
    colsF += list(range(1280, 1536))
    colsF += list(range(1536, 1792))
    colsV = list(range(1152, 1280)) + list(range(1792, 2048))
    w_inF = c(w_in[:, :, colsF])
    w_inV = c(w_in[:, :, colsV])
    w_all = np.concatenate([w_inF, w_inV], axis=2)
    w_inC = c(w_all.reshape(L, 8, 128, 16, 128).transpose(0, 3, 2, 1, 4).reshape(L, 16, 128, 1024))
    rows_o = list(range(0, 256))
    for cch in range(4):
        for h in (cch, cch + 4):
            rows_o += list(range(256 + 64 * h, 256 + 64 * h + 64))
    rows_o += list(range(768, 1024))
    w_outP = c(g['w_out'][:, rows_o, :])
    up = g['ffn_w_up']
    cols_up = []
    for grp in range(11):
        for part in range(2):
            for jj in range(2):
                j = 2 * grp + jj
                cols_up += list(range(part * D_FF + 128 * j, part * D_FF + 128 * j + 128))
    w_upP = c(up[:, :, cols_up])
    shared = {
        'ada_w': c(g['ada_w']),
        'ada_b': c(g['ada_b'].reshape(L, 48, 128).transpose(0, 2, 1)),
        'n1g': c(g['norm1_g'].reshape(L, 8, 128).transpose(0, 2, 1)),
        'n2g': c(g['norm2_g'].reshape(L, 8, 128).transpose(0, 2, 1)),
        'fng': pk8(g['final_norm_g']),
        'w_inC': w_inC,
        'w_outC': c(w_outP.reshape(L, 8, 128, 2, 512).transpose(0, 3, 2, 1, 4).reshape(L, 2, 128, 4096)),
        'w_upC': c(w_upP.reshape(L, 8, 128, 11, 512).transpose(0, 3, 2, 1, 4).reshape(L, 11, 128, 4096)),
        'w_dnC': c(g['ffn_w_down'].reshape(L, 11, 2, 128, D_MODEL).transpose(0, 1, 3, 2, 4).reshape(L, 11, 128, 2 * D_MODEL)),
        'lcw': c(g['lru_conv_w'].reshape(L, 4, 2, 128).transpose(0, 3, 2, 1).reshape(L, 128, 8)),
        'lcb': c(g['lru_conv_b'].reshape(L, 2, 128).transpose(0, 2, 1)),
        'lam': c(g['lru_lambda'].reshape(L, 2, 2, 128).transpose(0, 3, 1, 2).reshape(L, 128, 4)),
        'qkg': c(np.stack([np.tile(g['gqa_q_norm_g'], (1, 2)), np.tile(g['gqa_k_norm_g'], (1, 2))], axis=-1)),
        'dl': c(g['diff_lambda'].reshape(L, 128)),
        'dg': c(g['diff_norm_g']),
        'fcw': c(g['ffn_conv_w'].reshape(L, 3, 44, 128).transpose(0, 3, 2, 1).reshape(L, 128, 132)),
        'fcb': c(g['ffn_conv_b'].reshape(L, 44, 128).transpose(0, 2, 1)),
        'permg': _perm_matrix(64), 'permd': _perm_matrix(32),
        'ident': np.eye(128, dtype=f32),
    }
    gwsrc = g['lru_gate_w']
    gw = np.zeros((L, 8, 128, 128), f32)
    gbsrc = g['lru_gate_b']
    gbv = np.zeros((L, 128, 8), f32)
    for l in range(L):
        for dr in range(2):
            for cc in range(2):
                for ri in range(2):
                    idx = dr * 4 + cc * 2 + ri
                    for bl in range(2):
                        blk = 2 * cc + bl
                        gw[l, idx, bl * 64:(bl + 1) * 64, bl * 64:(bl + 1) * 64] = gwsrc[l, dr, blk, :, ri * 64:(ri + 1) * 64]
                        gbv[l, bl * 64:(bl + 1) * 64, idx] = gbsrc[l, dr, blk, ri * 64:(ri + 1) * 64]
    shared['gw'] = gw
    shared['gb'] = gbv

    mA4 = np.zeros((128, 9 * 128), f32)
    for j in range(9):
        mA4[j, j * 128:(j + 1) * 128] = 1.0

    def mB4_for(is_sample):
        m = np.zeros((128, 4 * BLK), f32)
        if not is_sample:
            qseg = np.arange(T) // 256 + 1
            for j in range(9):
                m[j, :] = np.where(qseg == j, 0.0, NEG)
                m[64 + j, :] = m[j, :]
                m[96 + j, :] = m[j, :]
        return m

    import ml_dtypes
    Ak = np.zeros((9, TK), f32)
    for j in range(9):
        Ak[j, j * 256:(j + 1) * 256] = 1.0
    Ak = Ak.astype(ml_dtypes.bfloat16)

    ropeg_s, roped_s = _rope_tables(64, True), _rope_tables(32, True)
    ropeg_p, roped_p = _rope_tables(64, False), _rope_tables(32, False)
    mB_s, mB_p = mB4_for(True), mB4_for(False)

    xs, xp = g['x_sample'], g['x_prompt']
    in_maps = []
    for core in range(8):
        m = dict(shared)
        m['Ak'] = Ak
        if core < 4:
            b = core
            m['xT'] = c(xs[b].T)
            m['cT'] = pk8(g['c'][b])
            m['flags'] = c(np.tile(np.array([[1.0, 0.0]], f32), (128, 1)))
            m['mB4'] = mB_s
            m['ropeg'], m['roped'] = ropeg_s, roped_s
            m['ckT'] = c(g['cache_gqa_k'][b].reshape(L, 256, 128).transpose(0, 2, 1))
            m['cv'] = c(g['cache_gqa_v'][b].reshape(L, 256, 128))
            m['cdkT'] = c(g['cache_diff_k'][b].reshape(L, 256, 256).transpose(0, 2, 1))
            m['cdv'] = c(g['cache_diff_v'][b].reshape(L, 256, 256))
            m['h0'] = c(g['state_lru'][b].reshape(L, 2, 2, 128).transpose(0, 3, 1, 2).reshape(L, 128, 4))
        else:
            i = (core - 4) % 2
            m['xT'] = c(xp[8 * i:8 * i + 8].reshape(T, D_MODEL).T)
            m['cT'] = pk8(g['c_ctx'])
            m['flags'] = c(np.tile(np.array([[0.0, 1.0]], f32), (128, 1)))
            m['mB4'] = mB_p
            m['ropeg'], m['roped'] = ropeg_p, roped_p
            m['ckT'] = np.zeros((L, 128, 256), f32)
            m['cv'] = np.zeros((L, 256, 128), f32)
            m['cdkT'] = np.zeros((L, 256, 256), f32)
            m['cdv'] = np.zeros((L, 256, 256), f32)
            m['h0'] = np.zeros((L, 128, 4), f32)
        in_maps.append(m)

    if 'nc' not in _CACHE:
        _CACHE['nc'] = build_program()
    nc = _CACHE['nc']
    res = run_bass_kernel_spmd(nc, in_maps, core_ids=list(range(8)))
    R = res.results

    y_sample = np.stack([R[b]['yT'].T for b in range(4)]).astype(f32)
    yp, ks, vs, dks, dvs, sts = [], [], [], [], [], []
    for i in range(2):
        r = R[4 + i]
        yT = r['yT']
        for s in range(8):
            sl = slice(256 * s, 256 * s + 256)
            yp.append(yT[:, sl].T)
            ks.append(np.stack([r['okT'][l][:, sl].T.reshape(256, 2, 64) for l in range(L)]))
            vs.append(np.stack([r['ov'][l][sl, :].reshape(256, 2, 64) for l in range(L)]))
            dks.append(np.stack([r['odkT'][l][:, sl].T.reshape(256, 4, 2, 32) for l in range(L)]))
            dvs.append(np.stack([r['odv'][l][sl, :].reshape(256, 4, 64) for l in range(L)]))
            o = r['ost'].reshape(128, L, 2, 2, 8)[:, :, :, :, s]
            sts.append(o.transpose(1, 2, 3, 0).reshape(L, 2, 256))
    out = (np.stack(yp).astype(f32), y_sample, np.stack(ks).astype(f32), np.stack(vs).astype(f32),
           np.stack(dks).astype(f32), np.stack(dvs).astype(f32), np.stack(sts).astype(f32))
    return tuple(np.ascontiguousarray(o) for o in out)
```
